# Optimizing a Trainium2 kernel written in Bass

```python
import math
import jax
import jax.numpy as jnp
from jax import lax
import numpy as np

D_MODEL = 1024
BATCH = 4
SEQ = 8192
DEPTH = 4
DEC_BATCH = 8
DEC_SEQ = 2048
PAST_LEN = 128

F32 = jnp.float32

N_MIXERS = 4
EXPAND = 2
E_WIDTH = EXPAND * D_MODEL
NORM_EPS = 1e-6
CHUNK = 64

HG_DK = 128
HG_HEADS = E_WIDTH // HG_DK
HG_DV = E_WIDTH // HG_HEADS

HY_EMB = 33
HY_BANDS = (HY_EMB - 1) // 2
HY_FH = 64
HY_INNER = 2
HY_SHORT = 3
HY_FAST_DECAY = 0.3
HY_SLOW_DECAY = 1.5
HY_TARGET = 1e-2
HY_FILTER_SCALE = 0.05

RT_HEADS = 4
RT_QK = D_MODEL
RT_DK = RT_QK // RT_HEADS
RT_DV = E_WIDTH // RT_HEADS
RT_ROPE_BASE = 10000.0

LRU_CONV = 4
LRU_BLOCKS = 16
LRU_BS = E_WIDTH // LRU_BLOCKS
LRU_C = 8.0

N_HG = len(range(0, DEPTH, N_MIXERS))
N_HY = len(range(1, DEPTH, N_MIXERS))
N_RT = len(range(2, DEPTH, N_MIXERS))
N_LRU = len(range(3, DEPTH, N_MIXERS))

kernel_name = 'hybrid_bidir_interleaved_encoder'


def _rms(x):
    return x * lax.rsqrt(jnp.mean(x * x, axis=-1, keepdims=True) + NORM_EPS)


def _flip(t):
    return jnp.flip(t, axis=1)


def _depthwise_conv(x, w, b, left):
    K, C = w.shape
    y = lax.conv_general_dilated(x, w.astype(x.dtype)[:, None, :], window_strides=(1,),
                                 padding=[(left, K - 1 - left)],
                                 dimension_numbers=('NWC', 'WIO', 'NWC'), feature_group_count=C)
    return y + b


def _to_chunks(t):
    B, L, H, d = t.shape
    return t.reshape(B, L // CHUNK, CHUNK, H, d).transpose(1, 0, 3, 2, 4)


def _from_chunks(t):
    n, B, H, C, d = t.shape
    return t.transpose(1, 0, 3, 2, 4).reshape(B, n * C, H, d)


def _gated_chunkwise(q, k, v, g):
    B, L, H, dk = q.shape
    dv = v.shape[-1]
    lower = jnp.tril(jnp.ones((CHUNK, CHUNK), dtype=bool))
    mid = CHUNK // 2

    def step(S, blk):
        qb, kb, vb, gb = blk
        b = jnp.cumsum(gb, axis=2)
        b_ref = b[:, :, mid:mid + 1]
        b_last = b[:, :, CHUNK - 1:]
        scores = jnp.einsum('bhtd,bhsd->bhts', qb * jnp.exp(b - b_ref), kb * jnp.exp(b_ref - b))
        scores = jnp.where(lower, scores, 0.0)
        o = (jnp.einsum('bhts,bhsv->bhtv', scores, vb)
             + jnp.einsum('bhtd,bhdv->bhtv', qb * jnp.exp(b), S))
        S = (jnp.exp(b_last)[:, :, 0, :, None] * S
             + jnp.einsum('bhsd,bhsv->bhdv', kb * jnp.exp(b_last - b), vb))
        return S, o

    S0 = jnp.zeros((B, H, dk, dv), F32)
    _, o = lax.scan(step, S0, (_to_chunks(q), _to_chunks(k), _to_chunks(v), _to_chunks(g)))
    return _from_chunks(o)


def _retention_chunkwise(q, k, v, log_gamma):
    B, L, H, dk = q.shape
    dv = v.shape[-1]
    pos = jnp.arange(CHUNK, dtype=F32)
    rel = pos[:, None] - pos[None, :]
    decay = jnp.where(rel >= 0, jnp.exp(log_gamma[:, None, None] * jnp.maximum(rel, 0.0)), 0.0)
    q_dec = jnp.exp(log_gamma[:, None] * (pos + 1.0))[None, :, :, None]
    k_dec = jnp.exp(log_gamma[:, None] * (CHUNK - 1.0 - pos))[None, :, :, None]
    c_dec = jnp.exp(log_gamma * CHUNK)[None, :, None, None]

    def step(R, blk):
        qb, kb, vb = blk
        scores = jnp.einsum('bhtd,bhsd->bhts', qb, kb) * decay
        o = (jnp.einsum('bhts,bhsv->bhtv', scores, vb)
             + q_dec * jnp.einsum('bhtd,bhdv->bhtv', qb, R))
        R = c_dec * R + jnp.einsum('bhsd,bhsv->bhdv', kb * k_dec, vb)
        return R, o

    R0 = jnp.zeros((B, H, dk, dv), F32)
    _, o = lax.scan(step, R0, (_to_chunks(q), _to_chunks(k), _to_chunks(v)))
    return _from_chunks(o)


def _rotary(t):
    B, L, H, d = t.shape
    inv = RT_ROPE_BASE ** (-jnp.arange(0, d, 2, dtype=F32) / d)
    ang = jnp.arange(L, dtype=F32)[:, None] * inv[None]
    cos = jnp.cos(ang)[None, :, None]
    sin = jnp.sin(ang)[None, :, None]
    t1, t2 = t[..., :d // 2], t[..., d // 2:]
    return jnp.concatenate([t1 * cos - t2 * sin, t1 * sin + t2 * cos], axis=-1)


def _hgrn2_mixer(h, lb, w_in, norm_g, w_out):
    B, L, _ = h.shape
    q, f_fw, f_bw, i, z = jnp.split(h @ w_in, 5, axis=-1)

    def heads(t):
        return t.reshape(B, L, HG_HEADS, -1)

    q = heads(jax.nn.silu(q))
    i = heads(i)

    def gate(f_raw):
        f = lb + (1.0 - lb) * jax.nn.sigmoid(f_raw)
        return heads(1.0 - f), heads(jnp.log(f))

    k_fw, g_fw = gate(f_fw)
    k_bw, g_bw = gate(f_bw)
    o = (_gated_chunkwise(q, k_fw, i, g_fw)
         + _flip(_gated_chunkwise(_flip(q), _flip(k_bw), _flip(i), _flip(g_bw))))
    o = (_rms(o) * norm_g).reshape(B, L, E_WIDTH)
    return (o * jax.nn.silu(z)) @ w_out


def _hyena_filters(L, w1, b1, w2, b2, w_out, freq):
    t = jnp.linspace(0.0, 1.0, L, dtype=F32)[:, None]
    w = 2.0 * math.pi * jnp.arange(L, dtype=F32)[:, None] / L
    bands = jnp.linspace(1e-4, HY_BANDS - 1, HY_BANDS, dtype=F32)[None]
    z = jnp.concatenate([t, jnp.cos(bands * w), -jnp.sin(bands * w)], axis=-1)
    a = jnp.sin(freq * (z @ w1 + b1))
    for j in range(HY_INNER):
        a = jnp.sin(freq * (a @ w2[j] + b2[j]))
    filt = a @ w_out
    max_decay = math.log(HY_TARGET) / HY_FAST_DECAY
    min_decay = math.log(HY_TARGET) / HY_SLOW_DECAY
    deltas = jnp.abs(jnp.linspace(min_decay, max_decay, E_WIDTH, dtype=F32))
    window = jnp.exp(-t * deltas)
    return filt[:, :E_WIDTH] * window, filt[:, E_WIDTH:] * window


def _bidir_fftconv(u, h_fw, h_bw):
    B, L, C = u.shape
    taps = jnp.concatenate([h_fw[:1] + h_bw[:1], h_fw[1:], jnp.zeros((1, C), F32), h_bw[:0:-1]], axis=0)
    u_f = jnp.fft.rfft(u, n=2 * L, axis=1)
    t_f = jnp.fft.rfft(taps, axis=0)
    return jnp.fft.irfft(u_f * t_f[None], n=2 * L, axis=1)[:, :L]


def _hyena_mixer(h, w_in, b_in, conv_w, conv_b, f_w1, f_b1, f_w2, f_b2, f_wout, f_freq, skip, w_out):
    B, L, _ = h.shape
    proj = h @ w_in + b_in
    vxx = _depthwise_conv(proj[..., :3 * E_WIDTH], conv_w, conv_b, left=HY_SHORT // 2)
    z = proj[..., 3 * E_WIDTH:]
    x0, x1, v = jnp.split(vxx, 3, axis=-1)
    h_fw, h_bw = _hyena_filters(L, f_w1, f_b1, f_w2, f_b2, f_wout, f_freq)
    u = x0 * v
    y = x1 * (_bidir_fftconv(u, h_fw, h_bw) + u * skip)
    return (y * jax.nn.silu(z)) @ w_out


def _retention_mixer(h, w_in, gn_g, w_out):
    B, L, _ = h.shape
    q, k, v, g = jnp.split(h @ w_in, [RT_QK, 2 * RT_QK, 2 * RT_QK + E_WIDTH], axis=-1)
    q = _rotary(q.reshape(B, L, RT_HEADS, RT_DK))
    k = _rotary(k.reshape(B, L, RT_HEADS, RT_DK)) * (RT_DK ** -0.5)
    v = v.reshape(B, L, RT_HEADS, RT_DV)
    head_idx = jnp.arange(RT_HEADS, dtype=F32)
    lg_fw = jnp.log1p(-jnp.exp2(-5.0 - head_idx))
    lg_bw = jnp.log1p(-jnp.exp2(-5.5 - head_idx))
    o = (_retention_chunkwise(q, k, v, lg_fw)
         + _flip(_retention_chunkwise(_flip(q), _flip(k), _flip(v), lg_bw)))
    o = _rms(o).reshape(B, L, E_WIDTH) * gn_g
    return (o * jax.nn.silu(g)) @ w_out


def _rglru_mixer(h, w_in, conv_w, conv_b, gate_w, gate_b, lam, w_out):
    B, L, _ = h.shape
    xb, z = jnp.split(h @ w_in, 2, axis=-1)
    xb = _depthwise_conv(xb, conv_w, conv_b, left=LRU_CONV // 2)
    xblk = xb.reshape(B, L, LRU_BLOCKS, LRU_BS)

    def combine(e1, e2):
        a1, b1 = e1
        a2, b2 = e2
        return a1 * a2, a2 * b1 + b2

    def direction(d, reverse):
        gates = (jnp.einsum('blnk,gnkj->gblnj', xblk, gate_w[d]).reshape(2, B, L, E_WIDTH)
                 + gate_b[d][:, None, None, :])
        r = jax.nn.sigmoid(gates[0])
        i = jax.nn.sigmoid(gates[1])
        log_a = -LRU_C * r * jax.nn.softplus(-lam[d].astype(F32))
        a = jnp.exp(log_a)
        b = jnp.sqrt(-jnp.expm1(2.0 * log_a)) * (i * xb)
        _, hs = lax.associative_scan(combine, (a, b), axis=1, reverse=reverse)
        return hs

    y = direction(0, False) + direction(1, True)
    return (y * jax.nn.silu(z)) @ w_out


def _trunk(x, c, ada_w, ada_b, norm_g, final_g, hg, hy, rt, lru):
    x = x.astype(F32)
    cs = jax.nn.silu(c.astype(F32))
    for layer in range(DEPTH):
        kind, j = layer % N_MIXERS, layer // N_MIXERS
        shift, scale, gate = jnp.split(cs @ ada_w[layer] + ada_b[layer], 3, axis=-1)
        h = _rms(x) * norm_g[layer] * (1.0 + scale[:, None]) + shift[:, None]
        if kind == 0:
            hg_lb, hg_w_in, hg_norm_g, hg_w_out = hg
            lb = jnp.cumsum(jax.nn.softmax(hg_lb.astype(F32), axis=0), axis=0)[layer]
            y = _hgrn2_mixer(h, lb, hg_w_in[j], hg_norm_g[j], hg_w_out[j])
        elif kind == 1:
            (hy_w_in, hy_b_in, hy_conv_w, hy_conv_b, hy_f_w1, hy_f_b1, hy_f_w2, hy_f_b2,
             hy_f_wout, hy_f_freq, hy_skip, hy_w_out) = hy
            y = _hyena_mixer(h, hy_w_in[j], hy_b_in[j], hy_conv_w[j], hy_conv_b[j], hy_f_w1[j],
                             hy_f_b1[j], hy_f_w2[j], hy_f_b2[j], hy_f_wout[j], hy_f_freq[j],
                             hy_skip[j], hy_w_out[j])
        elif kind == 2:
            rt_w_in, rt_gn_g, rt_w_out = rt
            y = _retention_mixer(h, rt_w_in[j], rt_gn_g[j], rt_w_out[j])
        else:
            lru_w_in, lru_conv_w, lru_conv_b, lru_gate_w, lru_gate_b, lru_lambda, lru_w_out = lru
            y = _rglru_mixer(h, lru_w_in[j], lru_conv_w[j], lru_conv_b[j], lru_gate_w[j],
                             lru_gate_b[j], lru_lambda[j], lru_w_out[j])
        x = x + gate[:, None] * y
    return _rms(x) * final_g


def setup_inputs(seed: int = 0) -> dict:
    key = jax.random.key(seed)
    ks = jax.random.split(key, 34)
    D, E = D_MODEL, E_WIDTH

    def nrm(k, shape, s):
        return s * jax.random.normal(k, shape, F32)

    a8 = jax.random.uniform(ks[32], (N_LRU, 2, E), F32, 0.9, 0.999)
    a = a8 ** (1.0 / LRU_C)
    return {
        'x_prompt': nrm(ks[0], (BATCH, SEQ, D), 1.0),
        'x_sample': nrm(ks[1], (DEC_BATCH, DEC_SEQ, D), 1.0),
        'c_prompt': nrm(ks[2], (BATCH, D), 1.0),
        'c_sample': nrm(ks[3], (DEC_BATCH, D), 1.0),
        'ada_w': nrm(ks[4], (DEPTH, D, 3 * D), 0.5 * D ** -0.5),
        'ada_b': nrm(ks[5], (DEPTH, 3 * D), 0.02),
        'norm_g': 1.0 + nrm(ks[6], (DEPTH, D), 0.02),
        'final_g': 1.0 + nrm(ks[7], (D,), 0.02),
        'hg_lb': nrm(ks[8], (DEPTH + 1, E), 0.1),
        'hg_w_in': nrm(ks[9], (N_HG, D, 5 * E), D ** -0.5),
        'hg_norm_g': 1.0 + nrm(ks[10], (N_HG, HG_DV), 0.02),
        'hg_w_out': nrm(ks[11], (N_HG, E, D), E ** -0.5),
        'hy_w_in': nrm(ks[12], (N_HY, D, 4 * E), D ** -0.5),
        'hy_b_in': nrm(ks[13], (N_HY, 4 * E), 0.02),
        'hy_conv_w': nrm(ks[14], (N_HY, HY_SHORT, 3 * E), HY_SHORT ** -0.5),
        'hy_conv_b': nrm(ks[15], (N_HY, 3 * E), 0.02),
        'hy_f_w1': nrm(ks[16], (N_HY, HY_EMB, HY_FH), HY_EMB ** -0.5),
        'hy_f_b1': nrm(ks[17], (N_HY, HY_FH), 0.1),
        'hy_f_w2': nrm(ks[18], (N_HY, HY_INNER, HY_FH, HY_FH), HY_FH ** -0.5),
        'hy_f_b2': nrm(ks[19], (N_HY, HY_INNER, HY_FH), 0.1),
        'hy_f_wout': nrm(ks[20], (N_HY, HY_FH, 2 * E), HY_FILTER_SCALE * HY_FH ** -0.5),
        'hy_f_freq': 1.0 + nrm(ks[21], (N_HY, HY_FH), 0.02),
        'hy_skip': nrm(ks[22], (N_HY, E), 0.5),
        'hy_w_out': nrm(ks[23], (N_HY, E, D), E ** -0.5),
        'rt_w_in': nrm(ks[24], (N_RT, D, 2 * RT_QK + 2 * E), D ** -0.5),
        'rt_gn_g': 1.0 + nrm(ks[25], (N_RT, E), 0.02),
        'rt_w_out': nrm(ks[26], (N_RT, E, D), E ** -0.5),
        'lru_w_in': nrm(ks[27], (N_LRU, D, 2 * E), D ** -0.5),
        'lru_conv_w': nrm(ks[28], (N_LRU, LRU_CONV, E), LRU_CONV ** -0.5),
        'lru_conv_b': nrm(ks[29], (N_LRU, E), 0.02),
        'lru_gate_w': nrm(ks[30], (N_LRU, 2, 2, LRU_BLOCKS, LRU_BS, LRU_BS), LRU_BS ** -0.5),
        'lru_gate_b': nrm(ks[31], (N_LRU, 2, 2, E), 0.02),
        'lru_lambda': jnp.log(a) - jnp.log1p(-a),
        'lru_w_out': nrm(ks[33], (N_LRU, E, D), E ** -0.5),
    }


def reference(x_prompt, x_sample, c_prompt, c_sample, ada_w, ada_b, norm_g, final_g,
              hg_lb, hg_w_in, hg_norm_g, hg_w_out,
              hy_w_in, hy_b_in, hy_conv_w, hy_conv_b, hy_f_w1, hy_f_b1, hy_f_w2, hy_f_b2,
              hy_f_wout, hy_f_freq, hy_skip, hy_w_out,
              rt_w_in, rt_gn_g, rt_w_out,
              lru_w_in, lru_conv_w, lru_conv_b, lru_gate_w, lru_gate_b, lru_lambda, lru_w_out):
    hg = (hg_lb, hg_w_in, hg_norm_g, hg_w_out)
    hy = (hy_w_in, hy_b_in, hy_conv_w, hy_conv_b, hy_f_w1, hy_f_b1, hy_f_w2, hy_f_b2,
          hy_f_wout, hy_f_freq, hy_skip, hy_w_out)
    rt = (rt_w_in, rt_gn_g, rt_w_out)
    lru = (lru_w_in, lru_conv_w, lru_conv_b, lru_gate_w, lru_gate_b, lru_lambda, lru_w_out)
    y_prompt = _trunk(x_prompt, c_prompt, ada_w, ada_b, norm_g, final_g, hg, hy, rt, lru).astype(x_prompt.dtype)
    y_sample = _trunk(x_sample, c_sample, ada_w, ada_b, norm_g, final_g, hg, hy, rt, lru).astype(x_sample.dtype)
    return (y_prompt, y_sample)
```

```python
import numpy as np
from contextlib import ExitStack
import concourse.bass as bass
import concourse.mybir as mybir
from concourse.bass_utils import run_bass_kernel_spmd

F32 = mybir.dt.float32
BF16 = mybir.dt.bfloat16
I32 = mybir.dt.int32
ALU = mybir.AluOpType
AF = mybir.ActivationFunctionType

D = 1024
E = 2048
CH = 64
TB = 512
EPS = 1e-6

def _isz(dt):
    return mybir.dt.size(dt)


def _region(ap):
    name = ap.tensor.name
    pat = ap.ap
    off = int(ap.offset)
    space = str(ap.space)
    z = _isz(ap.dtype)
    if space in ("SB", "PSUM"):
        pstride = pat[0][0]
        p_lo = off // pstride
        p_hi = p_lo + pat[0][1]
        base = off % pstride
        lo = base
        hi = base
        for st, cn in pat[1:]:
            ext = st * (cn - 1)
            if ext < 0:
                lo += ext
            else:
                hi += ext
        return (name, p_lo, p_hi, lo * z, (hi + 1) * z)
    pitch = int(ap.tensor.shape[-1])
    r_lo = r_hi = off // pitch
    c_lo = c_hi = off % pitch
    for st, cn in pat:
        ext = st * (cn - 1)
        if abs(st) >= pitch and st % pitch == 0:
            e = ext // pitch
            if e < 0:
                r_lo += e
            else:
                r_hi += e
        else:
            if ext < 0:
                c_lo += ext
            else:
                c_hi += ext
    if c_lo < 0 or c_hi >= pitch:
        r_lo += c_lo // pitch
        r_hi += c_hi // pitch
        c_lo, c_hi = 0, pitch - 1
    return (name, r_lo, r_hi + 1, c_lo, c_hi + 1)


def _overlap(a, b):
    return a[1] < b[2] and b[1] < a[2] and a[3] < b[4] and b[3] < a[4]


def _contains(a, b):
    return a[1] <= b[1] and b[2] <= a[2] and a[3] <= b[3] and b[4] <= a[4]


class Op:
    __slots__ = ("eng", "fn", "deps", "stream", "signal", "val", "sem", "dma_snap")

    def __init__(self, eng, fn, deps, stream):
        self.eng = eng
        self.fn = fn
        self.deps = deps
        self.stream = stream
        self.signal = False
        self.val = 0
        self.sem = None
        self.dma_snap = None


class Prog:
    ENGS = ("pe", "dve", "act", "pool", "sp")

    def __init__(self, nc):
        self.nc = nc
        self.ops = []
        self.acc = {}
        self.stack = ExitStack()
        self.n = 0
        self.barrier_at = []

    ARENA = 196 * 1024

    def sbuf(self, name, shape, dt):
        if not hasattr(self, "arena"):
            self.arena = self.stack.enter_context(self.nc.sbuf_tensor("arena", [128, self.ARENA], mybir.dt.uint8))
            self.bump = 0
            self.water = 0
        n = 1
        for d in shape[1:]:
            n *= d
        size = (n * _isz(dt) + 63) // 64 * 64
        off = self.bump
        self.bump += size
        assert self.bump <= self.ARENA, ("SBUF overflow", name, self.bump)
        v = self.arena[0:shape[0], off:off + n * _isz(dt)].bitcast(dt)
        if len(shape) == 3:
            v = v.rearrange("p (a b) -> p a b", a=shape[1])
        elif len(shape) == 4:
            v = v.rearrange("p (a b c) -> p a b c", a=shape[1], b=shape[2])
        return v

    def mark(self):
        self.water = self.bump

    def reset(self):
        self.bump = self.water

    def psum(self, name, shape, dt=F32):
        return self.stack.enter_context(self.nc.psum_tensor(name, list(shape), dt))

    def dram(self, name, shape, dt, kind="Internal"):
        return self.nc.dram_tensor(name, list(shape), dt, kind=kind).ap()

    def _deps(self, reg, is_write, idx, eng, is_dma):
        lst = self.acc.setdefault(reg[0], [])
        deps = []
        keep = []
        for (r, j, w, e, d) in lst:
            if _overlap(r, reg):
                if is_write or w:
                    same = (e == eng) and not d and not is_dma
                    if same:
                        if w and not is_write and eng != "pe":
                            deps.append(j)
                    else:
                        deps.append(j)
                if is_write and _contains(reg, r):
                    continue
                if (not is_write) and (not w) and e == eng and not d and not is_dma and r == reg:
                    continue
            keep.append((r, j, w, e, d))
        keep.append((reg, idx, is_write, eng, is_dma))
        self.acc[reg[0]] = keep
        return deps

    def add(self, eng, fn, reads=(), writes=(), stream=None, extra_deps=()):
        idx = len(self.ops)
        is_dma = stream is not None
        deps = set(extra_deps)
        for ap in reads:
            if ap is None:
                continue
            deps.update(self._deps(_region(ap), False, idx, eng, is_dma))
        for ap in writes:
            if ap is None:
                continue
            deps.update(self._deps(_region(ap), True, idx, eng, is_dma))
        deps.discard(idx)
        self.ops.append(Op(eng, fn, deps, stream))
        return idx

    def barrier(self):
        self.barrier_at.append(len(self.ops))
        self.acc = {}

    def dma(self, out, in_, stream, q="sp", **kw):
        return self.add(q, lambda e: e.dma_start(out=out, in_=in_, **kw), [in_], [out], stream=stream)

    def mm(self, out, lhsT, rhs, start=True, stop=True, extra_reads=()):
        return self.add("pe", lambda e: e.matmul(out, lhsT, rhs, start=start, stop=stop),
                        [lhsT, rhs] + list(extra_reads) + ([] if start else [out]), [out])

    def transpose(self, out, in_, ident):
        return self.add("pe", lambda e: e.transpose(out, in_, ident), [in_, ident], [out])

    def act(self, out, in_, func, bias=0.0, scale=1.0, accum_out=None, eng="act"):
        rd = [in_]
        if not isinstance(bias, (int, float)):
            rd.append(bias)
        if not isinstance(scale, (int, float)):
            rd.append(scale)
        wr = [out]
        kw = {}
        if accum_out is not None:
            wr.append(accum_out)
            kw["accum_out"] = accum_out
        return self.add(eng, lambda e: e.activation(out, in_, func, bias=bias, scale=scale, **kw), rd, wr)

    def tt(self, out, a, b, op, eng="dve"):
        return self.add(eng, lambda e: e.tensor_tensor(out, a, b, op), [a, b], [out])

    def ts(self, out, a, s1, s2, op0, op1=None, eng="dve", accum_out=None):
        rd = [a]
        if not isinstance(s1, (int, float)):
            rd.append(s1)
        if s2 is not None and not isinstance(s2, (int, float)):
            rd.append(s2)
        wr = [out]
        kw = {}
        if accum_out is not None:
            wr.append(accum_out)
            kw["accum_out"] = accum_out
        if op1 is None:
            return self.add(eng, lambda e: e.tensor_scalar(out, a, s1, None, op0, **kw), rd, wr)
        return self.add(eng, lambda e: e.tensor_scalar(out, a, s1, s2, op0, op1, **kw), rd, wr)

    def stt(self, out, a, s, b, op0, op1, eng="dve"):
        rd = [a, b]
        if not isinstance(s, (int, float)):
            rd.append(s)
        return self.add("dve", lambda e: e.scalar_tensor_tensor(out, a, s, b, op0, op1), rd, [out])

    def copy(self, out, in_, eng="dve"):
        if eng == "act":
            return self.add(eng, lambda e: e.copy(out, in_), [in_], [out])
        return self.add(eng, lambda e: e.tensor_copy(out, in_), [in_], [out])

    def memset(self, out, v, eng="pool"):
        return self.add(eng, lambda e: e.memset(out, v), [], [out])

    def scan(self, out, d0, d1, init, op0=ALU.mult, op1=ALU.add):
        rd = [d0, d1]
        if not isinstance(init, (int, float)):
            rd.append(init)
        return self.add("dve", lambda e: e.tensor_tensor_scan(out, d0, d1, init, op0, op1), rd, [out])

    def finalize(self, final_streams_wait=True):
        nc = self.nc
        engs = {"pe": nc.tensor, "dve": nc.vector, "act": nc.scalar, "pool": nc.gpsimd, "sp": nc.sync}
        ops = self.ops
        for op in ops:
            for j in op.deps:
                if ops[j].stream is None:
                    ops[j].signal = True
        last_before = []
        for b in self.barrier_at:
            lb = {}
            for i in range(b - 1, -1, -1):
                o = ops[i]
                if o.stream is None and o.eng not in lb:
                    lb[o.eng] = i
                    if len(lb) == 5:
                        break
            for i in lb.values():
                ops[i].signal = True
            last_before.append(lb)
        sems = {}

        def getsem(key):
            if key not in sems:
                sems[key] = self.stack.enter_context(nc.semaphore("s_" + key))
            return sems[key]

        cnt = {}
        stream_hist = {}
        active = {}
        freep = []
        nphys = 0
        bset = set(self.barrier_at)
        for i, op in enumerate(ops):
            if i in bset:
                freep.extend(sorted(active.values()))
                active = {}
            if op.stream is not None:
                if op.stream not in active:
                    if freep:
                        active[op.stream] = freep.pop(0)
                    else:
                        active[op.stream] = nphys
                        nphys += 1
                k = "d%d" % active[op.stream]
                cnt[k] = cnt.get(k, 0) + 16
                op.sem = k
                op.val = cnt[k]
            elif op.signal:
                k = "e_" + op.eng
                cnt[k] = cnt.get(k, 0) + 1
                op.sem = k
                op.val = cnt[k]
        waited = {e: {} for e in self.ENGS}
        stream_cnt = {}
        bi = 0
        barrier_pending = {e: None for e in self.ENGS}
        for i, op in enumerate(ops):
            while bi < len(self.barrier_at) and self.barrier_at[bi] <= i:
                snap = {}
                for e, j in last_before[bi].items():
                    snap[ops[j].sem] = ops[j].val
                for k, v in stream_cnt.items():
                    snap[k] = v
                for e in self.ENGS:
                    barrier_pending[e] = dict(snap) if barrier_pending[e] is None else {**barrier_pending[e], **snap}
                bi += 1
            eng = engs[op.eng]
            need = {}
            if barrier_pending[op.eng] is not None:
                need.update(barrier_pending[op.eng])
                barrier_pending[op.eng] = None
            for j in op.deps:
                d = ops[j]
                if d.stream is not None:
                    v = stream_cnt[d.sem]
                else:
                    v = d.val
                if need.get(d.sem, 0) < v:
                    need[d.sem] = v
            w = waited[op.eng]
            for k, v in need.items():
                if k == "e_" + op.eng and op.eng == "pe":
                    continue
                if w.get(k, 0) < v:
                    eng.wait_ge(getsem(k), v)
                    w[k] = v
            ins = op.fn(eng)
            if op.stream is not None:
                ins.then_inc(getsem(op.sem), 16)
                stream_cnt[op.sem] = op.val
            elif op.signal:
                ins.then_inc(getsem(op.sem), 1)
        if final_streams_wait:
            for k, v in stream_cnt.items():
                if waited["sp"].get(k, 0) < v:
                    nc.sync.wait_ge(getsem(k), v)
        self.nsems = len(sems)
        self.counts = cnt


class Cfg:
    def __init__(self, NS=4, SL=2048, kinds=(0, 1, 2, 3)):
        self.NS = NS
        self.SL = SL
        self.T = NS * SL
        self.NCH = self.T // CH
        self.NTB = self.T // TB
        self.kinds = tuple(kinds)
        self.depth = len(kinds)
        self.NB = 128
        self.NA = (2 * self.T) // 128


class K:
    pass


def bc_rows(ap_row, n):
    return ap_row.to_broadcast([n, ap_row.shape[-1]])


def build(cfg):
    nc = bass.Bass("TRN2", target_bir_lowering=False)
    P = Prog(nc)
    k = K()
    k.P = P
    k.cfg = cfg
    T, NS, SL, NCH, NTB = cfg.T, cfg.NS, cfg.SL, cfg.NCH, cfg.NTB
    dp = cfg.depth
    din = lambda n, s, dt=F32: P.dram(n, s, dt, "ExternalInput")
    k.xin = din("xin", [T, D])
    k.cin = din("cin", [NS, D])
    k.ada_w = din("ada_w", [dp, D, 3 * D])
    k.ada_b = din("ada_b", [dp, 3 * D])
    k.norm_g = din("norm_g", [dp, D])
    k.final_g = din("final_g", [1, D])
    k.hg_lb = din("hg_lb", [5, E])
    k.hg_w_in = din("hg_w_in", [D, 5 * E])
    k.hg_norm_g = din("hg_norm_g", [1, 128])
    k.hg_w_out = din("hg_w_out", [E, D])
    if 2 in cfg.kinds:
        k.rt_w_in = din("rt_w_in", [D, 6144])
        k.rt_gn_g = din("rt_gn_g", [1, E])
        k.rt_w_out = din("rt_w_out", [E, D])
        k.k_cos = din("k_cos", [128, T])
        k.k_sin = din("k_sin", [128, T])
        k.k_dtab = din("k_dtab", [128, 16 * 64])
        k.k_rtsc = din("k_rtsc", [2, 3, 1024, NCH])
    if 3 in cfg.kinds:
        k.lru_w_in = din("lru_w_in", [D, 2 * E])
        k.lru_conv_w = din("lru_conv_w", [4, E])
        k.lru_conv_b = din("lru_conv_b", [1, E])
        k.lru_gate_w = din("lru_gate_w", [2, 2, 16, 128, 128])
        k.lru_gate_b = din("lru_gate_b", [2, 2, E])
        k.lru_lambda = din("lru_lambda", [2, E])
        k.lru_w_out = din("lru_w_out", [E, D])
    if 1 in cfg.kinds:
        NA, NB = cfg.NA, cfg.NB
        NF = 2 * T
        k.hy_w_in = din("hy_w_in", [D, 4 * E])
        k.hy_b_in = din("hy_b_in", [1, 4 * E])
        k.hy_conv_w = din("hy_conv_w", [3, 3 * E])
        k.hy_conv_b = din("hy_conv_b", [1, 3 * E])
        k.hy_f_w1 = din("hy_f_w1", [33, 64])
        k.hy_f_b1 = din("hy_f_b1", [1, 64])
        k.hy_f_w2 = din("hy_f_w2", [2, 64, 64])
        k.hy_f_b2 = din("hy_f_b2", [2, 64])
        k.hy_f_wout = din("hy_f_wout", [64, 2 * E])
        k.hy_f_freq = din("hy_f_freq", [1, 64])
        k.hy_skip = din("hy_skip", [1, E])
        k.hy_w_out = din("hy_w_out", [E, D])
        k.k_fa = din("k_fa", [NA, 2 * NA])
        k.k_fb = din("k_fb", [NB, 3 * NB])
        k.k_fc = din("k_fc", [NB, 4 * NB])
        k.k_tw1 = din("k_tw1", [NB, 2 * NA])
        k.k_tw2 = din("k_tw2", [NA, 2 * NB])
        k.k_zT = din("k_zT", [33, NF])
        k.k_fmask = din("k_fmask", [2, NF])
        k.k_trow = din("k_trow", [1, NF])
        k.k_delta = din("k_delta", [128, 16])
        k.k_smask = din("k_smask", [128, NS])
        k.PR = P.dram("PR", [3 * E, T], F32)
        k.TAPS = P.dram("TAPS", [E, NF], F32)
        k.HS = P.dram("HS", [2, NB, E, NA], F32)
    k.k_carry1 = din("k_carry1", [128, 1])
    k.k_ident = din("k_ident", [128, 128])
    k.k_tri = din("k_tri", [64, 128])
    k.k_rmask = din("k_rmask", [128, TB])
    k.k_carry = din("k_carry", [128, 2 * NCH])
    k.yout = P.dram("yout", [T, D], F32, "ExternalOutput")
    k.X = P.dram("X", [T, D], F32)
    k.MOD = P.dram("MOD", [dp, NS, 3 * D], F32)
    k.HT = P.dram("HT", [D, T], BF16)
    k.QT = P.dram("QT", [E, T], F32)
    k.ZT = P.dram("ZT", [E, T], F32)
    k.QK = P.dram("QK", [2, 2, E, T], BF16)
    k.SC = P.dram("SC", [2, 3, E, NCH], F32)
    k.V = P.dram("V", [T, E], BF16)
    k.OT = P.dram("OT", [2, E, T], F32)
    k.ident = P.sbuf("ident", [128, 128], F32)
    k.identb = P.sbuf("identb", [128, 128], BF16)
    k.ones = P.sbuf("ones", [128, 128], F32)
    k.tri = P.sbuf("tri", [64, 128], F32)
    k.rmask = P.sbuf("rmask", [128, TB], F32)
    k.carry = P.sbuf("carry", [128, 2 * NCH], F32)
    k.epsT = P.sbuf("epsT", [128, 1], F32)
    P.dma(k.ident[:], k.k_ident, "c0")
    P.dma(k.tri[:], k.k_tri, "c1")
    P.dma(k.rmask[:], k.k_rmask, "c2")
    P.dma(k.carry[:], k.k_carry, "c3")
    k.carry1 = P.sbuf("carry1", [128, 1], F32)
    P.dma(k.carry1[:], k.k_carry1, "c4")
    P.copy(k.identb[:], k.ident[:])
    P.memset(k.ones[:], 1.0)
    P.memset(k.epsT[:], EPS)
    k.ps = P.psum("ps", [128, 6, 512], F32)
    k.psb = P.psum("psb", [128, 2, 1024], BF16)

    P.mark()
    phase_mod(k)
    P.barrier()
    for l, kind in enumerate(cfg.kinds):
        xsrc = k.xin if l == 0 else k.X
        phase_norm(k, l, xsrc)
        P.barrier()
        if kind == 0:
            hgrn2(k, l, xsrc)
        elif kind == 1:
            hyena(k, l, xsrc)
        elif kind == 2:
            retention(k, l, xsrc)
        elif kind == 3:
            rglru(k, l, xsrc)
        else:
            raise NotImplementedError
        P.barrier()
    phase_final(k, k.xin if dp == 0 else k.X)
    P.finalize()
    return nc, P


def phase_mod(k):
    P, cfg = k.P, k.cfg
    P.reset()
    NS = cfg.NS
    cT = P.sbuf("m_cT", [128, 8, NS], F32)
    sg = P.sbuf("m_sg", [128, 8, NS], F32)
    csT = P.sbuf("m_csT", [128, 8, NS], F32)
    for kt in range(8):
        P.dma(cT[:, kt, :], k.cin[:, kt * 128:(kt + 1) * 128].rearrange("s p -> p s"), "m_c",
              allow_slow_non_contiguous=True)
    P.act(sg[:], cT[:], AF.Sigmoid)
    P.tt(csT[:], cT[:], sg[:], ALU.mult)
    w = [P.sbuf("m_w%d" % i, [128, 8, 512], F32) for i in range(2)]
    bb = [P.sbuf("m_b%d" % i, [NS, 512], F32) for i in range(2)]
    ob = [P.sbuf("m_o%d" % i, [NS, 512], F32) for i in range(2)]
    it = 0
    for l in range(cfg.depth):
        for cb in range(6):
            s = it % 2
            P.dma(w[s][:], k.ada_w[l, :, cb * 512:(cb + 1) * 512].rearrange("(kt p) n -> p kt n", p=128), "m_w%d" % s)
            P.dma(bb[s][:], bc_rows(k.ada_b[l:l + 1, cb * 512:(cb + 1) * 512], NS), "m_b%d" % s)
            pb = k.ps[0:NS, it % 2, :]
            for kt in range(8):
                P.mm(pb, csT[:, kt, :], w[s][:, kt, :], start=(kt == 0), stop=(kt == 7))
            P.tt(ob[s][:], pb, bb[s][:], ALU.add)
            P.dma(k.MOD[l, :, cb * 512:(cb + 1) * 512], ob[s][:], "m_o%d" % s)
            it += 1


def rms_rstd(P, k, xt, junk, ss, rstd):
    P.act(junk, xt, AF.Square, accum_out=ss)
    P.act(rstd, ss, AF.Sqrt, bias=k.epsT[:], scale=1.0 / D)
    P.add("dve", lambda e: e.reciprocal(rstd, rstd), [rstd], [rstd])


def phase_norm(k, l, xsrc):
    P, cfg = k.P, k.cfg
    P.reset()
    T, NS, SL = cfg.T, cfg.NS, cfg.SL
    g_bc = P.sbuf("n_g", [128, D], F32)
    A_bc = P.sbuf("n_A", [128, D], F32)
    sh_bc = P.sbuf("n_sh", [128, D], F32)
    xt = [P.sbuf("n_x%d" % i, [128, D], F32) for i in range(2)]
    junk = P.sbuf("n_junk", [128, D], F32)
    hb = [P.sbuf("n_h%d" % i, [128, D], BF16) for i in range(2)]
    hT = [P.sbuf("n_hT%d" % i, [128, 8, TB], BF16) for i in range(2)]
    ss = [P.sbuf("n_ss%d" % i, [128, 1], F32) for i in range(2)]
    rstd = [P.sbuf("n_rs%d" % i, [128, 1], F32) for i in range(2)]
    P.dma(g_bc[:], bc_rows(k.norm_g[l:l + 1, :], 128), "n_g")
    ntile = T // 128
    P.dma(xt[0][:], xsrc[0:128, :], "n_x0")
    for i in range(ntile):
        s = i % 2
        slot = (i * 128) // SL
        if (i * 128) % SL == 0:
            P.dma(A_bc[:], bc_rows(k.MOD[l, slot:slot + 1, D:2 * D], 128), "n_A")
            P.dma(sh_bc[:], bc_rows(k.MOD[l, slot:slot + 1, 0:D], 128), "n_sh")
            P.stt(A_bc[:], A_bc[:], 1.0, g_bc[:], ALU.add, ALU.mult)
        if i + 1 < ntile:
            P.dma(xt[1 - s][:], xsrc[(i + 1) * 128:(i + 2) * 128, :], "n_x%d" % (1 - s))
        rms_rstd(P, k, xt[s][:], junk[:], ss[s][:], rstd[s][:])
        P.stt(junk[:], xt[s][:], rstd[s][:], A_bc[:], ALU.mult, ALU.mult)
        P.tt(hb[s][:], junk[:], sh_bc[:], ALU.add, eng="pool")
        tb, sub = divmod(i, 4)
        hs = tb % 2
        for kt in range(8):
            pst = k.psb[:, kt % 2, (kt // 2) * 128:(kt // 2) * 128 + 128] if False else k.psb[:, kt // 4, (kt % 4) * 128:(kt % 4) * 128 + 128]
            P.transpose(pst, hb[s][:, kt * 128:(kt + 1) * 128], k.identb[:])
        for half in range(2):
            src = k.psb[:, half, 0:512].rearrange("p (a b) -> p a b", a=4)
            dst = hT[hs][:, half * 4:half * 4 + 4, sub * 128:(sub + 1) * 128]
            P.copy(dst, src, eng=("act" if half == 0 else "dve"))
        if sub == 3:
            P.dma(k.HT[:, tb * TB:(tb + 1) * TB].rearrange("(kt p) t -> p kt t", p=128), hT[hs][:], "n_hT%d" % hs)


def phase_final(k, xsrc):
    P, cfg = k.P, k.cfg
    P.reset()
    T = cfg.T
    g_bc = P.sbuf("f_g", [128, D], F32)
    xt = [P.sbuf("f_x%d" % i, [128, D], F32) for i in range(2)]
    yt = [P.sbuf("f_y%d" % i, [128, D], F32) for i in range(2)]
    junk = P.sbuf("f_junk", [128, D], F32)
    ss = [P.sbuf("f_ss%d" % i, [128, 1], F32) for i in range(2)]
    rstd = [P.sbuf("f_rs%d" % i, [128, 1], F32) for i in range(2)]
    P.dma(g_bc[:], bc_rows(k.final_g[0:1, :], 128), "f_g")
    ntile = T // 128
    P.dma(xt[0][:], xsrc[0:128, :], "f_x0")
    for i in range(ntile):
        s = i % 2
        if i + 1 < ntile:
            P.dma(xt[1 - s][:], xsrc[(i + 1) * 128:(i + 2) * 128, :], "f_x%d" % (1 - s))
        rms_rstd(P, k, xt[s][:], junk[:], ss[s][:], rstd[s][:])
        P.stt(yt[s][:], xt[s][:], rstd[s][:], g_bc[:], ALU.mult, ALU.mult)
        P.dma(k.yout[i * 128:(i + 1) * 128, :], yt[s][:], "f_y%d" % s)


def gemm_fm(k, W, groups, epi, pfx):
    P, cfg = k.P, k.cfg
    NTB = cfg.NTB
    wst = [P.sbuf(pfx + "wst%d" % i, [128, 8, 512], F32) for i in range(2)]
    wbf = [P.sbuf(pfx + "wbf%d" % i, [128, 8, 512], BF16) for i in range(2)]
    hT = [P.sbuf(pfx + "hT%d" % i, [128, 8, TB], BF16) for i in range(2)]

    def loadw(gi):
        s = gi % 2
        for j, (col, tag) in enumerate(groups[gi]):
            P.dma(wst[s][:, :, j * 128:(j + 1) * 128], W[:, col:col + 128].rearrange("(kt p) n -> p kt n", p=128),
                  pfx + "w%d" % s)
        ncol = 128 * len(groups[gi])
        P.copy(wbf[s][:, :, 0:ncol], wst[s][:, :, 0:ncol], eng="pool")

    items = [(gi, tb) for gi in range(len(groups)) for tb in range(NTB)]

    def loadh(ii):
        gi, tb = items[ii]
        P.dma(hT[ii % 2][:], k.HT[:, tb * TB:(tb + 1) * TB].rearrange("(kt p) t -> p kt t", p=128), pfx + "h%d" % (ii % 2))

    loadw(0)
    loadh(0)
    it = 0
    for ii, (gi, tb) in enumerate(items):
        if ii + 1 < len(items):
            loadh(ii + 1)
        if tb == 0 and gi + 1 < len(groups):
            loadw(gi + 1)
        for j, (col, tag) in enumerate(groups[gi]):
            pb = k.ps[:, it % 4, :]
            for kt in range(8):
                P.mm(pb, wbf[gi % 2][:, kt, j * 128:(j + 1) * 128], hT[ii % 2][:, kt, :], start=(kt == 0), stop=(kt == 7))
            epi(tag, col, tb, pb, it)
            it += 1


def gemm_tm(k, W, col0, ncols, dst, pfx):
    P, cfg = k.P, k.cfg
    NTB = cfg.NTB
    wst = [P.sbuf(pfx + "wst%d" % i, [128, 8, 512], F32) for i in range(2)]
    wbf = [P.sbuf(pfx + "wbf%d" % i, [128, 8, 512], BF16) for i in range(2)]
    hT = [P.sbuf(pfx + "hT%d" % i, [128, 8, TB], BF16) for i in range(2)]
    vt = [P.sbuf(pfx + "vt%d" % i, [128, 512], BF16) for i in range(2)]
    ng = ncols // 512

    def loadw(gi):
        s = gi % 2
        P.dma(wst[s][:], W[:, col0 + gi * 512:col0 + (gi + 1) * 512].rearrange("(kt p) n -> p kt n", p=128), pfx + "w%d" % s)
        P.copy(wbf[s][:], wst[s][:], eng="pool")

    items = [(gi, tb) for gi in range(ng) for tb in range(NTB)]

    def loadh(ii):
        gi, tb = items[ii]
        P.dma(hT[ii % 2][:], k.HT[:, tb * TB:(tb + 1) * TB].rearrange("(kt p) t -> p kt t", p=128), pfx + "h%d" % (ii % 2))

    loadw(0)
    loadh(0)
    it = 0
    for ii, (gi, tb) in enumerate(items):
        if ii + 1 < len(items):
            loadh(ii + 1)
        if tb == 0 and gi + 1 < ng:
            loadw(gi + 1)
        for sub in range(4):
            pb = k.ps[:, 4 + it % 2, :]
            for kt in range(8):
                P.mm(pb, hT[ii % 2][:, kt, sub * 128:(sub + 1) * 128], wbf[gi % 2][:, kt, :], start=(kt == 0), stop=(kt == 7))
            P.copy(vt[it % 2][:], pb, eng=("act" if it % 2 == 0 else "dve"))
            r0 = tb * TB + sub * 128
            P.dma(dst[r0:r0 + 128, gi * 512:(gi + 1) * 512], vt[it % 2][:], pfx + "v%d" % (it % 2))
            it += 1


def chunk_engine(k, ND, NV, NU, G, pfx, SCsrc=None):
    P, cfg = k.P, k.cfg
    NCH = cfg.NCH
    SCsrc = k.SC if SCsrc is None else SCsrc
    CB = 4
    BW = CB * CH
    NTB = NCH // CB
    NG = NU // G
    GD = G * ND
    GV = G * NV
    VW = NV * 128
    S = [P.sbuf(pfx + "S%d" % g, [128, GD, VW], F32) for g in range(NG)]
    Sbf = [P.sbuf(pfx + "Sb%d" % g, [128, GD, VW], BF16) for g in range(NG)]
    t1 = [P.sbuf(pfx + "t1%d" % i, [128, GD, VW], F32) for i in range(2)]
    t2 = [P.sbuf(pfx + "t2%d" % i, [128, GD, VW], F32) for i in range(2)]
    PT = [P.sbuf(pfx + "PT%d" % i, [64, G, 64], BF16) for i in range(2)]
    ktok = [P.sbuf(pfx + "kt%d" % i, [64, GD * 128], BF16) for i in range(2)]
    qT = [[P.sbuf(pfx + "q%d_%d" % (b, g), [128, GD, BW], BF16) for g in range(NG)] for b in range(2)]
    kT = [[P.sbuf(pfx + "k%d_%d" % (b, g), [128, GD, BW], BF16) for g in range(NG)] for b in range(2)]
    vb = [[P.sbuf(pfx + "v%d_%d" % (b, g), [64, CB, GV * 128], BF16) for g in range(NG)] for b in range(2)]
    sc = [[P.sbuf(pfx + "s%d_%d" % (b, g), [128, 3, GD, CB], F32) for g in range(NG)] for b in range(2)]
    ob = [[P.sbuf(pfx + "o%d_%d" % (b, g), [128, GV, BW], F32) for g in range(NG)] for b in range(2)]

    def load(dr, bi, blk):
        b = bi % 2
        for g in range(NG):
            r0 = g * GD * 128
            tsl = slice(blk * BW, (blk + 1) * BW)
            P.dma(qT[b][g][:], k.QK[dr, 0, r0:r0 + GD * 128, tsl].rearrange("(j p) t -> p j t", p=128), pfx + "lq%d_%d" % (b, g))
            P.dma(kT[b][g][:], k.QK[dr, 1, r0:r0 + GD * 128, tsl].rearrange("(j p) t -> p j t", p=128), pfx + "lk%d_%d" % (b, g))
            c0 = g * GV * 128
            P.dma(vb[b][g][:], k.V[tsl, c0:c0 + GV * 128].rearrange("(c s) n -> s c n", s=64), pfx + "lv%d_%d" % (b, g))
            for j3 in range(3):
                P.dma(sc[b][g][:, j3, :, :], SCsrc[dr, j3, r0:r0 + GD * 128, blk * CB:(blk + 1) * CB].rearrange("(j p) c -> p j c", p=128),
                      pfx + "ls%d_%d" % (b, g), allow_slow_non_contiguous=True)

    it = 0
    for dr in range(2):
        for g in range(NG):
            P.memset(S[g][:], 0.0)
        blks = list(range(NTB)) if dr == 0 else list(range(NTB - 1, -1, -1))
        load(dr, 0, blks[0])
        for bi, blk in enumerate(blks):
            b = bi % 2
            if bi + 1 < len(blks):
                load(dr, bi + 1, blks[bi + 1])
            cs = list(range(CB)) if dr == 0 else list(range(CB - 1, -1, -1))
            for c in cs:
                csl = slice(c * 64, (c + 1) * 64)
                for g in range(NG):
                    p2 = it % 2
                    psA = k.ps[:, p2, :]
                    psS = k.ps[:, 2 + 2 * p2:4 + 2 * p2, :] if GD * VW > 512 else k.ps[:, 2 + p2:3 + p2, :]
                    psS = psS.rearrange("p a b -> p (a b)")
                    pstr = k.psb[0:64, p2, 0:GD * 128]
                    for u in range(G):
                        for j in range(ND):
                            P.mm(psA[0:64, u * 64:(u + 1) * 64], kT[b][g][:, u * ND + j, csl], qT[b][g][:, u * ND + j, csl],
                                 start=(j == 0), stop=(j == ND - 1))
                    for uj in range(GD):
                        P.transpose(pstr[:, uj * 128:(uj + 1) * 128], kT[b][g][:, uj, csl], k.identb[:])
                    trim = k.tri[:, dr * 64:(dr + 1) * 64]
                    P.tt(PT[p2][:], psA[0:64, 0:G * 64].rearrange("p (u t) -> p u t", u=G),
                         trim.rearrange("p (o t) -> p o t", o=1).to_broadcast([64, G, 64]), ALU.mult)
                    P.copy(ktok[p2][:], pstr, eng="act")
                    P.tt(Sbf[g][:], S[g][:], sc[b][g][:, 0, :, c:c + 1].to_broadcast([128, GD, VW]), ALU.mult, eng="pool")
                    for u in range(G):
                        for i in range(NV):
                            po = psA[:, 256 + (u * NV + i) * 64:256 + (u * NV + i + 1) * 64]
                            P.mm(po, vb[b][g][:, c, (u * NV + i) * 128:(u * NV + i + 1) * 128], PT[p2][:, u, :], start=True, stop=False)
                            for j in range(ND):
                                P.mm(po, Sbf[g][:, u * ND + j, i * 128:(i + 1) * 128], qT[b][g][:, u * ND + j, csl],
                                     start=False, stop=(j == ND - 1))
                    P.copy(ob[b][g][:, :, csl], psA[:, 256:256 + GV * 64].rearrange("p (a t) -> p a t", a=GV), eng="act")
                    for uj in range(GD):
                        u = uj // ND
                        P.mm(psS[:, uj * VW:(uj + 1) * VW], ktok[p2][:, uj * 128:(uj + 1) * 128],
                             vb[b][g][:, c, u * VW:(u + 1) * VW], start=True, stop=True)
                    P.tt(t1[p2][:], S[g][:], sc[b][g][:, 1, :, c:c + 1].to_broadcast([128, GD, VW]), ALU.mult, eng="pool")
                    P.tt(t2[p2][:], psS.rearrange("p (a v) -> p a v", a=GD),
                         sc[b][g][:, 2, :, c:c + 1].to_broadcast([128, GD, VW]), ALU.mult)
                    P.tt(S[g][:], t1[p2][:], t2[p2][:], ALU.add)
                    it += 1
            for g in range(NG):
                r0 = g * GV * 128
                P.dma(k.OT[dr, r0:r0 + GV * 128, blk * BW:(blk + 1) * BW].rearrange("(i p) t -> p i t", p=128), ob[b][g][:],
                      pfx + "so%d_%d" % (b, g))


def outproj_setup(k, l, Wout, pfx):
    P = k.P
    wo = P.sbuf(pfx + "wo", [128, 16, D], BF16)
    wst = [P.sbuf(pfx + "wos%d" % i, [128, 16, 256], F32) for i in range(2)]
    for q in range(4):
        P.dma(wst[q % 2][:], Wout[:, q * 256:(q + 1) * 256].rearrange("(e p) n -> p e n", p=128), pfx + "wo%d" % (q % 2))
        P.copy(wo[:, :, q * 256:(q + 1) * 256], wst[q % 2][:], eng="pool")
    st = K()
    st.wo = wo
    st.gate = P.sbuf(pfx + "gate", [128, D], F32)
    st.xt = [P.sbuf(pfx + "x%d" % i, [128, D], F32) for i in range(2)]
    st.ty = [P.sbuf(pfx + "ty%d" % i, [128, D], F32) for i in range(2)]
    st.it = 0
    return st


def outproj_block(k, l, tb, GT, st, xsrc, pfx):
    P, cfg = k.P, k.cfg
    if (tb * TB) % cfg.SL == 0:
        slot = (tb * TB) // cfg.SL
        P.dma(st.gate[:], bc_rows(k.MOD[l, slot:slot + 1, 2 * D:3 * D], 128), pfx + "gate")
    for sub in range(4):
        s = st.it % 2
        r0 = tb * TB + sub * 128
        P.dma(st.xt[s][:], xsrc[r0:r0 + 128, :], pfx + "x%d" % s)
        for dh in range(2):
            pb = k.ps[:, 4 + dh, :]
            for e in range(16):
                P.mm(pb, GT[:, e, sub * 128:(sub + 1) * 128], st.wo[:, e, dh * 512:(dh + 1) * 512], start=(e == 0), stop=(e == 15))
            P.tt(st.ty[s][:, dh * 512:(dh + 1) * 512], pb, st.gate[:, dh * 512:(dh + 1) * 512], ALU.mult)
        P.tt(st.ty[s][:], st.ty[s][:], st.xt[s][:], ALU.add, eng="pool")
        P.dma(k.X[r0:r0 + 128, :], st.ty[s][:], pfx + "y%d" % s)
        st.it += 1


def hgrn2(k, l, xsrc):
    P, cfg = k.P, k.cfg
    NTB, NCH = cfg.NTB, cfg.NCH
    W = k.hg_w_in
    P.reset()
    oq = [P.sbuf("ha_o%d" % i, [128, TB], F32) for i in range(2)]
    sgq = [P.sbuf("ha_s%d" % i, [128, TB], F32) for i in range(2)]

    def epiA(tag, col, tb, pb, it):
        s = it % 2
        dst = k.QT if tag == "q" else k.ZT
        row = col if tag == "q" else col - 4 * E
        P.act(sgq[s][:], pb, AF.Sigmoid)
        P.tt(oq[s][:], pb, sgq[s][:], ALU.mult)
        P.dma(dst[row:row + 128, tb * TB:(tb + 1) * TB], oq[s][:], "ha_o%d" % s)

    cols = [(h * 128, "q") for h in range(16)] + [(4 * E + h * 128, "z") for h in range(16)]
    groups = [cols[i:i + 4] for i in range(0, len(cols), 4)]
    gemm_fm(k, W, groups, epiA, "ha_")
    gemm_tm(k, W, 3 * E, E, k.V, "hv_")
    P.barrier()
    P.reset()
    lbr = P.sbuf("hb_lbr", [128, 16, 5], F32)
    lbe = P.sbuf("hb_lbe", [128, 16, 5], F32)
    den = P.sbuf("hb_den", [128, 16], F32)
    num = P.sbuf("hb_num", [128, 16], F32)
    lbv = P.sbuf("hb_lb", [128, 16], F32)
    oml = P.sbuf("hb_oml", [128, 16], F32)
    for r in range(5):
        P.dma(lbr[:, :, r], k.hg_lb[r, :].rearrange("(j p) -> p j", p=128), "hb_lb", allow_slow_non_contiguous=True)
    P.act(lbe[:], lbr[:], AF.Exp)
    P.add("dve", lambda e: e.reduce_sum(den[:], lbe[:], mybir.AxisListType.X), [lbe[:]], [den[:]])
    P.add("dve", lambda e: e.reduce_sum(num[:], lbe[:, :, 0:l + 1], mybir.AxisListType.X), [lbe[:]], [num[:]])
    P.add("dve", lambda e: e.reciprocal(den[:], den[:]), [den[:]], [den[:]])
    P.tt(lbv[:], num[:], den[:], ALU.mult)
    P.ts(oml[:], lbv[:], -1.0, 1.0, ALU.mult, ALU.add)
    nb = 2
    tl = lambda nm, sh, dt=F32: [P.sbuf("hb_%s%d" % (nm, i), sh, dt) for i in range(nb)]
    qb, sg, ff, gg, kk, bb, bc, e1, e2 = (tl(n, [128, TB]) for n in ("qb", "sg", "ff", "gg", "kk", "bb", "bc", "e1", "e2"))
    qt = tl("qt", [128, TB], BF16)
    kt_ = tl("kt", [128, TB], BF16)
    scs = tl("sc", [128, 3, 8])
    dtmp = tl("dt", [128, 8])

    def epiB(tag, col, tb, pb, it):
        _, dr, h = tag
        s = it % nb
        tsl = slice(tb * TB, (tb + 1) * TB)
        P.dma(qb[s][:], k.QT[h * 128:(h + 1) * 128, tsl], "hb_q%d" % s)
        P.act(sg[s][:], pb, AF.Sigmoid)
        P.ts(ff[s][:], sg[s][:], oml[:, h:h + 1], lbv[:, h:h + 1], ALU.mult, ALU.add)
        P.act(gg[s][:], ff[s][:], AF.Ln)
        P.ts(kk[s][:], ff[s][:], -1.0, 1.0, ALU.mult, ALU.add, eng="pool")
        if dr == 0:
            P.scan(bb[s][:], k.rmask[:], gg[s][:], 0.0)
        else:
            P.scan(bb[s][:, ::-1], k.rmask[:], gg[s][:, ::-1], 0.0)
        b3 = bb[s][:].rearrange("p (c t) -> p c t", t=64)
        P.tt(bc[s][:].rearrange("p (c t) -> p c t", t=64), b3, b3[:, :, 32:33].to_broadcast([128, 8, 64]), ALU.subtract, eng="pool")
        P.act(e1[s][:], bc[s][:], AF.Exp)
        P.act(e2[s][:], bc[s][:], AF.Exp, scale=-1.0)
        P.tt(qt[s][:], qb[s][:], e1[s][:], ALU.mult)
        P.tt(kt_[s][:], kk[s][:], e2[s][:], ALU.mult, eng="pool")
        refc = b3[:, :, 32]
        lastc = b3[:, :, 63] if dr == 0 else b3[:, :, 0]
        P.act(scs[s][:, 0, :], refc, AF.Exp)
        P.act(scs[s][:, 1, :], lastc, AF.Exp)
        P.tt(dtmp[s][:], lastc, refc, ALU.subtract)
        P.act(scs[s][:, 2, :], dtmp[s][:], AF.Exp)
        cr = k.carry[:, dr * NCH + tb * 8:dr * NCH + tb * 8 + 8]
        P.tt(scs[s][:, 0:2, :], scs[s][:, 0:2, :], cr.rearrange("p (o c) -> p o c", o=1).to_broadcast([128, 2, 8]), ALU.mult)
        rows = slice(h * 128, (h + 1) * 128)
        P.dma(k.QK[dr, 0, rows, tsl], qt[s][:], "hb_sq%d" % s)
        P.dma(k.QK[dr, 1, rows, tsl], kt_[s][:], "hb_sk%d" % s)
        P.dma(k.SC[dr, :, rows, tb * 8:(tb + 1) * 8].rearrange("j p c -> p j c"), scs[s][:], "hb_ss%d" % s,
              allow_slow_non_contiguous=True)

    cols = [(E + dr * E + h * 128, ("f", dr, h)) for dr in range(2) for h in range(16)]
    groups = [cols[i:i + 4] for i in range(0, len(cols), 4)]
    gemm_fm(k, W, groups, epiB, "hb_")
    P.barrier()
    P.reset()
    chunk_engine(k, 1, 1, 16, 4, "hc_")
    P.barrier()
    P.reset()
    st = outproj_setup(k, l, k.hg_w_out, "hd_")
    ngc = P.sbuf("hd_ng", [128, 1], F32)
    P.dma(ngc[:], k.hg_norm_g.rearrange("o p -> p o"), "hd_ng", allow_slow_non_contiguous=True)
    GT = [P.sbuf("hd_GT%d" % i, [128, 16, TB], BF16) for i in range(2)]
    tl = lambda nm, sh, dt=F32: [P.sbuf("hd_%s%d" % (nm, i), sh, dt) for i in range(2)]
    of, obk, zz, oo, sq, rs = (tl(n, [128, TB]) for n in ("of", "ob", "zz", "oo", "sq", "rs"))
    it = 0
    for tb in range(NTB):
        tsl = slice(tb * TB, (tb + 1) * TB)
        for h in range(16):
            s = it % 2
            rows = slice(h * 128, (h + 1) * 128)
            P.dma(of[s][:], k.OT[0, rows, tsl], "hd_of%d" % s)
            P.dma(obk[s][:], k.OT[1, rows, tsl], "hd_ob%d" % s)
            P.dma(zz[s][:], k.ZT[rows, tsl], "hd_zz%d" % s)
            P.tt(oo[s][:], of[s][:], obk[s][:], ALU.add, eng="pool")
            P.act(sq[s][:], oo[s][:], AF.Square)
            pb = k.ps[:, it % 4, :]
            P.mm(pb, k.ones[:], sq[s][:])
            P.act(rs[s][:], pb, AF.Sqrt, bias=k.epsT[:], scale=1.0 / 128)
            P.add("dve", lambda e, a=rs[s][:]: e.reciprocal(a, a), [rs[s][:]], [rs[s][:]])
            P.tt(oo[s][:], oo[s][:], rs[s][:], ALU.mult)
            P.stt(GT[tb % 2][:, h, :], oo[s][:], ngc[:], zz[s][:], ALU.mult, ALU.mult)
            it += 1
        outproj_block(k, l, tb, GT[tb % 2], st, xsrc, "hd_")


def host_consts(cfg, carry):
    NCH = cfg.NCH
    cps = cfg.SL // CH
    c = {}
    c["k_ident"] = np.eye(128, dtype=np.float32)
    ii = np.arange(64)
    fw = (ii[:, None] <= ii[None, :]).astype(np.float32)
    bw = (ii[:, None] >= ii[None, :]).astype(np.float32)
    c["k_tri"] = np.concatenate([fw, bw], axis=1)
    rm = np.ones((128, TB), np.float32)
    rm[:, ::64] = 0.0
    c["k_rmask"] = rm
    cr = np.ones((128, 2 * NCH), np.float32)
    for ch in range(NCH):
        if ch % cps == 0 and ch > 0:
            cr[:, ch] = carry
        if (ch + 1) % cps == 0 and ch < NCH - 1:
            cr[:, NCH + ch] = carry
    c["k_carry"] = cr
    c["k_carry1"] = np.full((128, 1), carry, np.float32)
    return c


def host_consts_rt(cfg, carry, seq_pos):
    NCH = cfg.NCH
    cps = cfg.SL // CH
    c = {}
    inv = (10000.0 ** (-np.arange(0, 256, 2, dtype=np.float32) / np.float32(256))).astype(np.float32)
    ang = (seq_pos.astype(np.float32)[None, :] * inv[:, None]).astype(np.float32)
    c["k_cos"] = np.cos(ang).astype(np.float32)
    c["k_sin"] = np.sin(ang).astype(np.float32)
    hidx = np.arange(4, dtype=np.float32)
    lg = [np.log1p(-np.exp2(-5.0 - hidx)).astype(np.float32), np.log1p(-np.exp2(-5.5 - hidx)).astype(np.float32)]
    pos = np.arange(64, dtype=np.float64)
    dt = np.zeros((2, 4, 2, 64), np.float64)
    sc = np.ones((2, 3, 1024, NCH), np.float64)
    for dr in range(2):
        cnt = (pos + 1.0) if dr == 0 else (64.0 - pos)
        for hd in range(4):
            g = float(lg[dr][hd])
            dt[dr, hd, 0] = np.exp(g * cnt)
            dt[dr, hd, 1] = np.exp(-g * cnt) * (256.0 ** -0.5)
            rows = slice(hd * 256, (hd + 1) * 256)
            sc[dr, 1, rows, :] = np.exp(g * 64.0)
            sc[dr, 2, rows, :] = np.exp(g * 64.0)
        for ch in range(NCH):
            bnd = (ch % cps == 0 and ch > 0) if dr == 0 else ((ch + 1) % cps == 0 and ch < NCH - 1)
            if bnd:
                sc[dr, 0, :, ch] *= carry
                sc[dr, 1, :, ch] *= carry
    c["k_dtab"] = np.broadcast_to(dt.reshape(1, -1), (128, 16 * 64)).astype(np.float32).copy()
    c["k_rtsc"] = sc.astype(np.float32)
    return c


WNAMES = ["ada_w", "ada_b", "norm_g", "hg_lb", "hg_w_in", "hg_norm_g", "hg_w_out"]


def core_inputs(cfg, x_slots, c_slots, carry, w, seq_pos=None, Lc=None):
    SL = cfg.SL
    xin = np.zeros((cfg.T, D), np.float32)
    cin = np.zeros((cfg.NS, D), np.float32)
    for s in range(cfg.NS):
        if x_slots[s] is not None:
            xin[s * SL:(s + 1) * SL] = x_slots[s]
            cin[s] = c_slots[s]
    m = {"xin": xin, "cin": cin}
    m.update(host_consts(cfg, carry))
    dp = cfg.depth
    m["ada_w"] = np.ascontiguousarray(w["ada_w"][:dp])
    m["ada_b"] = np.ascontiguousarray(w["ada_b"][:dp])
    m["norm_g"] = np.ascontiguousarray(w["norm_g"][:dp])
    m["final_g"] = np.ascontiguousarray(w["final_g"]).reshape(1, D)
    m["hg_lb"] = np.ascontiguousarray(w["hg_lb"])
    m["hg_w_in"] = np.ascontiguousarray(w["hg_w_in"][0])
    m["hg_norm_g"] = np.ascontiguousarray(w["hg_norm_g"][0]).reshape(1, 128)
    m["hg_w_out"] = np.ascontiguousarray(w["hg_w_out"][0])
    if 2 in cfg.kinds:
        m.update(host_consts_rt(cfg, carry, seq_pos))
        m["rt_w_in"] = np.ascontiguousarray(w["rt_w_in"][0])
        m["rt_gn_g"] = np.ascontiguousarray(w["rt_gn_g"][0]).reshape(1, E)
        m["rt_w_out"] = np.ascontiguousarray(w["rt_w_out"][0])
    if 1 in cfg.kinds:
        m.update(host_consts_hy(cfg, Lc, [xs is not None for xs in x_slots]))
        for nm in ["hy_w_in", "hy_conv_w", "hy_f_w1", "hy_f_w2", "hy_f_b2", "hy_f_wout", "hy_w_out"]:
            m[nm] = np.ascontiguousarray(w[nm][0])
        for nm in ["hy_b_in", "hy_conv_b", "hy_f_b1", "hy_f_freq", "hy_skip"]:
            m[nm] = np.ascontiguousarray(w[nm][0]).reshape(1, -1)
    if 3 in cfg.kinds:
        m["lru_w_in"] = np.ascontiguousarray(w["lru_w_in"][0])
        m["lru_conv_w"] = np.ascontiguousarray(w["lru_conv_w"][0])
        m["lru_conv_b"] = np.ascontiguousarray(w["lru_conv_b"][0]).reshape(1, E)
        m["lru_gate_w"] = np.ascontiguousarray(w["lru_gate_w"][0])
        m["lru_gate_b"] = np.ascontiguousarray(w["lru_gate_b"][0])
        m["lru_lambda"] = np.ascontiguousarray(w["lru_lambda"][0])
        m["lru_w_out"] = np.ascontiguousarray(w["lru_w_out"][0])
    return m


def retention(k, l, xsrc):
    P, cfg = k.P, k.cfg
    NTB, NCH = cfg.NTB, cfg.NCH
    W = k.rt_w_in
    P.reset()
    dtab = P.sbuf("ra_dtab", [128, 16, 64], F32)
    P.dma(dtab[:], k.k_dtab.rearrange("p (a b) -> p a b", a=16), "ra_dt")
    tl = lambda nm, sh, dt=F32, n=2: [P.sbuf("ra_%s%d" % (nm, i), sh, dt) for i in range(n)]
    x1, x2, cs_, sn_, o1, o2, ta, tb_ = (tl(n, [128, TB]) for n in ("x1", "x2", "cs", "sn", "o1", "o2", "ta", "tb"))
    obf = tl("obf", [128, TB], BF16, 4)
    oz = tl("oz", [128, TB])
    sgz = tl("sgz", [128, TB])
    cnt = {"p": 0, "o": 0}

    def epiA(tag, col, tb, pb, it):
        kind, hd, a = tag
        tsl = slice(tb * TB, (tb + 1) * TB)
        if kind == "g":
            s = it % 2
            P.act(sgz[s][:], pb, AF.Sigmoid)
            P.tt(oz[s][:], pb, sgz[s][:], ALU.mult)
            P.dma(k.ZT[col - 4096:col - 4096 + 128, tsl], oz[s][:], "ra_oz%d" % s)
            return
        s = cnt["p"] % 2
        if a == 0:
            P.copy(x1[s][:], pb, eng="act")
            return
        P.copy(x2[s][:], pb, eng="act")
        cnt["p"] += 1
        qk = 0 if kind == "q" else 1
        P.dma(cs_[s][:], k.k_cos[:, tsl], "ra_c%d" % s)
        P.dma(sn_[s][:], k.k_sin[:, tsl], "ra_s%d" % s)
        P.tt(o1[s][:], x1[s][:], cs_[s][:], ALU.mult)
        P.tt(ta[s][:], x2[s][:], sn_[s][:], ALU.mult, eng="pool")
        P.tt(o1[s][:], o1[s][:], ta[s][:], ALU.subtract)
        P.tt(o2[s][:], x1[s][:], sn_[s][:], ALU.mult, eng="pool")
        P.tt(tb_[s][:], x2[s][:], cs_[s][:], ALU.mult)
        P.tt(o2[s][:], o2[s][:], tb_[s][:], ALU.add, eng="pool")
        for dr in range(2):
            dsl = dtab[:, (dr * 4 + hd) * 2 + qk, :].rearrange("p (o t) -> p o t", o=1).to_broadcast([128, TB // 64, 64])
            for half, src in enumerate((o1[s], o2[s])):
                ob_ = obf[cnt["o"] % 4]
                P.tt(ob_[:].rearrange("p (c t) -> p c t", t=64), src[:].rearrange("p (c t) -> p c t", t=64), dsl, ALU.mult,
                     eng=("dve" if half == 0 else "pool"))
                r0 = (hd * 2 + half) * 128
                P.dma(k.QK[dr, qk, r0:r0 + 128, tsl], ob_[:], "ra_so%d" % (cnt["o"] % 4))
                cnt["o"] += 1

    groups = []
    for hd in range(4):
        groups.append([(hd * 256, ("q", hd, 0)), (hd * 256 + 128, ("q", hd, 1)),
                       (1024 + hd * 256, ("k", hd, 0)), (1024 + hd * 256 + 128, ("k", hd, 1))])
    gcols = [(4096 + j * 128, ("g", 0, 0)) for j in range(16)]
    groups += [gcols[i:i + 4] for i in range(0, 16, 4)]
    gemm_fm(k, W, groups, epiA, "ra_")
    gemm_tm(k, W, 2048, E, k.V, "rv_")
    P.barrier()
    P.reset()
    chunk_engine(k, 2, 4, 4, 1, "rc_", SCsrc=k.k_rtsc)
    P.barrier()
    P.reset()
    st = outproj_setup(k, l, k.rt_w_out, "rd_")
    gn = P.sbuf("rd_gn", [128, 16], F32)
    P.dma(gn[:], k.rt_gn_g.rearrange("o (j p) -> p (o j)", p=128), "rd_gn", allow_slow_non_contiguous=True)
    GT = [P.sbuf("rd_GT%d" % i, [128, 16, TB], BF16) for i in range(2)]
    tl = lambda nm, sh, dt=F32, n=2: [P.sbuf("rd_%s%d" % (nm, i), sh, dt) for i in range(n)]
    of, obk, zz, sq = (tl(n, [128, TB]) for n in ("of", "ob", "zz", "sq"))
    oo = tl("oo", [128, TB], F32, 8)
    rs = tl("rs", [128, TB])
    it = 0
    hi = 0
    for tb in range(NTB):
        tsl = slice(tb * TB, (tb + 1) * TB)
        for hd in range(4):
            pb = k.ps[:, hi % 4, :]
            for i in range(4):
                e = hd * 4 + i
                s = it % 2
                rows = slice(e * 128, (e + 1) * 128)
                P.dma(of[s][:], k.OT[0, rows, tsl], "rd_of%d" % s)
                P.dma(obk[s][:], k.OT[1, rows, tsl], "rd_ob%d" % s)
                o_ = oo[(hi % 2) * 4 + i]
                P.tt(o_[:], of[s][:], obk[s][:], ALU.add, eng="pool")
                P.act(sq[s][:], o_[:], AF.Square)
                P.mm(pb, k.ones[:], sq[s][:], start=(i == 0), stop=(i == 3))
                it += 1
            r_ = rs[hi % 2]
            P.act(r_[:], pb, AF.Sqrt, bias=k.epsT[:], scale=1.0 / 512)
            P.add("dve", lambda e_, a=r_[:]: e_.reciprocal(a, a), [r_[:]], [r_[:]])
            for i in range(4):
                e = hd * 4 + i
                s = it % 2
                o_ = oo[(hi % 2) * 4 + i]
                P.dma(zz[s][:], k.ZT[e * 128:(e + 1) * 128, tsl], "rd_zz%d" % s)
                P.tt(o_[:], o_[:], r_[:], ALU.mult, eng=("dve" if i % 2 == 0 else "pool"))
                P.stt(GT[tb % 2][:, e, :], o_[:], gn[:, e:e + 1], zz[s][:], ALU.mult, ALU.mult, eng=("dve" if i % 2 == 1 else "pool"))
                it += 1
            hi += 1
        outproj_block(k, l, tb, GT[tb % 2], st, xsrc, "rd_")


def rglru(k, l, xsrc):
    P, cfg = k.P, k.cfg
    NTB, T, SL = cfg.NTB, cfg.T, cfg.SL
    W = k.lru_w_in
    P.reset()
    oq = [P.sbuf("la_o%d" % i, [128, TB], F32) for i in range(2)]
    sgq = [P.sbuf("la_s%d" % i, [128, TB], F32) for i in range(2)]

    def epiA(tag, col, tb, pb, it):
        s = it % 2
        tsl = slice(tb * TB, (tb + 1) * TB)
        if tag == "x":
            P.copy(oq[s][:], pb, eng="act")
            P.dma(k.QT[col:col + 128, tsl], oq[s][:], "la_o%d" % s)
        else:
            P.act(sgq[s][:], pb, AF.Sigmoid)
            P.tt(oq[s][:], pb, sgq[s][:], ALU.mult)
            P.dma(k.ZT[col - E:col - E + 128, tsl], oq[s][:], "la_o%d" % s)

    cols = [(j * 128, "x") for j in range(16)] + [(E + j * 128, "z") for j in range(16)]
    groups = [cols[i:i + 4] for i in range(0, 32, 4)]
    gemm_fm(k, W, groups, epiA, "la_")
    P.barrier()
    P.reset()
    BL = min(1024, SL)
    NB = T // BL
    cw = P.sbuf("lb_cw", [128, 16, 4], F32)
    cb = P.sbuf("lb_cb", [128, 16], F32)
    gb = P.sbuf("lb_gb", [128, 4, 16], F32)
    lam = P.sbuf("lb_lam", [128, 2, 16], F32)
    m8 = P.sbuf("lb_m8", [128, 2, 16], F32)
    for jj in range(4):
        P.dma(cw[:, :, jj], k.lru_conv_w[jj, :].rearrange("(j p) -> p j", p=128), "lb_c", allow_slow_non_contiguous=True)
    P.dma(cb[:], k.lru_conv_b.rearrange("o (j p) -> p (o j)", p=128), "lb_c", allow_slow_non_contiguous=True)
    for d in range(2):
        P.dma(lam[:, d, :], k.lru_lambda[d, :].rearrange("(j p) -> p j", p=128), "lb_c", allow_slow_non_contiguous=True)
        for g in range(2):
            P.dma(gb[:, d * 2 + g, :], k.lru_gate_b[d, g, :].rearrange("(j p) -> p j", p=128), "lb_c", allow_slow_non_contiguous=True)
    P.act(m8[:], lam[:], AF.Exp, scale=-1.0)
    P.act(m8[:], m8[:], AF.Ln, bias=k.ones[:, 0:1], scale=1.0)
    P.ts(m8[:], m8[:], -8.0, None, ALU.mult)
    gw = [P.sbuf("lb_gw%d" % i, [128, 4, 128], F32) for i in range(2)]
    tl = lambda nm, sh, dt=F32, n=2: [P.sbuf("lb_%s%d" % (nm, i), sh, dt) for i in range(n)]
    xr = tl("xr", [128, BL + 3])
    xb, rr, ii, aa, a2, sq, bx, bb, hh, hf, zz = (tl(n, [128, BL]) for n in ("xb", "rr", "ii", "aa", "a2", "sq", "bx", "bb", "hh", "hf", "zz"))
    gt = tl("gt", [128, BL], BF16)
    hinit = tl("hi", [128, 1])
    it = 0
    for j in range(16):
        rows = slice(j * 128, (j + 1) * 128)
        gws = gw[j % 2]
        for d in range(2):
            for g in range(2):
                P.dma(gws[:, d * 2 + g, :], k.lru_gate_w[d, g, j, :, :], "lb_gw%d" % (j % 2))
        for d in range(2):
            order = list(range(NB)) if d == 0 else list(range(NB - 1, -1, -1))
            prev = None
            for bi in order:
                s = it % 2
                t0 = bi * BL
                lo = max(t0 - 2, 0)
                hi_ = min(t0 + BL + 1, T)
                if t0 == 0:
                    P.memset(xr[s][:, 0:2], 0.0)
                if t0 + BL == T:
                    P.memset(xr[s][:, BL + 2:BL + 3], 0.0)
                P.dma(xr[s][:, lo - (t0 - 2):hi_ - (t0 - 2)], k.QT[rows, lo:hi_], "lb_x%d" % s)
                if t0 % SL == 0 and t0 > 0:
                    P.ts(xr[s][:, 0:2], xr[s][:, 0:2], k.carry1[:], None, ALU.mult, eng="pool")
                if (t0 + BL) % SL == 0 and t0 + BL < T:
                    P.ts(xr[s][:, BL + 2:BL + 3], xr[s][:, BL + 2:BL + 3], k.carry1[:], None, ALU.mult, eng="pool")
                P.ts(xb[s][:], xr[s][:, 0:BL], cw[:, j, 0:1], cb[:, j:j + 1], ALU.mult, ALU.add)
                P.stt(xb[s][:], xr[s][:, 1:BL + 1], cw[:, j, 1:2], xb[s][:], ALU.mult, ALU.add, eng="pool")
                P.stt(xb[s][:], xr[s][:, 2:BL + 2], cw[:, j, 2:3], xb[s][:], ALU.mult, ALU.add)
                P.stt(xb[s][:], xr[s][:, 3:BL + 3], cw[:, j, 3:4], xb[s][:], ALU.mult, ALU.add, eng="pool")
                for hb in range(BL // 512):
                    hsl = slice(hb * 512, (hb + 1) * 512)
                    p0 = k.ps[:, (2 * hb) % 4, :]
                    p1 = k.ps[:, (2 * hb + 1) % 4, :]
                    P.mm(p0, gws[:, d * 2 + 0, :], xb[s][:, hsl])
                    P.mm(p1, gws[:, d * 2 + 1, :], xb[s][:, hsl])
                    P.act(rr[s][:, hsl], p0, AF.Sigmoid, bias=gb[:, d * 2 + 0, j:j + 1])
                    P.act(ii[s][:, hsl], p1, AF.Sigmoid, bias=gb[:, d * 2 + 1, j:j + 1])
                P.act(aa[s][:], rr[s][:], AF.Exp, scale=m8[:, d, j:j + 1])
                P.tt(a2[s][:], aa[s][:], aa[s][:], ALU.mult, eng="pool")
                P.act(sq[s][:], a2[s][:], AF.Sqrt, bias=k.ones[:, 0:1], scale=-1.0)
                P.tt(bx[s][:], ii[s][:], xb[s][:], ALU.mult)
                P.tt(bb[s][:], sq[s][:], bx[s][:], ALU.mult, eng="pool")
                if prev is None:
                    init = 0.0
                else:
                    bnd = (t0 % SL == 0) if d == 0 else ((t0 + BL) % SL == 0)
                    if bnd:
                        P.ts(hinit[s][:], prev, k.carry1[:], None, ALU.mult)
                        init = hinit[s][:]
                    else:
                        init = prev
                if d == 0:
                    P.scan(hh[s][:], aa[s][:], bb[s][:], init)
                    prev = hh[s][:, BL - 1:BL]
                    P.dma(k.OT[0, rows, t0:t0 + BL], hh[s][:], "lb_sh%d" % s)
                else:
                    P.scan(hh[s][:, ::-1], aa[s][:, ::-1], bb[s][:, ::-1], init)
                    prev = hh[s][:, 0:1]
                    P.dma(hf[s][:], k.OT[0, rows, t0:t0 + BL], "lb_lf%d" % s)
                    P.dma(zz[s][:], k.ZT[rows, t0:t0 + BL], "lb_lz%d" % s)
                    P.tt(hf[s][:], hf[s][:], hh[s][:], ALU.add, eng="pool")
                    P.tt(gt[s][:], hf[s][:], zz[s][:], ALU.mult)
                    P.dma(k.QK[0, 0, rows, t0:t0 + BL], gt[s][:], "lb_sg%d" % s)
                it += 1
    P.barrier()
    P.reset()
    st = outproj_setup(k, l, k.lru_w_out, "lc_")
    GT = [P.sbuf("lc_GT%d" % i, [128, 16, TB], BF16) for i in range(2)]
    for tb in range(NTB):
        P.dma(GT[tb % 2][:], k.QK[0, 0, :, tb * TB:(tb + 1) * TB].rearrange("(e p) t -> p e t", p=128), "lc_g%d" % (tb % 2))
        outproj_block(k, l, tb, GT[tb % 2], st, xsrc, "lc_")


def host_consts_hy(cfg, Lc, real_slots):
    NA, NB, T = cfg.NA, cfg.NB, cfg.T
    NF = 2 * T
    c = {}
    a = np.arange(NA, dtype=np.float64)
    b = np.arange(NB, dtype=np.float64)
    tha = 2 * np.pi * np.outer(a, a) / NA
    thb = 2 * np.pi * np.outer(b, b) / NB
    FAr, FAi = np.cos(tha), -np.sin(tha)
    FBr, FBi = np.cos(thb), -np.sin(thb)
    c["k_fa"] = np.concatenate([FAr, FAi], 1).astype(np.float32)
    c["k_fb"] = np.concatenate([FBr, FBi, -FBi], 1).astype(np.float32)
    c["k_fc"] = np.concatenate([FBr, -FBi, FBi, FBr], 1).astype(np.float32)
    th = 2 * np.pi * np.outer(b, a) / NF
    c["k_tw1"] = np.concatenate([np.cos(th), -np.sin(th)], 1).astype(np.float32)
    c["k_tw2"] = np.concatenate([np.cos(th.T), np.sin(th.T)], 1).astype(np.float32)
    n = np.arange(NF)
    mf = (n < Lc)
    mb = (n > NF - Lc) | (n == 0)
    pos = np.where(mf, n, np.where(mb, NF - n, 0)).astype(np.float32)
    pos[0] = 0.0
    t = (pos / np.float32(Lc - 1)).astype(np.float32)
    wv = (np.float32(2.0 * np.pi) * pos / np.float32(Lc)).astype(np.float32)
    bands = np.linspace(1e-4, 15, 16, dtype=np.float32)
    ang = (bands[:, None] * wv[None, :]).astype(np.float32)
    z = np.concatenate([t[None, :], np.cos(ang), -np.sin(ang)], 0).astype(np.float32)
    valid = (mf | mb)
    c["k_zT"] = (z * valid[None, :]).astype(np.float32)
    c["k_fmask"] = np.stack([mf, mb]).astype(np.float32)
    c["k_trow"] = (t * valid).astype(np.float32).reshape(1, NF)
    import math
    max_decay = math.log(1e-2) / 0.3
    min_decay = math.log(1e-2) / 1.5
    deltas = np.abs(np.linspace(min_decay, max_decay, E, dtype=np.float32))
    c["k_delta"] = np.ascontiguousarray(deltas.reshape(16, 128).T).astype(np.float32)
    sm = np.zeros((128, cfg.NS), np.float32)
    for s_, r in enumerate(real_slots):
        sm[:, s_] = 1.0 if r else 0.0
    c["k_smask"] = sm
    return c


def hy_fft(k, src, K1, pfx, sink):
    P, cfg = k.P, k.cfg
    NA, NB = cfg.NA, cfg.NB
    CG, LG = 4, 16
    xin = [P.sbuf(pfx + "xin%d" % i, [K1, LG, NB], F32) for i in range(2)]
    Bt = [P.sbuf(pfx + "B%d" % i, [NB, 2, CG, NA], F32) for i in range(2)]
    mt = [P.sbuf(pfx + "m%d" % i, [NB, CG, NA], F32) for i in range(4)]
    psA = k.ps[0:NB, 0:2, :].rearrange("p a b -> p (a b)")
    psXr = k.ps[0:NB, 2, 0:CG * NA]
    psXi = k.ps[0:NB, 3, 0:CG * NA]
    twr = k.tw1[:, 0:NA].rearrange("p (o k) -> p o k", o=1).to_broadcast([NB, CG, NA])
    twi = k.tw1[:, NA:2 * NA].rearrange("p (o k) -> p o k", o=1).to_broadcast([NB, CG, NA])
    it = 0
    for lg in range(E // LG):
        xs = xin[lg % 2]
        P.dma(xs[:], src[lg * LG:(lg + 1) * LG, 0:K1 * NB].rearrange("c (n1 n2) -> n1 c n2", n2=NB), pfx + "x%d" % (lg % 2))
        for gi in range(LG // CG):
            s = it % 2
            for cc in range(CG):
                P.mm(psA[:, cc * 2 * NA:(cc + 1) * 2 * NA], xs[:, gi * CG + cc, :], k.fa[0:K1, :])
            A4 = psA[:, 0:CG * 2 * NA].rearrange("p (c r k) -> p c r k", c=CG, r=2)
            Ar, Ai = A4[:, :, 0, :], A4[:, :, 1, :]
            P.tt(mt[0][:], Ar, twr, ALU.mult)
            P.tt(mt[1][:], Ai, twi, ALU.mult)
            P.tt(Bt[s][:, 0, :, :], mt[0][:], mt[1][:], ALU.subtract, eng="pool")
            P.tt(mt[2][:], Ar, twi, ALU.mult)
            P.tt(mt[3][:], Ai, twr, ALU.mult)
            P.tt(Bt[s][:, 1, :, :], mt[2][:], mt[3][:], ALU.add, eng="pool")
            Brf = Bt[s][:, 0, :, :].rearrange("p c k -> p (c k)")
            Bif = Bt[s][:, 1, :, :].rearrange("p c k -> p (c k)")
            P.mm(psXr, k.fb[:, 0:NB], Brf, start=True, stop=False)
            P.mm(psXr, k.fb[:, 2 * NB:3 * NB], Bif, start=False, stop=True)
            P.mm(psXi, k.fb[:, NB:2 * NB], Brf, start=True, stop=False)
            P.mm(psXi, k.fb[:, 0:NB], Bif, start=False, stop=True)
            sink(lg, gi, lg * LG + gi * CG, psXr, psXi)
            it += 1


def hyena(k, l, xsrc):
    P, cfg = k.P, k.cfg
    NTB, T, SL, NS = cfg.NTB, cfg.T, cfg.SL, cfg.NS
    NA, NB = cfg.NA, cfg.NB
    NF = 2 * T
    CG, LG = 4, 16
    W = k.hy_w_in
    UT, X1, YT = k.QT, k.OT[0], k.OT[1]
    P.reset()
    bin_ = P.sbuf("ya_bin", [128, 64], F32)
    P.dma(bin_[:], k.hy_b_in.rearrange("o (j p) -> p (o j)", p=128), "ya_b", allow_slow_non_contiguous=True)
    oq = [P.sbuf("ya_o%d" % i, [128, TB], F32) for i in range(2)]
    sgq = [P.sbuf("ya_s%d" % i, [128, TB], F32) for i in range(2)]
    zb = [P.sbuf("ya_z%d" % i, [128, TB], F32) for i in range(2)]

    def epiA(tag, col, tb, pb, it):
        s = it % 2
        j = col // 128
        tsl = slice(tb * TB, (tb + 1) * TB)
        if tag == "x":
            P.act(oq[s][:], pb, AF.Identity, bias=bin_[:, j:j + 1])
            P.dma(k.PR[col:col + 128, tsl], oq[s][:], "ya_o%d" % s)
        else:
            P.act(zb[s][:], pb, AF.Identity, bias=bin_[:, j:j + 1])
            P.act(sgq[s][:], pb, AF.Sigmoid, bias=bin_[:, j:j + 1])
            P.tt(oq[s][:], zb[s][:], sgq[s][:], ALU.mult)
            P.dma(k.ZT[col - 3 * E:col - 3 * E + 128, tsl], oq[s][:], "ya_o%d" % s)

    cols = [(j * 128, "x") for j in range(48)] + [(3 * E + j * 128, "z") for j in range(16)]
    groups = [cols[i:i + 4] for i in range(0, 64, 4)]
    gemm_fm(k, W, groups, epiA, "ya_")
    P.barrier()
    P.reset()
    BL = min(1024, SL)
    NBK = T // BL
    cw = P.sbuf("yb_cw", [128, 48, 3], F32)
    cb = P.sbuf("yb_cb", [128, 48], F32)
    smask = P.sbuf("yb_sm", [128, NS], F32)
    for jj in range(3):
        P.dma(cw[:, :, jj], k.hy_conv_w[jj, :].rearrange("(j p) -> p j", p=128), "yb_c", allow_slow_non_contiguous=True)
    P.dma(cb[:], k.hy_conv_b.rearrange("o (j p) -> p (o j)", p=128), "yb_c", allow_slow_non_contiguous=True)
    P.dma(smask[:], k.k_smask, "yb_c")
    xr = [[P.sbuf("yb_xr%d_%d" % (i, b), [128, BL + 2], F32) for b in range(3)] for i in range(2)]
    xc = [[P.sbuf("yb_xc%d_%d" % (i, b), [128, BL], F32) for b in range(3)] for i in range(2)]
    uu = [P.sbuf("yb_u%d" % i, [128, BL], F32) for i in range(2)]
    it = 0
    for j in range(16):
        for bi in range(NBK):
            s = it % 2
            t0 = bi * BL
            lo = max(t0 - 1, 0)
            hi_ = min(t0 + BL + 1, T)
            for b in range(3):
                jt = b * 16 + j
                xx = xr[s][b]
                if t0 == 0:
                    P.memset(xx[:, 0:1], 0.0)
                if t0 + BL == T:
                    P.memset(xx[:, BL + 1:BL + 2], 0.0)
                P.dma(xx[:, lo - (t0 - 1):hi_ - (t0 - 1)], k.PR[jt * 128:(jt + 1) * 128, lo:hi_], "yb_x%d_%d" % (s, b))
                if t0 % SL == 0 and t0 > 0:
                    P.ts(xx[:, 0:1], xx[:, 0:1], k.carry1[:], None, ALU.mult, eng="pool")
                if (t0 + BL) % SL == 0 and t0 + BL < T:
                    P.ts(xx[:, BL + 1:BL + 2], xx[:, BL + 1:BL + 2], k.carry1[:], None, ALU.mult, eng="pool")
                o = xc[s][b]
                P.ts(o[:], xx[:, 0:BL], cw[:, jt, 0:1], cb[:, jt:jt + 1], ALU.mult, ALU.add, eng=("pool" if b == 1 else "dve"))
                P.stt(o[:], xx[:, 1:BL + 1], cw[:, jt, 1:2], o[:], ALU.mult, ALU.add)
                P.stt(o[:], xx[:, 2:BL + 2], cw[:, jt, 2:3], o[:], ALU.mult, ALU.add)
            slot = t0 // SL
            P.stt(uu[s][:], xc[s][0][:], smask[:, slot:slot + 1], xc[s][2][:], ALU.mult, ALU.mult)
            P.dma(UT[j * 128:(j + 1) * 128, t0:t0 + BL], uu[s][:], "yb_su%d" % s)
            P.dma(X1[j * 128:(j + 1) * 128, t0:t0 + BL], xc[s][1][:], "yb_s1%d" % s)
            it += 1
    P.barrier()
    P.reset()
    w1 = P.sbuf("yc_w1", [33, 64], F32)
    w2 = P.sbuf("yc_w2", [64, 2, 64], F32)
    wout = P.sbuf("yc_wo", [64, 2 * E], F32)
    fq = P.sbuf("yc_fq", [64, 1], F32)
    fbr = P.sbuf("yc_fbr", [64, 3], F32)
    fbs = P.sbuf("yc_fbs", [64, 3], F32)
    ndel = P.sbuf("yc_nd", [128, 16], F32)
    P.dma(w1[:], k.hy_f_w1, "yc_c")
    for jj in range(2):
        P.dma(w2[:, jj, :], k.hy_f_w2[jj, :, :], "yc_c")
        P.dma(fbr[:, 1 + jj:2 + jj], k.hy_f_b2[jj:jj + 1, :].rearrange("o p -> p o"), "yc_c", allow_slow_non_contiguous=True)
    P.dma(wout[:], k.hy_f_wout, "yc_c")
    P.dma(fq[:], k.hy_f_freq.rearrange("o p -> p o"), "yc_c", allow_slow_non_contiguous=True)
    P.dma(fbr[:, 0:1], k.hy_f_b1.rearrange("o p -> p o"), "yc_c", allow_slow_non_contiguous=True)
    P.dma(ndel[:], k.k_delta, "yc_c")
    P.ts(fq[:], fq[:], float(1.0 / (2.0 * np.pi)), None, ALU.mult)
    P.ts(fbs[:], fbr[:], fq[:], None, ALU.mult)
    P.ts(ndel[:], ndel[:], -1.0, None, ALU.mult, eng="pool")
    tl = lambda nm, sh, dt=F32, n=2: [P.sbuf("yc_%s%d" % (nm, i), sh, dt) for i in range(n)]
    zt = tl("zt", [33, TB])
    mrow = tl("mr", [64, 2, TB])
    trow = tl("tr", [128, TB])
    uf = tl("uf", [64, TB], F32, 3)
    ui = tl("ui", [64, TB], I32, 3)
    aa = tl("aa", [64, TB], F32, 3)
    afw = tl("afw", [64, TB])
    abw = tl("abw", [64, TB])
    win = tl("win", [128, TB])
    tp = tl("tp", [128, TB])
    SC2PI = float(2.0 * np.pi * (1.0 - 1e-6))
    it = 0
    for nb in range(NF // TB):
        s = nb % 2
        nsl = slice(nb * TB, (nb + 1) * TB)
        P.dma(zt[s][:], k.k_zT[:, nsl], "yc_z%d" % s)
        P.dma(mrow[s][:, 0, :], bc_rows(k.k_fmask[0:1, nsl], 64), "yc_m%d" % s)
        P.dma(mrow[s][:, 1, :], bc_rows(k.k_fmask[1:2, nsl], 64), "yc_m%d" % s)
        P.dma(trow[s][:], bc_rows(k.k_trow[0:1, nsl], 128), "yc_t%d" % s)
        a_in = zt[s][:]
        for ly in range(3):
            pb = k.ps[0:64, 4 + (ly % 2), :]
            lw = w1[:] if ly == 0 else w2[:, ly - 1, :]
            P.mm(pb, lw, a_in)
            P.ts(uf[ly][:], pb, fq[:], fbs[:, ly:ly + 1], ALU.mult, ALU.add)
            P.copy(ui[ly][:], uf[ly][:])
            P.tt(uf[ly][:], uf[ly][:], ui[ly][:], ALU.subtract)
            P.act(aa[ly][:], uf[ly][:], AF.Sin, scale=SC2PI)
            a_in = aa[ly][:]
        P.tt(afw[s][:], aa[2][:], mrow[s][:, 0, :], ALU.mult, eng="pool")
        P.tt(abw[s][:], aa[2][:], mrow[s][:, 1, :], ALU.mult, eng="pool")
        for j in range(16):
            q = it % 2
            pb = k.ps[:, it % 4, :]
            P.mm(pb, wout[:, j * 128:(j + 1) * 128], afw[s][:], start=True, stop=False)
            P.mm(pb, wout[:, E + j * 128:E + (j + 1) * 128], abw[s][:], start=False, stop=True)
            P.act(win[q][:], trow[s][:], AF.Exp, scale=ndel[:, j:j + 1])
            P.tt(tp[q][:], pb, win[q][:], ALU.mult)
            P.dma(k.TAPS[j * 128:(j + 1) * 128, nsl], tp[q][:], "yc_s%d" % q)
            it += 1
    P.barrier()
    P.reset()
    k.fa = P.sbuf("yd_fa", [NA, 2 * NA], F32)
    k.fb = P.sbuf("yd_fb", [NB, 3 * NB], F32)
    k.fc = P.sbuf("yd_fc", [NB, 4 * NB], F32)
    k.tw1 = P.sbuf("yd_tw1", [NB, 2 * NA], F32)
    k.tw2 = P.sbuf("yd_tw2", [NA, 2 * NB], F32)
    P.dma(k.fa[:], k.k_fa, "yd_c")
    P.dma(k.fb[:], k.k_fb, "yd_c")
    P.dma(k.fc[:], k.k_fc, "yd_c")
    P.dma(k.tw1[:], k.k_tw1, "yd_c")
    P.dma(k.tw2[:], k.k_tw2, "yd_c")
    wm = P.bump
    hsb = [P.sbuf("yd_h%d" % i, [NB, 2, CG, NA], F32) for i in range(2)]
    cnt = {"i": 0}

    def sink1(lg, gi, c0, psXr, psXi):
        s = cnt["i"] % 2
        P.copy(hsb[s][:, 0, :, :], psXr.rearrange("p (c k) -> p c k", c=CG), eng="act")
        P.copy(hsb[s][:, 1, :, :], psXi.rearrange("p (c k) -> p c k", c=CG), eng="act")
        for ri in range(2):
            P.dma(k.HS[ri, :, c0:c0 + CG, :], hsb[s][:, ri, :, :], "yd_sh%d" % s)
        cnt["i"] += 1

    hy_fft(k, k.TAPS, NA, "yd1_", sink1)
    P.barrier()
    P.bump = wm
    Hh = [P.sbuf("yd_H%d" % i, [NB, 2, CG, NA], F32) for i in range(2)]
    Yt = [P.sbuf("yd_Y%d" % i, [NB, 2, CG, NA], F32) for i in range(2)]
    Dt = [P.sbuf("yd_D%d" % i, [NA, 2, CG, NB], F32) for i in range(2)]
    m2 = [P.sbuf("yd_mm%d" % i, [128, CG * 128], F32) for i in range(4)]
    yout = [P.sbuf("yd_yo%d" % i, [NA // 2, LG, NB], F32) for i in range(2)]
    psD = k.ps[0:NA, 4:6, :].rearrange("p a b -> p (a b)")
    psY = k.psb[0:NA // 2, 0, :].bitcast(F32)[:, 0:CG * NB]
    t2r = k.tw2[:, 0:NB].rearrange("p (o n) -> p o n", o=1).to_broadcast([NA, CG, NB])
    t2i = k.tw2[:, NB:2 * NB].rearrange("p (o n) -> p o n", o=1).to_broadcast([NA, CG, NB])
    cnt["i"] = 0

    def sink2(lg, gi, c0, psXr, psXi):
        s = cnt["i"] % 2
        for ri in range(2):
            P.dma(Hh[s][:, ri, :, :], k.HS[ri, :, c0:c0 + CG, :], "yd_lh%d" % s)
        Xr = psXr.rearrange("p (c k) -> p c k", c=CG)
        Xi = psXi.rearrange("p (c k) -> p c k", c=CG)
        mv = [m2[i][0:NB, 0:CG * NA].rearrange("p (c k) -> p c k", c=CG) for i in range(4)]
        P.tt(mv[0], Xr, Hh[s][:, 0, :, :], ALU.mult)
        P.tt(mv[1], Xi, Hh[s][:, 1, :, :], ALU.mult)
        P.tt(Yt[s][:, 0, :, :], mv[0], mv[1], ALU.subtract, eng="pool")
        P.tt(mv[2], Xr, Hh[s][:, 1, :, :], ALU.mult)
        P.tt(mv[3], Xi, Hh[s][:, 0, :, :], ALU.mult)
        P.tt(Yt[s][:, 1, :, :], mv[2], mv[3], ALU.add, eng="pool")
        for cc in range(CG):
            po = psD[:, cc * 2 * NB:(cc + 1) * 2 * NB]
            P.mm(po, Yt[s][:, 0, cc, :], k.fc[:, 0:2 * NB], start=True, stop=False)
            P.mm(po, Yt[s][:, 1, cc, :], k.fc[:, 2 * NB:4 * NB], start=False, stop=True)
        D4 = psD[:, 0:CG * 2 * NB].rearrange("p (c r n) -> p c r n", c=CG, r=2)
        Dr, Di = D4[:, :, 0, :], D4[:, :, 1, :]
        nv = [m2[i][0:NA, 0:CG * NB].rearrange("p (c n) -> p c n", c=CG) for i in range(4)]
        P.tt(nv[0], Dr, t2r, ALU.mult)
        P.tt(nv[1], Di, t2i, ALU.mult)
        P.tt(Dt[s][:, 0, :, :], nv[0], nv[1], ALU.subtract, eng="pool")
        P.tt(nv[2], Dr, t2i, ALU.mult)
        P.tt(nv[3], Di, t2r, ALU.mult)
        P.tt(Dt[s][:, 1, :, :], nv[2], nv[3], ALU.add, eng="pool")
        P.mm(psY, k.fa[:, 0:NA // 2], Dt[s][:, 0, :, :].rearrange("p c n -> p (c n)"), start=True, stop=False)
        P.mm(psY, k.fa[:, NA:NA + NA // 2], Dt[s][:, 1, :, :].rearrange("p c n -> p (c n)"), start=False, stop=True)
        yo = yout[lg % 2]
        P.act(yo[:, gi * CG:(gi + 1) * CG, :], psY.rearrange("p (c n) -> p c n", c=CG), AF.Copy, scale=float(1.0 / NF))
        if gi == LG // CG - 1:
            P.dma(YT[lg * LG:(lg + 1) * LG, :].rearrange("c (n1 n2) -> n1 c n2", n2=NB), yo[:], "yd_sy%d" % (lg % 2))
        cnt["i"] += 1

    hy_fft(k, UT, NA // 2, "yd2_", sink2)
    P.barrier()
    P.reset()
    skp = P.sbuf("ye_sk", [128, 16], F32)
    P.dma(skp[:], k.hy_skip.rearrange("o (j p) -> p (o j)", p=128), "ye_c", allow_slow_non_contiguous=True)
    tl = lambda nm, sh, dt=F32, n=2: [P.sbuf("ye_%s%d" % (nm, i), sh, dt) for i in range(n)]
    yy, u2, x1t, zz, tt_ = (tl(n, [128, BL]) for n in ("yy", "u2", "x1", "zz", "tt"))
    gt = tl("gt", [128, BL], BF16)
    it = 0
    for j in range(16):
        rows = slice(j * 128, (j + 1) * 128)
        for bi in range(NBK):
            s = it % 2
            tsl = slice(bi * BL, (bi + 1) * BL)
            P.dma(yy[s][:], YT[rows, tsl], "ye_y%d" % s)
            P.dma(u2[s][:], UT[rows, tsl], "ye_u%d" % s)
            P.dma(x1t[s][:], X1[rows, tsl], "ye_x%d" % s)
            P.dma(zz[s][:], k.ZT[rows, tsl], "ye_z%d" % s)
            P.stt(tt_[s][:], u2[s][:], skp[:, j:j + 1], yy[s][:], ALU.mult, ALU.add)
            P.tt(tt_[s][:], tt_[s][:], x1t[s][:], ALU.mult, eng="pool")
            P.tt(gt[s][:], tt_[s][:], zz[s][:], ALU.mult, eng="pool")
            P.dma(k.QK[0, 0, rows, tsl], gt[s][:], "ye_g%d" % s)
            it += 1
    P.barrier()
    P.reset()
    st = outproj_setup(k, l, k.hy_w_out, "yf_")
    GT = [P.sbuf("yf_GT%d" % i, [128, 16, TB], BF16) for i in range(2)]
    for tb in range(NTB):
        P.dma(GT[tb % 2][:], k.QK[0, 0, :, tb * TB:(tb + 1) * TB].rearrange("(e p) t -> p e t", p=128), "yf_g%d" % (tb % 2))
        outproj_block(k, l, tb, GT[tb % 2], st, xsrc, "yf_")


_CACHE = {}


def kernel(**inputs):
    cfg = Cfg(NS=4, SL=2048, kinds=(0, 1, 2, 3))
    if "nc" not in _CACHE:
        _CACHE["nc"] = build(cfg)[0]
    nc = _CACHE["nc"]
    w = {n: np.asarray(v) for n, v in inputs.items() if n not in ("x_prompt", "x_sample", "c_prompt", "c_sample")}
    xp = np.asarray(inputs["x_prompt"], np.float32)
    xs = np.asarray(inputs["x_sample"], np.float32)
    cp = np.asarray(inputs["c_prompt"], np.float32)
    cs = np.asarray(inputs["c_sample"], np.float32)
    SL, T = cfg.SL, cfg.T
    in_maps = []
    for i in range(4):
        in_maps.append(core_inputs(cfg, [xp[i, s * SL:(s + 1) * SL] for s in range(4)], [cp[i]] * 4, 1.0, w,
                                   np.arange(T), T))
    for j in range(4):
        a, b = 2 * j, 2 * j + 1
        in_maps.append(core_inputs(cfg, [xs[a], None, xs[b], None], [cs[a], None, cs[b], None], 0.0, w,
                                   np.arange(T) % SL, SL))
    res = run_bass_kernel_spmd(nc, in_maps, core_ids=list(range(8)))
    yp = np.stack([np.asarray(res.results[i]["yout"], np.float32) for i in range(4)], 0)
    ys = np.zeros(xs.shape, np.float32)
    for j in range(4):
        yo = np.asarray(res.results[4 + j]["yout"], np.float32)
        ys[2 * j] = yo[0:SL]
        ys[2 * j + 1] = yo[2 * SL:3 * SL]
    return (yp, ys)
```

```python
import numpy as np
from contextlib import ExitStack
import concourse.bass as bass
import concourse.mybir as mybir
from concourse.bass_utils import run_bass_kernel_spmd

F32 = mybir.dt.float32
BF16 = mybir.dt.bfloat16
I32 = mybir.dt.int32
ALU = mybir.AluOpType
AF = mybir.ActivationFunctionType

D = 1024
E = 2048
CH = 64
TB = 512
EPS = 1e-6

def _isz(dt):
    return mybir.dt.size(dt)


def _region(ap):
    name = ap.tensor.name
    pat = ap.ap
    off = int(ap.offset)
    space = str(ap.space)
    z = _isz(ap.dtype)
    if space in ("SB", "PSUM"):
        pstride = pat[0][0]
        p_lo = off // pstride
        p_hi = p_lo + pat[0][1]
        base = off % pstride
        lo = base
        hi = base
        for st, cn in pat[1:]:
            ext = st * (cn - 1)
            if ext < 0:
                lo += ext
            else:
                hi += ext
        return (name, p_lo, p_hi, lo * z, (hi + 1) * z)
    pitch = int(ap.tensor.shape[-1])
    r_lo = r_hi = off // pitch
    c_lo = c_hi = off % pitch
    for st, cn in pat:
        ext = st * (cn - 1)
        if abs(st) >= pitch and st % pitch == 0:
            e = ext // pitch
            if e < 0:
                r_lo += e
            else:
                r_hi += e
        else:
            if ext < 0:
                c_lo += ext
            else:
                c_hi += ext
    if c_lo < 0 or c_hi >= pitch:
        r_lo += c_lo // pitch
        r_hi += c_hi // pitch
        c_lo, c_hi = 0, pitch - 1
    return (name, r_lo, r_hi + 1, c_lo, c_hi + 1)


def _overlap(a, b):
    return a[1] < b[2] and b[1] < a[2] and a[3] < b[4] and b[3] < a[4]


def _contains(a, b):
    return a[1] <= b[1] and b[2] <= a[2] and a[3] <= b[3] and b[4] <= a[4]


class Op:
    __slots__ = ("eng", "fn", "deps", "stream", "signal", "val", "sem", "dma_snap", "phase")

    def __init__(self, eng, fn, deps, stream):
        self.eng = eng
        self.fn = fn
        self.deps = deps
        self.stream = stream
        self.signal = False
        self.val = 0
        self.sem = None
        self.dma_snap = None


class Prog:
    ENGS = ("pe", "dve", "act", "pool", "sp")

    def __init__(self, nc):
        self.nc = nc
        self.ops = []
        self.acc = {}
        self.stack = ExitStack()
        self.n = 0
        self.barrier_at = []

    ARENA = 150 * 1024
    RSV = 40 * 1024

    def sbuf_r(self, name, shape, dt):
        t = self.stack.enter_context(self.nc.sbuf_tensor(name, list(shape), dt))
        return t[tuple(slice(None) for _ in shape)]

    def sbuf(self, name, shape, dt):
        if not hasattr(self, "arena"):
            self.arena = self.stack.enter_context(self.nc.sbuf_tensor("arena", [128, self.ARENA], mybir.dt.uint8))
            self.bump = 0
            self.water = 0
            self.rbump = self.ARENA
        n = 1
        for d in shape[1:]:
            n *= d
        size = (n * _isz(dt) + 63) // 64 * 64
        off = self.bump
        self.bump += size
        assert self.bump <= self.ARENA, ("SBUF overflow", name, self.bump)
        v = self.arena[0:shape[0], off:off + n * _isz(dt)].bitcast(dt)
        if len(shape) == 3:
            v = v.rearrange("p (a b) -> p a b", a=shape[1])
        elif len(shape) == 4:
            v = v.rearrange("p (a b c) -> p a b c", a=shape[1], b=shape[2])
        return v

    def mark(self):
        self.water = self.bump

    def reset(self, label=None):
        self.bump = self.water
        self.rbump = self.ARENA
        self.nphase = getattr(self, "nphase", 0) + 1
        self.cur_phase = "%02d_%s" % (self.nphase, label or "")

    def psum(self, name, shape, dt=F32):
        return self.stack.enter_context(self.nc.psum_tensor(name, list(shape), dt))

    def dram(self, name, shape, dt, kind="Internal"):
        return self.nc.dram_tensor(name, list(shape), dt, kind=kind).ap()

    def _deps(self, reg, is_write, idx, eng, is_dma):
        lst = self.acc.setdefault(reg[0], [])
        deps = []
        keep = []
        for (r, j, w, e, d) in lst:
            if _overlap(r, reg):
                if is_write or w:
                    same = (e == eng) and not d and not is_dma
                    if same:
                        if w and not is_write and eng != "pe":
                            deps.append(j)
                    else:
                        deps.append(j)
                if is_write and _contains(reg, r):
                    continue
                if (not is_write) and (not w) and e == eng and not d and not is_dma and r == reg:
                    continue
            keep.append((r, j, w, e, d))
        keep.append((reg, idx, is_write, eng, is_dma))
        self.acc[reg[0]] = keep
        return deps

    def add(self, eng, fn, reads=(), writes=(), stream=None, extra_deps=()):
        idx = len(self.ops)
        is_dma = stream is not None
        deps = set(extra_deps)
        for ap in reads:
            if ap is None:
                continue
            deps.update(self._deps(_region(ap), False, idx, eng, is_dma))
        for ap in writes:
            if ap is None:
                continue
            deps.update(self._deps(_region(ap), True, idx, eng, is_dma))
        deps.discard(idx)
        self.ops.append(Op(eng, fn, deps, stream))
        self.ops[-1].phase = getattr(self, "cur_phase", "00")
        return idx

    def barrier(self):
        self.barrier_at.append(len(self.ops))
        self.acc = {}

    def dma(self, out, in_, stream, q="sp", **kw):
        return self.add(q, lambda e: e.dma_start(out=out, in_=in_, **kw), [in_], [out], stream=stream)

    def mm(self, out, lhsT, rhs, start=True, stop=True, extra_reads=()):
        return self.add("pe", lambda e: e.matmul(out, lhsT, rhs, start=start, stop=stop),
                        [lhsT, rhs] + list(extra_reads) + ([] if start else [out]), [out])

    def transpose(self, out, in_, ident):
        return self.add("pe", lambda e: e.transpose(out, in_, ident), [in_, ident], [out])

    def act(self, out, in_, func, bias=0.0, scale=1.0, accum_out=None, eng="act"):
        rd = [in_]
        if not isinstance(bias, (int, float)):
            rd.append(bias)
        if not isinstance(scale, (int, float)):
            rd.append(scale)
        wr = [out]
        kw = {}
        if accum_out is not None:
            wr.append(accum_out)
            kw["accum_out"] = accum_out
        return self.add(eng, lambda e: e.activation(out, in_, func, bias=bias, scale=scale, **kw), rd, wr)

    def tt(self, out, a, b, op, eng="dve"):
        return self.add(eng, lambda e: e.tensor_tensor(out, a, b, op), [a, b], [out])

    def ts(self, out, a, s1, s2, op0, op1=None, eng="dve", accum_out=None):
        rd = [a]
        if not isinstance(s1, (int, float)):
            rd.append(s1)
        if s2 is not None and not isinstance(s2, (int, float)):
            rd.append(s2)
        wr = [out]
        kw = {}
        if accum_out is not None:
            wr.append(accum_out)
            kw["accum_out"] = accum_out
        if op1 is None:
            return self.add(eng, lambda e: e.tensor_scalar(out, a, s1, None, op0, **kw), rd, wr)
        return self.add(eng, lambda e: e.tensor_scalar(out, a, s1, s2, op0, op1, **kw), rd, wr)

    def stt(self, out, a, s, b, op0, op1, eng="dve"):
        rd = [a, b]
        if not isinstance(s, (int, float)):
            rd.append(s)
        return self.add("dve", lambda e: e.scalar_tensor_tensor(out, a, s, b, op0, op1), rd, [out])

    def copy(self, out, in_, eng="dve"):
        if eng == "act":
            return self.add(eng, lambda e: e.copy(out, in_), [in_], [out])
        return self.add(eng, lambda e: e.tensor_copy(out, in_), [in_], [out])

    def memset(self, out, v, eng="pool"):
        return self.add(eng, lambda e: e.memset(out, v), [], [out])

    def scan(self, out, d0, d1, init, op0=ALU.mult, op1=ALU.add):
        rd = [d0, d1]
        if not isinstance(init, (int, float)):
            rd.append(init)
        return self.add("dve", lambda e: e.tensor_tensor_scan(out, d0, d1, init, op0, op1), rd, [out])

    def finalize(self, final_streams_wait=True):
        nc = self.nc
        engs = {"pe": nc.tensor, "dve": nc.vector, "act": nc.scalar, "pool": nc.gpsimd, "sp": nc.sync}
        ops = self.ops
        for op in ops:
            for j in op.deps:
                if ops[j].stream is None:
                    ops[j].signal = True
        last_before = []
        for b in self.barrier_at:
            lb = {}
            for i in range(b - 1, -1, -1):
                o = ops[i]
                if o.stream is None and o.eng not in lb:
                    lb[o.eng] = i
                    if len(lb) == 5:
                        break
            for i in lb.values():
                ops[i].signal = True
            last_before.append(lb)
        sems = {}

        def getsem(key):
            if key not in sems:
                sems[key] = self.stack.enter_context(nc.semaphore("s_" + key))
            return sems[key]

        cnt = {}
        stream_hist = {}
        active = {}
        freep = []
        nphys = 0
        bset = set(self.barrier_at)
        for i, op in enumerate(ops):
            if i in bset:
                freep.extend(sorted(active.values()))
                active = {}
            if op.stream is not None:
                if op.stream not in active:
                    if freep:
                        active[op.stream] = freep.pop(0)
                    else:
                        active[op.stream] = nphys
                        nphys += 1
                k = "d%d" % active[op.stream]
                cnt[k] = cnt.get(k, 0) + 16
                op.sem = k
                op.val = cnt[k]
            elif op.signal:
                k = "e_" + op.eng
                cnt[k] = cnt.get(k, 0) + 1
                op.sem = k
                op.val = cnt[k]
        waited = {e: {} for e in self.ENGS}
        self.iname = {}
        stream_cnt = {}
        bi = 0
        barrier_pending = {e: None for e in self.ENGS}
        for i, op in enumerate(ops):
            while bi < len(self.barrier_at) and self.barrier_at[bi] <= i:
                snap = {}
                for e, j in last_before[bi].items():
                    snap[ops[j].sem] = ops[j].val
                for k, v in stream_cnt.items():
                    snap[k] = v
                for e in self.ENGS:
                    barrier_pending[e] = dict(snap) if barrier_pending[e] is None else {**barrier_pending[e], **snap}
                bi += 1
            eng = engs[op.eng]
            need = {}
            if barrier_pending[op.eng] is not None:
                need.update(barrier_pending[op.eng])
                barrier_pending[op.eng] = None
            for j in op.deps:
                d = ops[j]
                if d.stream is not None:
                    v = stream_cnt[d.sem]
                else:
                    v = d.val
                if need.get(d.sem, 0) < v:
                    need[d.sem] = v
            w = waited[op.eng]
            for k, v in need.items():
                if k == "e_" + op.eng and op.eng == "pe":
                    continue
                if w.get(k, 0) < v:
                    eng.wait_ge(getsem(k), v)
                    w[k] = v
            ins = op.fn(eng)
            try:
                self.iname[ins.ins.name] = op.phase
            except Exception:
                pass
            if op.stream is not None:
                ins.then_inc(getsem(op.sem), 16)
                stream_cnt[op.sem] = op.val
            elif op.signal:
                ins.then_inc(getsem(op.sem), 1)
        if final_streams_wait:
            for k, v in stream_cnt.items():
                if waited["sp"].get(k, 0) < v:
                    nc.sync.wait_ge(getsem(k), v)
        self.nsems = len(sems)
        self.counts = cnt


class Lanes:
    def __init__(self, width, stagger=0):
        self.width = width
        self.stagger = stagger
        self.active = []

    def step(self):
        for g in list(self.active):
            try:
                next(g)
            except StopIteration:
                self.active.remove(g)

    def push(self, gen):
        if gen is None:
            return
        while len(self.active) >= self.width:
            self.step()
        self.active.append(gen)
        for _ in range(self.stagger):
            self.step()

    def drain(self):
        while self.active:
            self.step()


class Cfg:
    def __init__(self, NS=4, SL=2048, kinds=(0, 1, 2, 3)):
        self.NS = NS
        self.SL = SL
        self.T = NS * SL
        self.NCH = self.T // CH
        self.NTB = self.T // TB
        self.kinds = tuple(kinds)
        self.depth = len(kinds)
        self.NB = 128
        self.NA = (2 * self.T) // 128


class K:
    pass


def bc_rows(ap_row, n):
    return ap_row.to_broadcast([n, ap_row.shape[-1]])


def build(cfg):
    nc = bass.Bass("TRN2", target_bir_lowering=False)
    P = Prog(nc)
    k = K()
    k.P = P
    k.cfg = cfg
    T, NS, SL, NCH, NTB = cfg.T, cfg.NS, cfg.SL, cfg.NCH, cfg.NTB
    dp = cfg.depth
    din = lambda n, s, dt=F32: P.dram(n, s, dt, "ExternalInput")
    k.xin = din("xin", [T, D])
    k.cin = din("cin", [NS, D])
    k.ada_w = din("ada_w", [dp, D, 3 * D])
    k.ada_b = din("ada_b", [dp, 3 * D])
    k.norm_g = din("norm_g", [dp, D])
    k.final_g = din("final_g", [1, D])
    k.hg_lb = din("hg_lb", [5, E])
    k.hg_w_in = din("hg_w_in", [D, 5 * E])
    k.hg_norm_g = din("hg_norm_g", [1, 128])
    k.hg_w_out = din("hg_w_out", [E, D])
    if 2 in cfg.kinds:
        k.rt_w_in = din("rt_w_in", [D, 6144])
        k.rt_gn_g = din("rt_gn_g", [1, E])
        k.rt_w_out = din("rt_w_out", [E, D])
        k.k_cos = din("k_cos", [128, T])
        k.k_sin = din("k_sin", [128, T])
        k.k_dtab = din("k_dtab", [128, 16 * 64])
        k.k_rtsc = din("k_rtsc", [2, 3, 1024, NCH])
    if 3 in cfg.kinds:
        k.lru_w_in = din("lru_w_in", [D, 2 * E])
        k.lru_conv_w = din("lru_conv_w", [4, E])
        k.lru_conv_b = din("lru_conv_b", [1, E])
        k.lru_gate_w = din("lru_gate_w", [2, 2, 16, 128, 128])
        k.lru_gate_b = din("lru_gate_b", [2, 2, E])
        k.lru_lambda = din("lru_lambda", [2, E])
        k.lru_w_out = din("lru_w_out", [E, D])
    if 1 in cfg.kinds:
        NA, NB = cfg.NA, cfg.NB
        NF = 2 * T
        k.hy_w_in = din("hy_w_in", [D, 4 * E])
        k.hy_b_in = din("hy_b_in", [1, 4 * E])
        k.hy_conv_w = din("hy_conv_w", [3, 3 * E])
        k.hy_conv_b = din("hy_conv_b", [1, 3 * E])
        k.hy_f_w1 = din("hy_f_w1", [33, 64])
        k.hy_f_b1 = din("hy_f_b1", [1, 64])
        k.hy_f_w2 = din("hy_f_w2", [2, 64, 64])
        k.hy_f_b2 = din("hy_f_b2", [2, 64])
        k.hy_f_wout = din("hy_f_wout", [64, 2 * E])
        k.hy_f_freq = din("hy_f_freq", [1, 64])
        k.hy_skip = din("hy_skip", [1, E])
        k.hy_w_out = din("hy_w_out", [E, D])
        k.k_fa = din("k_fa", [NA, 2 * NA])
        k.k_fb = din("k_fb", [NB, 3 * NB])
        k.k_fc = din("k_fc", [NB, 4 * NB])
        k.k_tw1 = din("k_tw1", [NB, 2 * NA])
        k.k_tw2 = din("k_tw2", [NA, 2 * NB])
        k.k_zT = din("k_zT", [33, NF])
        k.k_fmask = din("k_fmask", [2, NF])
        k.k_trow = din("k_trow", [1, NF])
        k.k_delta = din("k_delta", [128, 16])
        k.k_smask = din("k_smask", [128, NS])
        k.PR = P.dram("PR", [3 * E, T], F32)
        k.TAPS = P.dram("TAPS", [E, NF], F32)
        k.HS = P.dram("HS", [2, NB, E, NA], F32)
    k.k_carry1 = din("k_carry1", [128, 1])
    k.k_ident = din("k_ident", [128, 128])
    k.k_tri = din("k_tri", [64, 128])
    k.k_rmask = din("k_rmask", [128, TB])
    k.k_carry = din("k_carry", [128, 2 * NCH])
    k.yout = P.dram("yout", [T, D], F32, "ExternalOutput")
    k.X = P.dram("X", [T, D], F32)
    k.MOD = P.dram("MOD", [dp, NS, 3 * D], F32)
    k.HT = P.dram("HT", [D, T], BF16)
    k.QT = P.dram("QT", [E, T], F32)
    k.ZT = P.dram("ZT", [E, T], F32)
    k.QK = P.dram("QK", [2, 2, E, T], BF16)
    k.SC = P.dram("SC", [2, 3, E, NCH], F32)
    k.V = P.dram("V", [T, E], BF16)
    k.OT = P.dram("OT", [2, E, T], F32)
    k.ident = P.sbuf("ident", [128, 128], F32)
    k.identb = P.sbuf("identb", [128, 128], BF16)
    k.ones = P.sbuf("ones", [128, 128], F32)
    k.tri = P.sbuf("tri", [64, 128], F32)
    k.rmask = P.sbuf("rmask", [128, TB], F32)
    k.carry = P.sbuf("carry", [128, 2 * NCH], F32)
    k.epsT = P.sbuf("epsT", [128, 1], F32)
    P.dma(k.ident[:], k.k_ident, "c0")
    P.dma(k.tri[:], k.k_tri, "c1")
    P.dma(k.rmask[:], k.k_rmask, "c2")
    P.dma(k.carry[:], k.k_carry, "c3")
    k.carry1 = P.sbuf("carry1", [128, 1], F32)
    P.dma(k.carry1[:], k.k_carry1, "c4")
    P.copy(k.identb[:], k.ident[:])
    P.memset(k.ones[:], 1.0)
    P.memset(k.epsT[:], EPS)
    k.ps = P.psum("ps", [128, 6, 512], F32)
    k.psb = P.psum("psb", [128, 2, 1024], BF16)

    P.mark()
    phase_mod(k)
    P.barrier()
    for l, kind in enumerate(cfg.kinds):
        xsrc = k.xin if l == 0 else k.X
        phase_norm(k, l, xsrc)
        P.barrier()
        if kind == 0:
            hgrn2(k, l, xsrc)
        elif kind == 1:
            hyena(k, l, xsrc)
        elif kind == 2:
            retention(k, l, xsrc)
        elif kind == 3:
            rglru(k, l, xsrc)
        else:
            raise NotImplementedError
        P.barrier()
    phase_final(k, k.xin if dp == 0 else k.X)
    P.finalize()
    return nc, P


def phase_mod(k):
    P, cfg = k.P, k.cfg
    P.reset("phase_mod_")
    NS = cfg.NS
    cT = P.sbuf("m_cT", [128, 8, NS], F32)
    sg = P.sbuf("m_sg", [128, 8, NS], F32)
    csT = P.sbuf("m_csT", [128, 8, NS], F32)
    for kt in range(8):
        P.dma(cT[:, kt, :], k.cin[:, kt * 128:(kt + 1) * 128].rearrange("s p -> p s"), "m_c",
              allow_slow_non_contiguous=True)
    P.act(sg[:], cT[:], AF.Sigmoid)
    P.tt(csT[:], cT[:], sg[:], ALU.mult)
    w = [P.sbuf("m_w%d" % i, [128, 8, 512], F32) for i in range(2)]
    bb = [P.sbuf("m_b%d" % i, [NS, 512], F32) for i in range(2)]
    ob = [P.sbuf("m_o%d" % i, [NS, 512], F32) for i in range(2)]
    it = 0
    for l in range(cfg.depth):
        for cb in range(6):
            s = it % 2
            P.dma(w[s][:], k.ada_w[l, :, cb * 512:(cb + 1) * 512].rearrange("(kt p) n -> p kt n", p=128), "m_w%d" % s)
            P.dma(bb[s][:], bc_rows(k.ada_b[l:l + 1, cb * 512:(cb + 1) * 512], NS), "m_b%d" % s)
            pb = k.ps[0:NS, it % 2, :]
            for kt in range(8):
                P.mm(pb, csT[:, kt, :], w[s][:, kt, :], start=(kt == 0), stop=(kt == 7))
            P.tt(ob[s][:], pb, bb[s][:], ALU.add)
            P.dma(k.MOD[l, :, cb * 512:(cb + 1) * 512], ob[s][:], "m_o%d" % s)
            it += 1


def rms_rstd(P, k, xt, junk, ss, rstd):
    P.act(junk, xt, AF.Square, accum_out=ss)
    P.act(rstd, ss, AF.Sqrt, bias=k.epsT[:], scale=1.0 / D)
    P.add("dve", lambda e: e.reciprocal(rstd, rstd), [rstd], [rstd])


def phase_norm(k, l, xsrc):
    P, cfg = k.P, k.cfg
    P.reset("phase_norm_")
    T, NS, SL = cfg.T, cfg.NS, cfg.SL
    g_bc = P.sbuf("n_g", [128, D], F32)
    A_bc = P.sbuf("n_A", [128, D], F32)
    sh_bc = P.sbuf("n_sh", [128, D], F32)
    xt = [P.sbuf("n_x%d" % i, [128, D], F32) for i in range(2)]
    junk = P.sbuf("n_junk", [128, D], F32)
    hb = [P.sbuf("n_h%d" % i, [128, D], BF16) for i in range(2)]
    hT = [P.sbuf("n_hT%d" % i, [128, 8, TB], BF16) for i in range(2)]
    ss = [P.sbuf("n_ss%d" % i, [128, 1], F32) for i in range(2)]
    rstd = [P.sbuf("n_rs%d" % i, [128, 1], F32) for i in range(2)]
    P.dma(g_bc[:], bc_rows(k.norm_g[l:l + 1, :], 128), "n_g")
    ntile = T // 128
    P.dma(xt[0][:], xsrc[0:128, :], "n_x0")
    for i in range(ntile):
        s = i % 2
        slot = (i * 128) // SL
        if (i * 128) % SL == 0:
            P.dma(A_bc[:], bc_rows(k.MOD[l, slot:slot + 1, D:2 * D], 128), "n_A")
            P.dma(sh_bc[:], bc_rows(k.MOD[l, slot:slot + 1, 0:D], 128), "n_sh")
            P.stt(A_bc[:], A_bc[:], 1.0, g_bc[:], ALU.add, ALU.mult)
        if i + 1 < ntile:
            P.dma(xt[1 - s][:], xsrc[(i + 1) * 128:(i + 2) * 128, :], "n_x%d" % (1 - s))
        rms_rstd(P, k, xt[s][:], junk[:], ss[s][:], rstd[s][:])
        P.stt(junk[:], xt[s][:], rstd[s][:], A_bc[:], ALU.mult, ALU.mult)
        P.tt(hb[s][:], junk[:], sh_bc[:], ALU.add, eng="pool")
        tb, sub = divmod(i, 4)
        hs = tb % 2
        for kt in range(8):
            pst = k.psb[:, kt % 2, (kt // 2) * 128:(kt // 2) * 128 + 128] if False else k.psb[:, kt // 4, (kt % 4) * 128:(kt % 4) * 128 + 128]
            P.transpose(pst, hb[s][:, kt * 128:(kt + 1) * 128], k.identb[:])
        for half in range(2):
            src = k.psb[:, half, 0:512].rearrange("p (a b) -> p a b", a=4)
            dst = hT[hs][:, half * 4:half * 4 + 4, sub * 128:(sub + 1) * 128]
            P.copy(dst, src, eng=("act" if half == 0 else "dve"))
        if sub == 3:
            P.dma(k.HT[:, tb * TB:(tb + 1) * TB].rearrange("(kt p) t -> p kt t", p=128), hT[hs][:], "n_hT%d" % hs)


def phase_final(k, xsrc):
    P, cfg = k.P, k.cfg
    P.reset("phase_final_")
    T = cfg.T
    g_bc = P.sbuf("f_g", [128, D], F32)
    xt = [P.sbuf("f_x%d" % i, [128, D], F32) for i in range(2)]
    yt = [P.sbuf("f_y%d" % i, [128, D], F32) for i in range(2)]
    junk = P.sbuf("f_junk", [128, D], F32)
    ss = [P.sbuf("f_ss%d" % i, [128, 1], F32) for i in range(2)]
    rstd = [P.sbuf("f_rs%d" % i, [128, 1], F32) for i in range(2)]
    P.dma(g_bc[:], bc_rows(k.final_g[0:1, :], 128), "f_g")
    ntile = T // 128
    P.dma(xt[0][:], xsrc[0:128, :], "f_x0")
    for i in range(ntile):
        s = i % 2
        if i + 1 < ntile:
            P.dma(xt[1 - s][:], xsrc[(i + 1) * 128:(i + 2) * 128, :], "f_x%d" % (1 - s))
        rms_rstd(P, k, xt[s][:], junk[:], ss[s][:], rstd[s][:])
        P.stt(yt[s][:], xt[s][:], rstd[s][:], g_bc[:], ALU.mult, ALU.mult)
        P.dma(k.yout[i * 128:(i + 1) * 128, :], yt[s][:], "f_y%d" % s)


def gemm_fm(k, W, groups, epi, pfx):
    P, cfg = k.P, k.cfg
    NTB = cfg.NTB
    wst = [P.sbuf(pfx + "wst%d" % i, [128, 8, 512], F32) for i in range(2)]
    wbf = [P.sbuf(pfx + "wbf%d" % i, [128, 8, 512], BF16) for i in range(2)]
    hT = [P.sbuf(pfx + "hT%d" % i, [128, 8, TB], BF16) for i in range(2)]

    def loadw(gi):
        s = gi % 2
        for j, (col, tag) in enumerate(groups[gi]):
            P.dma(wst[s][:, :, j * 128:(j + 1) * 128], W[:, col:col + 128].rearrange("(kt p) n -> p kt n", p=128),
                  pfx + "w%d" % s)
        ncol = 128 * len(groups[gi])
        P.copy(wbf[s][:, :, 0:ncol], wst[s][:, :, 0:ncol], eng="pool")

    items = [(gi, tb) for gi in range(len(groups)) for tb in range(NTB)]

    def loadh(ii):
        gi, tb = items[ii]
        P.dma(hT[ii % 2][:], k.HT[:, tb * TB:(tb + 1) * TB].rearrange("(kt p) t -> p kt t", p=128), pfx + "h%d" % (ii % 2))

    loadw(0)
    loadh(0)
    it = 0
    lanes = Lanes(3)
    for ii, (gi, tb) in enumerate(items):
        if ii + 1 < len(items):
            loadh(ii + 1)
        if tb == 0 and gi + 1 < len(groups):
            loadw(gi + 1)
        for j, (col, tag) in enumerate(groups[gi]):
            pb = k.ps[:, it % 4, :]
            for kt in range(8):
                P.mm(pb, wbf[gi % 2][:, kt, j * 128:(j + 1) * 128], hT[ii % 2][:, kt, :], start=(kt == 0), stop=(kt == 7))
            lanes.push(epi(tag, col, tb, pb, it))
            it += 1
    lanes.drain()


def gemm_tm(k, W, col0, ncols, dst, pfx):
    P, cfg = k.P, k.cfg
    NTB = cfg.NTB
    wst = [P.sbuf(pfx + "wst%d" % i, [128, 8, 512], F32) for i in range(2)]
    wbf = [P.sbuf(pfx + "wbf%d" % i, [128, 8, 512], BF16) for i in range(2)]
    hT = [P.sbuf(pfx + "hT%d" % i, [128, 8, TB], BF16) for i in range(2)]
    vt = [P.sbuf(pfx + "vt%d" % i, [128, 512], BF16) for i in range(2)]
    ng = ncols // 512

    def loadw(gi):
        s = gi % 2
        P.dma(wst[s][:], W[:, col0 + gi * 512:col0 + (gi + 1) * 512].rearrange("(kt p) n -> p kt n", p=128), pfx + "w%d" % s)
        P.copy(wbf[s][:], wst[s][:], eng="pool")

    items = [(gi, tb) for gi in range(ng) for tb in range(NTB)]

    def loadh(ii):
        gi, tb = items[ii]
        P.dma(hT[ii % 2][:], k.HT[:, tb * TB:(tb + 1) * TB].rearrange("(kt p) t -> p kt t", p=128), pfx + "h%d" % (ii % 2))

    loadw(0)
    loadh(0)
    it = 0
    for ii, (gi, tb) in enumerate(items):
        if ii + 1 < len(items):
            loadh(ii + 1)
        if tb == 0 and gi + 1 < ng:
            loadw(gi + 1)
        for sub in range(4):
            pb = k.ps[:, 4 + it % 2, :]
            for kt in range(8):
                P.mm(pb, hT[ii % 2][:, kt, sub * 128:(sub + 1) * 128], wbf[gi % 2][:, kt, :], start=(kt == 0), stop=(kt == 7))
            P.copy(vt[it % 2][:], pb, eng=("act" if it % 2 == 0 else "dve"))
            r0 = tb * TB + sub * 128
            P.dma(dst[r0:r0 + 128, gi * 512:(gi + 1) * 512], vt[it % 2][:], pfx + "v%d" % (it % 2))
            it += 1


def chunk_engine(k, ND, NV, NU, G, pfx, SCsrc=None):
    P, cfg = k.P, k.cfg
    NCH = cfg.NCH
    SCsrc = k.SC if SCsrc is None else SCsrc
    CB = 4
    BW = CB * CH
    NTB = NCH // CB
    NG = NU // G
    GD = G * ND
    GV = G * NV
    VW = NV * 128
    S = [P.sbuf(pfx + "S%d" % g, [128, GD, VW], F32) for g in range(NG)]
    Sbf = [P.sbuf(pfx + "Sb%d" % g, [128, GD, VW], BF16) for g in range(NG)]
    t1 = [P.sbuf(pfx + "t1%d" % i, [128, GD, VW], F32) for i in range(2)]
    t2 = [P.sbuf(pfx + "t2%d" % i, [128, GD, VW], F32) for i in range(2)]
    PT = [P.sbuf(pfx + "PT%d" % i, [64, G, 64], BF16) for i in range(2)]
    ktok = [P.sbuf(pfx + "kt%d" % i, [64, GD * 128], BF16) for i in range(2)]
    qT = [[P.sbuf(pfx + "q%d_%d" % (b, g), [128, GD, BW], BF16) for g in range(NG)] for b in range(2)]
    kT = [[P.sbuf(pfx + "k%d_%d" % (b, g), [128, GD, BW], BF16) for g in range(NG)] for b in range(2)]
    vb = [[P.sbuf(pfx + "v%d_%d" % (b, g), [64, CB, GV * 128], BF16) for g in range(NG)] for b in range(2)]
    sc = [[P.sbuf(pfx + "s%d_%d" % (b, g), [128, 3, GD, CB], F32) for g in range(NG)] for b in range(2)]
    ob = [[P.sbuf(pfx + "o%d_%d" % (b, g), [128, GV, BW], F32) for g in range(NG)] for b in range(2)]

    def load(dr, bi, blk):
        b = bi % 2
        for g in range(NG):
            r0 = g * GD * 128
            tsl = slice(blk * BW, (blk + 1) * BW)
            P.dma(qT[b][g][:], k.QK[dr, 0, r0:r0 + GD * 128, tsl].rearrange("(j p) t -> p j t", p=128), pfx + "lq%d_%d" % (b, g))
            P.dma(kT[b][g][:], k.QK[dr, 1, r0:r0 + GD * 128, tsl].rearrange("(j p) t -> p j t", p=128), pfx + "lk%d_%d" % (b, g))
            c0 = g * GV * 128
            P.dma(vb[b][g][:], k.V[tsl, c0:c0 + GV * 128].rearrange("(c s) n -> s c n", s=64), pfx + "lv%d_%d" % (b, g))
            for j3 in range(3):
                P.dma(sc[b][g][:, j3, :, :], SCsrc[dr, j3, r0:r0 + GD * 128, blk * CB:(blk + 1) * CB].rearrange("(j p) c -> p j c", p=128),
                      pfx + "ls%d_%d" % (b, g), allow_slow_non_contiguous=True)

    def body(dr, b, c, g, it):
        csl = slice(c * 64, (c + 1) * 64)
        p2 = it % 2
        psA = k.ps[:, p2, :]
        psS = k.ps[:, 2 + 2 * p2:4 + 2 * p2, :] if GD * VW > 512 else k.ps[:, 2 + p2:3 + p2, :]
        psS = psS.rearrange("p a b -> p (a b)")
        pstr = k.psb[0:64, p2, 0:GD * 128]
        for u in range(G):
            for j in range(ND):
                P.mm(psA[0:64, u * 64:(u + 1) * 64], kT[b][g][:, u * ND + j, csl], qT[b][g][:, u * ND + j, csl],
                     start=(j == 0), stop=(j == ND - 1))
        for uj in range(GD):
            P.transpose(pstr[:, uj * 128:(uj + 1) * 128], kT[b][g][:, uj, csl], k.identb[:])
        yield
        trim = k.tri[:, dr * 64:(dr + 1) * 64]
        P.tt(PT[p2][:], psA[0:64, 0:G * 64].rearrange("p (u t) -> p u t", u=G),
             trim.rearrange("p (o t) -> p o t", o=1).to_broadcast([64, G, 64]), ALU.mult)
        P.copy(ktok[p2][:], pstr, eng="act")
        if ND == 1:
            P.tt(Sbf[g][:], S[g][:], sc[b][g][:, 0, :, c:c + 1].to_broadcast([128, GD, VW]), ALU.mult, eng="pool")
        else:
            for uj in range(GD):
                P.act(Sbf[g][:, uj, :], S[g][:, uj, :], AF.Copy, scale=sc[b][g][:, 0, uj, c:c + 1])
        yield
        for u in range(G):
            for i in range(NV):
                po = psA[:, 256 + (u * NV + i) * 64:256 + (u * NV + i + 1) * 64]
                P.mm(po, vb[b][g][:, c, (u * NV + i) * 128:(u * NV + i + 1) * 128], PT[p2][:, u, :], start=True, stop=False)
                for j in range(ND):
                    P.mm(po, Sbf[g][:, u * ND + j, i * 128:(i + 1) * 128], qT[b][g][:, u * ND + j, csl],
                         start=False, stop=(j == ND - 1))
        for uj in range(GD):
            u = uj // ND
            P.mm(psS[:, uj * VW:(uj + 1) * VW], ktok[p2][:, uj * 128:(uj + 1) * 128],
                 vb[b][g][:, c, u * VW:(u + 1) * VW], start=True, stop=True)
        yield
        P.copy(ob[b][g][:, :, csl], psA[:, 256:256 + GV * 64].rearrange("p (a t) -> p a t", a=GV), eng="act")
        if ND == 1:
            P.tt(t1[p2][:], S[g][:], sc[b][g][:, 1, :, c:c + 1].to_broadcast([128, GD, VW]), ALU.mult, eng="pool")
            yield
            P.tt(t2[p2][:], psS.rearrange("p (a v) -> p a v", a=GD),
                 sc[b][g][:, 2, :, c:c + 1].to_broadcast([128, GD, VW]), ALU.mult)
            yield
            P.tt(S[g][:], t1[p2][:], t2[p2][:], ALU.add)
        else:
            for uj in range(GD):
                P.ts(t1[p2][:, uj, :], S[g][:, uj, :], sc[b][g][:, 1, uj, c:c + 1], None, ALU.mult, eng="pool")
            yield
            for uj in range(GD):
                P.stt(S[g][:, uj, :], psS[:, uj * VW:(uj + 1) * VW], sc[b][g][:, 2, uj, c:c + 1], t1[p2][:, uj, :], ALU.mult, ALU.add)
                yield

    it = 0
    for dr in range(2):
        for g in range(NG):
            P.memset(S[g][:], 0.0)
        blks = list(range(NTB)) if dr == 0 else list(range(NTB - 1, -1, -1))
        load(dr, 0, blks[0])
        for bi, blk in enumerate(blks):
            b = bi % 2
            if bi + 1 < len(blks):
                load(dr, bi + 1, blks[bi + 1])
            cs = list(range(CB)) if dr == 0 else list(range(CB - 1, -1, -1))
            for c in cs:
                lanes = Lanes(2)
                for g in range(NG):
                    lanes.push(body(dr, b, c, g, it))
                    it += 1
                lanes.drain()
            for g in range(NG):
                r0 = g * GV * 128
                P.dma(k.OT[dr, r0:r0 + GV * 128, blk * BW:(blk + 1) * BW].rearrange("(i p) t -> p i t", p=128), ob[b][g][:],
                      pfx + "so%d_%d" % (b, g))


def outproj_setup(k, l, Wout, pfx):
    P = k.P
    wo = P.sbuf(pfx + "wo", [128, 16, D], BF16)
    st = K()
    st.wo = wo
    st.gate = P.sbuf(pfx + "gate", [128, D], F32)
    st.xt = [P.sbuf(pfx + "x%d" % i, [128, D], F32) for i in range(2)]
    st.ty = [P.sbuf(pfx + "ty%d" % i, [128, D], F32) for i in range(2)]
    st.it = 0
    mark = P.bump
    wst = [P.sbuf(pfx + "wos%d" % i, [128, 16, 256], F32) for i in range(2)]
    for q in range(4):
        P.dma(wst[q % 2][:], Wout[:, q * 256:(q + 1) * 256].rearrange("(e p) n -> p e n", p=128), pfx + "wo%d" % (q % 2))
        P.copy(wo[:, :, q * 256:(q + 1) * 256], wst[q % 2][:], eng="pool")
    P.bump = mark
    return st


def outproj_block(k, l, tb, GT, st, xsrc, pfx):
    P, cfg = k.P, k.cfg
    if (tb * TB) % cfg.SL == 0:
        slot = (tb * TB) // cfg.SL
        P.dma(st.gate[:], bc_rows(k.MOD[l, slot:slot + 1, 2 * D:3 * D], 128), pfx + "gate")
    for sub in range(4):
        s = st.it % 2
        r0 = tb * TB + sub * 128
        P.dma(st.xt[s][:], xsrc[r0:r0 + 128, :], pfx + "x%d" % s)
        for dh in range(2):
            pb = k.ps[:, 4 + dh, :]
            for e in range(16):
                P.mm(pb, GT[:, e, sub * 128:(sub + 1) * 128], st.wo[:, e, dh * 512:(dh + 1) * 512], start=(e == 0), stop=(e == 15))
            P.tt(st.ty[s][:, dh * 512:(dh + 1) * 512], pb, st.gate[:, dh * 512:(dh + 1) * 512], ALU.mult)
        P.tt(st.ty[s][:], st.ty[s][:], st.xt[s][:], ALU.add, eng="pool")
        P.dma(k.X[r0:r0 + 128, :], st.ty[s][:], pfx + "y%d" % s)
        st.it += 1


def hgrn2(k, l, xsrc):
    P, cfg = k.P, k.cfg
    NTB, NCH = cfg.NTB, cfg.NCH
    W = k.hg_w_in
    P.reset("hgrn2_A")
    oq = [P.sbuf("ha_o%d" % i, [128, TB], F32) for i in range(4)]
    sgq = [P.sbuf("ha_s%d" % i, [128, TB], F32) for i in range(4)]

    def epiA(tag, col, tb, pb, it):
        s = it % 4
        dst = k.QT if tag == "q" else k.ZT
        row = col if tag == "q" else col - 4 * E
        P.act(sgq[s][:], pb, AF.Sigmoid)
        yield
        P.tt(oq[s][:], pb, sgq[s][:], ALU.mult, eng=("dve" if it % 2 == 0 else "dve"))
        yield
        P.dma(dst[row:row + 128, tb * TB:(tb + 1) * TB], oq[s][:], "ha_o%d" % s)

    cols = [(h * 128, "q") for h in range(16)] + [(4 * E + h * 128, "z") for h in range(16)]
    groups = [cols[i:i + 4] for i in range(0, len(cols), 4)]
    gemm_fm(k, W, groups, epiA, "ha_")
    P.barrier()
    P.reset("hgrn2_A2")
    gemm_tm(k, W, 3 * E, E, k.V, "hv_")
    P.barrier()
    P.reset("hgrn2_B")
    lbr = P.sbuf("hb_lbr", [128, 16, 5], F32)
    lbe = P.sbuf("hb_lbe", [128, 16, 5], F32)
    den = P.sbuf("hb_den", [128, 16], F32)
    num = P.sbuf("hb_num", [128, 16], F32)
    lbv = P.sbuf("hb_lb", [128, 16], F32)
    oml = P.sbuf("hb_oml", [128, 16], F32)
    for r in range(5):
        P.dma(lbr[:, :, r], k.hg_lb[r, :].rearrange("(j p) -> p j", p=128), "hb_lb", allow_slow_non_contiguous=True)
    P.act(lbe[:], lbr[:], AF.Exp)
    P.add("dve", lambda e: e.reduce_sum(den[:], lbe[:], mybir.AxisListType.X), [lbe[:]], [den[:]])
    P.add("dve", lambda e: e.reduce_sum(num[:], lbe[:, :, 0:l + 1], mybir.AxisListType.X), [lbe[:]], [num[:]])
    P.add("dve", lambda e: e.reciprocal(den[:], den[:]), [den[:]], [den[:]])
    P.tt(lbv[:], num[:], den[:], ALU.mult)
    P.ts(oml[:], lbv[:], -1.0, 1.0, ALU.mult, ALU.add)
    nb = 3
    tl = lambda nm, sh, dt=F32: [P.sbuf("hb_%s%d" % (nm, i), sh, dt) for i in range(nb)]
    qb, sg, ff, gg, kk, bb, bc, e1, e2 = (tl(n, [128, TB]) for n in ("qb", "sg", "ff", "gg", "kk", "bb", "bc", "e1", "e2"))
    qt = tl("qt", [128, TB], BF16)
    kt_ = tl("kt", [128, TB], BF16)
    scs = tl("sc", [128, 3, 8])
    dtmp = tl("dt", [128, 8])

    def epiB(tag, col, tb, pb, it):
        _, dr, h = tag
        s = it % nb
        tsl = slice(tb * TB, (tb + 1) * TB)
        P.dma(qb[s][:], k.QT[h * 128:(h + 1) * 128, tsl], "hb_q%d" % s)
        P.act(sg[s][:], pb, AF.Sigmoid)
        yield
        P.ts(ff[s][:], sg[s][:], oml[:, h:h + 1], lbv[:, h:h + 1], ALU.mult, ALU.add)
        yield
        P.act(gg[s][:], ff[s][:], AF.Ln)
        P.ts(kk[s][:], ff[s][:], -1.0, 1.0, ALU.mult, ALU.add, eng="pool")
        yield
        if dr == 0:
            P.scan(bb[s][:], k.rmask[:], gg[s][:], 0.0)
        else:
            P.scan(bb[s][:, ::-1], k.rmask[:], gg[s][:, ::-1], 0.0)
        yield
        b3 = bb[s][:].rearrange("p (c t) -> p c t", t=64)
        P.tt(bc[s][:].rearrange("p (c t) -> p c t", t=64), b3, b3[:, :, 32:33].to_broadcast([128, 8, 64]), ALU.subtract, eng="pool")
        yield
        P.act(e1[s][:], bc[s][:], AF.Exp)
        yield
        P.act(e2[s][:], bc[s][:], AF.Exp, scale=-1.0)
        P.tt(qt[s][:], qb[s][:], e1[s][:], ALU.mult)
        yield
        P.tt(kt_[s][:], kk[s][:], e2[s][:], ALU.mult, eng="pool")
        refc = b3[:, :, 32]
        lastc = b3[:, :, 63] if dr == 0 else b3[:, :, 0]
        P.act(scs[s][:, 0, :], refc, AF.Exp)
        yield
        P.act(scs[s][:, 1, :], lastc, AF.Exp)
        P.tt(dtmp[s][:], lastc, refc, ALU.subtract)
        yield
        P.act(scs[s][:, 2, :], dtmp[s][:], AF.Exp)
        cr = k.carry[:, dr * NCH + tb * 8:dr * NCH + tb * 8 + 8]
        yield
        P.tt(scs[s][:, 0:2, :], scs[s][:, 0:2, :], cr.rearrange("p (o c) -> p o c", o=1).to_broadcast([128, 2, 8]), ALU.mult)
        rows = slice(h * 128, (h + 1) * 128)
        P.dma(k.QK[dr, 0, rows, tsl], qt[s][:], "hb_sq%d" % s)
        yield
        P.dma(k.QK[dr, 1, rows, tsl], kt_[s][:], "hb_sk%d" % s)
        P.dma(k.SC[dr, :, rows, tb * 8:(tb + 1) * 8].rearrange("j p c -> p j c"), scs[s][:], "hb_ss%d" % s,
              allow_slow_non_contiguous=True)

    cols = [(E + dr * E + h * 128, ("f", dr, h)) for dr in range(2) for h in range(16)]
    groups = [cols[i:i + 4] for i in range(0, len(cols), 4)]
    gemm_fm(k, W, groups, epiB, "hb_")
    P.barrier()
    P.reset("hgrn2_C")
    chunk_engine(k, 1, 1, 16, 4, "hc_")
    P.barrier()
    P.reset("hgrn2_D")
    st = outproj_setup(k, l, k.hg_w_out, "hd_")
    ngc = P.sbuf("hd_ng", [128, 1], F32)
    P.dma(ngc[:], k.hg_norm_g.rearrange("o p -> p o"), "hd_ng", allow_slow_non_contiguous=True)
    GT = [P.sbuf("hd_GT%d" % i, [128, 16, TB], BF16) for i in range(2)]
    tl = lambda nm, sh, dt=F32: [P.sbuf("hd_%s%d" % (nm, i), sh, dt) for i in range(2)]
    NL = 4
    tl = lambda nm, sh, dt=F32: [P.sbuf("hd_%s%d" % (nm, i), sh, dt) for i in range(NL)]
    of, obk, zz, oo, sq, rs = (tl(n, [128, TB]) for n in ("of", "ob", "zz", "oo", "sq", "rs"))

    def body(tb, h, it):
        s = it % NL
        tsl = slice(tb * TB, (tb + 1) * TB)
        rows = slice(h * 128, (h + 1) * 128)
        P.dma(of[s][:], k.OT[0, rows, tsl], "hd_of%d" % s)
        P.dma(obk[s][:], k.OT[1, rows, tsl], "hd_ob%d" % s)
        P.dma(zz[s][:], k.ZT[rows, tsl], "hd_zz%d" % s)
        yield
        P.tt(oo[s][:], of[s][:], obk[s][:], ALU.add, eng="pool")
        yield
        P.act(sq[s][:], oo[s][:], AF.Square)
        yield
        pb = k.ps[:, it % 4, :]
        P.mm(pb, k.ones[:], sq[s][:])
        yield
        P.act(rs[s][:], pb, AF.Sqrt, bias=k.epsT[:], scale=1.0 / 128)
        yield
        P.add("dve", lambda e, a=rs[s][:]: e.reciprocal(a, a), [rs[s][:]], [rs[s][:]])
        yield
        P.tt(oo[s][:], oo[s][:], rs[s][:], ALU.mult, eng="pool")
        yield
        P.stt(GT[tb % 2][:, h, :], oo[s][:], ngc[:], zz[s][:], ALU.mult, ALU.mult)

    it = 0
    for tb in range(NTB):
        lanes = Lanes(NL)
        for h in range(16):
            lanes.push(body(tb, h, it))
            it += 1
        lanes.drain()
        outproj_block(k, l, tb, GT[tb % 2], st, xsrc, "hd_")


def host_consts(cfg, carry):
    NCH = cfg.NCH
    cps = cfg.SL // CH
    c = {}
    c["k_ident"] = np.eye(128, dtype=np.float32)
    ii = np.arange(64)
    fw = (ii[:, None] <= ii[None, :]).astype(np.float32)
    bw = (ii[:, None] >= ii[None, :]).astype(np.float32)
    c["k_tri"] = np.concatenate([fw, bw], axis=1)
    rm = np.ones((128, TB), np.float32)
    rm[:, ::64] = 0.0
    c["k_rmask"] = rm
    cr = np.ones((128, 2 * NCH), np.float32)
    for ch in range(NCH):
        if ch % cps == 0 and ch > 0:
            cr[:, ch] = carry
        if (ch + 1) % cps == 0 and ch < NCH - 1:
            cr[:, NCH + ch] = carry
    c["k_carry"] = cr
    c["k_carry1"] = np.full((128, 1), carry, np.float32)
    return c


def host_consts_rt(cfg, carry, seq_pos):
    NCH = cfg.NCH
    cps = cfg.SL // CH
    c = {}
    inv = (10000.0 ** (-np.arange(0, 256, 2, dtype=np.float32) / np.float32(256))).astype(np.float32)
    ang = (seq_pos.astype(np.float32)[None, :] * inv[:, None]).astype(np.float32)
    c["k_cos"] = np.cos(ang).astype(np.float32)
    c["k_sin"] = np.sin(ang).astype(np.float32)
    hidx = np.arange(4, dtype=np.float32)
    lg = [np.log1p(-np.exp2(-5.0 - hidx)).astype(np.float32), np.log1p(-np.exp2(-5.5 - hidx)).astype(np.float32)]
    pos = np.arange(64, dtype=np.float64)
    dt = np.zeros((2, 4, 2, 64), np.float64)
    sc = np.ones((2, 3, 1024, NCH), np.float64)
    for dr in range(2):
        cnt = (pos + 1.0) if dr == 0 else (64.0 - pos)
        for hd in range(4):
            g = float(lg[dr][hd])
            dt[dr, hd, 0] = np.exp(g * cnt)
            dt[dr, hd, 1] = np.exp(-g * cnt) * (256.0 ** -0.5)
            rows = slice(hd * 256, (hd + 1) * 256)
            sc[dr, 1, rows, :] = np.exp(g * 64.0)
            sc[dr, 2, rows, :] = np.exp(g * 64.0)
        for ch in range(NCH):
            bnd = (ch % cps == 0 and ch > 0) if dr == 0 else ((ch + 1) % cps == 0 and ch < NCH - 1)
            if bnd:
                sc[dr, 0, :, ch] *= carry
                sc[dr, 1, :, ch] *= carry
    c["k_dtab"] = np.broadcast_to(dt.reshape(1, -1), (128, 16 * 64)).astype(np.float32).copy()
    c["k_rtsc"] = sc.astype(np.float32)
    return c


WNAMES = ["ada_w", "ada_b", "norm_g", "hg_lb", "hg_w_in", "hg_norm_g", "hg_w_out"]


def core_inputs(cfg, x_slots, c_slots, carry, w, seq_pos=None, Lc=None):
    SL = cfg.SL
    xin = np.zeros((cfg.T, D), np.float32)
    cin = np.zeros((cfg.NS, D), np.float32)
    for s in range(cfg.NS):
        if x_slots[s] is not None:
            xin[s * SL:(s + 1) * SL] = x_slots[s]
            cin[s] = c_slots[s]
    m = {"xin": xin, "cin": cin}
    m.update(host_consts(cfg, carry))
    dp = cfg.depth
    m["ada_w"] = np.ascontiguousarray(w["ada_w"][:dp])
    m["ada_b"] = np.ascontiguousarray(w["ada_b"][:dp])
    m["norm_g"] = np.ascontiguousarray(w["norm_g"][:dp])
    m["final_g"] = np.ascontiguousarray(w["final_g"]).reshape(1, D)
    m["hg_lb"] = np.ascontiguousarray(w["hg_lb"])
    m["hg_w_in"] = np.ascontiguousarray(w["hg_w_in"][0])
    m["hg_norm_g"] = np.ascontiguousarray(w["hg_norm_g"][0]).reshape(1, 128)
    m["hg_w_out"] = np.ascontiguousarray(w["hg_w_out"][0])
    if 2 in cfg.kinds:
        m.update(host_consts_rt(cfg, carry, seq_pos))
        m["rt_w_in"] = np.ascontiguousarray(w["rt_w_in"][0])
        m["rt_gn_g"] = np.ascontiguousarray(w["rt_gn_g"][0]).reshape(1, E)
        m["rt_w_out"] = np.ascontiguousarray(w["rt_w_out"][0])
    if 1 in cfg.kinds:
        m.update(host_consts_hy(cfg, Lc, [xs is not None for xs in x_slots]))
        for nm in ["hy_w_in", "hy_conv_w", "hy_f_w1", "hy_f_w2", "hy_f_b2", "hy_f_wout", "hy_w_out"]:
            m[nm] = np.ascontiguousarray(w[nm][0])
        for nm in ["hy_b_in", "hy_conv_b", "hy_f_b1", "hy_f_freq", "hy_skip"]:
            m[nm] = np.ascontiguousarray(w[nm][0]).reshape(1, -1)
    if 3 in cfg.kinds:
        m["lru_w_in"] = np.ascontiguousarray(w["lru_w_in"][0])
        m["lru_conv_w"] = np.ascontiguousarray(w["lru_conv_w"][0])
        m["lru_conv_b"] = np.ascontiguousarray(w["lru_conv_b"][0]).reshape(1, E)
        m["lru_gate_w"] = np.ascontiguousarray(w["lru_gate_w"][0])
        m["lru_gate_b"] = np.ascontiguousarray(w["lru_gate_b"][0])
        m["lru_lambda"] = np.ascontiguousarray(w["lru_lambda"][0])
        m["lru_w_out"] = np.ascontiguousarray(w["lru_w_out"][0])
    return m


def retention(k, l, xsrc):
    P, cfg = k.P, k.cfg
    NTB, NCH = cfg.NTB, cfg.NCH
    W = k.rt_w_in
    P.reset("retention_A")
    dtab = P.sbuf("ra_dtab", [128, 16, 64], F32)
    P.dma(dtab[:], k.k_dtab.rearrange("p (a b) -> p a b", a=16), "ra_dt")
    NP = 3
    tl = lambda nm, sh, dt=F32, n=NP: [P.sbuf("ra_%s%d" % (nm, i), sh, dt) for i in range(n)]
    x1, x2, cs_, sn_, o1, o2, ta, tb_ = (tl(n, [128, TB]) for n in ("x1", "x2", "cs", "sn", "o1", "o2", "ta", "tb"))
    obf = tl("obf", [128, TB], BF16, 8)
    oz = tl("oz", [128, TB], F32, 4)
    sgz = tl("sgz", [128, TB], F32, 4)
    cnt = {"p": 0, "o": 0}

    def epiA(tag, col, tb, pb, it):
        kind, hd, a = tag
        tsl = slice(tb * TB, (tb + 1) * TB)
        if kind == "g":
            s = it % 4
            P.act(sgz[s][:], pb, AF.Sigmoid)
            yield
            P.tt(oz[s][:], pb, sgz[s][:], ALU.mult)
            yield
            P.dma(k.ZT[col - 4096:col - 4096 + 128, tsl], oz[s][:], "ra_oz%d" % s)
            return
        s = cnt["p"] % NP
        if a == 0:
            P.copy(x1[s][:], pb, eng="act")
            return
        P.copy(x2[s][:], pb, eng="act")
        cnt["p"] += 1
        qk = 0 if kind == "q" else 1
        P.dma(cs_[s][:], k.k_cos[:, tsl], "ra_c%d" % s)
        P.dma(sn_[s][:], k.k_sin[:, tsl], "ra_s%d" % s)
        yield
        P.tt(o1[s][:], x1[s][:], cs_[s][:], ALU.mult)
        P.tt(ta[s][:], x2[s][:], sn_[s][:], ALU.mult, eng="pool")
        yield
        P.tt(o2[s][:], x1[s][:], sn_[s][:], ALU.mult, eng="pool")
        P.tt(tb_[s][:], x2[s][:], cs_[s][:], ALU.mult)
        yield
        P.tt(o1[s][:], o1[s][:], ta[s][:], ALU.subtract)
        P.tt(o2[s][:], o2[s][:], tb_[s][:], ALU.add, eng="pool")
        yield
        for dr in range(2):
            dsl = dtab[:, (dr * 4 + hd) * 2 + qk, :].rearrange("p (o t) -> p o t", o=1).to_broadcast([128, TB // 64, 64])
            for half, src in enumerate((o1[s], o2[s])):
                oi = cnt["o"] % 8
                ob_ = obf[oi]
                P.tt(ob_[:].rearrange("p (c t) -> p c t", t=64), src[:].rearrange("p (c t) -> p c t", t=64), dsl, ALU.mult,
                     eng=("dve" if half == 0 else "pool"))
                r0 = (hd * 2 + half) * 128
                cnt["o"] += 1
                yield
                P.dma(k.QK[dr, qk, r0:r0 + 128, tsl], ob_[:], "ra_so%d" % oi)

    groups = []
    for hd in range(4):
        groups.append([(hd * 256, ("q", hd, 0)), (hd * 256 + 128, ("q", hd, 1)),
                       (1024 + hd * 256, ("k", hd, 0)), (1024 + hd * 256 + 128, ("k", hd, 1))])
    gcols = [(4096 + j * 128, ("g", 0, 0)) for j in range(16)]
    groups += [gcols[i:i + 4] for i in range(0, 16, 4)]
    gemm_fm(k, W, groups, epiA, "ra_")
    P.barrier()
    P.reset("retention_A2")
    gemm_tm(k, W, 2048, E, k.V, "rv_")
    P.barrier()
    P.reset("retention_B")
    chunk_engine(k, 2, 4, 4, 1, "rc_", SCsrc=k.k_rtsc)
    P.barrier()
    P.reset("retention_C")
    st = outproj_setup(k, l, k.rt_w_out, "rd_")
    gn = P.sbuf("rd_gn", [128, 16], F32)
    P.dma(gn[:], k.rt_gn_g.rearrange("o (j p) -> p (o j)", p=128), "rd_gn", allow_slow_non_contiguous=True)
    GT = [P.sbuf("rd_GT%d" % i, [128, 16, TB], BF16) for i in range(2)]
    NL = 2
    tl = lambda nm, sh, dt=F32, n=2: [P.sbuf("rd_%s%d" % (nm, i), sh, dt) for i in range(n)]
    of, obk, zz = (tl(n, [128, TB], F32, NL * 4) for n in ("of", "ob", "zz"))
    sq = obk
    oo = of
    rs = tl("rs", [128, TB], F32, NL)

    def body(tb, hd, hi):
        tsl = slice(tb * TB, (tb + 1) * TB)
        ln = hi % NL
        pb = k.ps[:, hi % 4, :]
        for i in range(4):
            e = hd * 4 + i
            s = ln * 4 + i
            rows = slice(e * 128, (e + 1) * 128)
            P.dma(of[s][:], k.OT[0, rows, tsl], "rd_of%d" % s)
            P.dma(obk[s][:], k.OT[1, rows, tsl], "rd_ob%d" % s)
            P.dma(zz[s][:], k.ZT[rows, tsl], "rd_zz%d" % s)
        yield
        for i in range(4):
            s = ln * 4 + i
            P.tt(oo[s][:], of[s][:], obk[s][:], ALU.add, eng=("pool" if i % 2 == 0 else "dve"))
            yield
            P.act(sq[s][:], oo[s][:], AF.Square)
            yield
        for i in range(4):
            s = ln * 4 + i
            P.mm(pb, k.ones[:], sq[s][:], start=(i == 0), stop=(i == 3))
        yield
        r_ = rs[ln]
        P.act(r_[:], pb, AF.Sqrt, bias=k.epsT[:], scale=1.0 / 512)
        yield
        P.add("dve", lambda e_, a=r_[:]: e_.reciprocal(a, a), [r_[:]], [r_[:]])
        yield
        for i in range(4):
            e = hd * 4 + i
            s = ln * 4 + i
            P.tt(oo[s][:], oo[s][:], r_[:], ALU.mult, eng="pool")
            yield
            P.stt(GT[tb % 2][:, e, :], oo[s][:], gn[:, e:e + 1], zz[s][:], ALU.mult, ALU.mult)
            yield

    hi = 0
    for tb in range(NTB):
        lanes = Lanes(NL)
        for hd in range(4):
            lanes.push(body(tb, hd, hi))
            hi += 1
        lanes.drain()
        outproj_block(k, l, tb, GT[tb % 2], st, xsrc, "rd_")


def rglru(k, l, xsrc):
    P, cfg = k.P, k.cfg
    NTB, T, SL = cfg.NTB, cfg.T, cfg.SL
    W = k.lru_w_in
    P.reset("rglru_A")
    oq = [P.sbuf("la_o%d" % i, [128, TB], F32) for i in range(4)]
    sgq = [P.sbuf("la_s%d" % i, [128, TB], F32) for i in range(4)]

    def epiA(tag, col, tb, pb, it):
        s = it % 4
        tsl = slice(tb * TB, (tb + 1) * TB)
        if tag == "x":
            P.copy(oq[s][:], pb, eng="act")
            yield
            P.dma(k.QT[col:col + 128, tsl], oq[s][:], "la_o%d" % s)
        else:
            P.act(sgq[s][:], pb, AF.Sigmoid)
            yield
            P.tt(oq[s][:], pb, sgq[s][:], ALU.mult)
            yield
            P.dma(k.ZT[col - E:col - E + 128, tsl], oq[s][:], "la_o%d" % s)

    cols = [(j * 128, "x") for j in range(16)] + [(E + j * 128, "z") for j in range(16)]
    groups = [cols[i:i + 4] for i in range(0, 32, 4)]
    gemm_fm(k, W, groups, epiA, "la_")
    P.barrier()
    P.reset("rglru_B")
    BL = 512
    NB = T // BL
    NL = 6
    cw = P.sbuf("lb_cw", [128, 16, 4], F32)
    cb = P.sbuf("lb_cb", [128, 16], F32)
    gb = P.sbuf("lb_gb", [128, 4, 16], F32)
    lam = P.sbuf("lb_lam", [128, 2, 16], F32)
    m8 = P.sbuf("lb_m8", [128, 2, 16], F32)
    for jj in range(4):
        P.dma(cw[:, :, jj], k.lru_conv_w[jj, :].rearrange("(j p) -> p j", p=128), "lb_c", allow_slow_non_contiguous=True)
    P.dma(cb[:], k.lru_conv_b.rearrange("o (j p) -> p (o j)", p=128), "lb_c", allow_slow_non_contiguous=True)
    for d in range(2):
        P.dma(lam[:, d, :], k.lru_lambda[d, :].rearrange("(j p) -> p j", p=128), "lb_c", allow_slow_non_contiguous=True)
        for g in range(2):
            P.dma(gb[:, d * 2 + g, :], k.lru_gate_b[d, g, :].rearrange("(j p) -> p j", p=128), "lb_c", allow_slow_non_contiguous=True)
    P.act(m8[:], lam[:], AF.Exp, scale=-1.0)
    P.act(m8[:], m8[:], AF.Ln, bias=k.ones[:, 0:1], scale=1.0)
    P.ts(m8[:], m8[:], -8.0, None, ALU.mult)
    gw = [P.sbuf("lb_gw%d" % i, [128, 4, 128], F32) for i in range(NL)]
    tl = lambda nm, sh, dt=F32, n=NL: [P.sbuf("lb_%s%d" % (nm, i), sh, dt) for i in range(n)]
    xr = tl("xr", [128, BL + 3])
    xb, rr, ii, aa, hf, zz = (tl(n, [128, BL]) for n in ("xb", "rr", "ii", "aa", "hf", "zz"))
    a2, sq, bx, bb = rr, rr, ii, ii
    hh = [[P.sbuf("lb_hh%d_%d" % (i, q), [128, BL], F32) for q in range(2)] for i in range(NL)]
    gt = tl("gt", [128, BL], BF16)
    hinit = tl("hi", [128, 1])

    def chain(j, s):
        rows = slice(j * 128, (j + 1) * 128)
        gws = gw[s]
        for d in range(2):
            for g in range(2):
                P.dma(gws[:, d * 2 + g, :], k.lru_gate_w[d, g, j, :, :], "lb_gw%d" % s)
        yield
        cnt = 0
        for d in range(2):
            order = list(range(NB)) if d == 0 else list(range(NB - 1, -1, -1))
            prev = None
            for bi in order:
                t0 = bi * BL
                hcur = hh[s][cnt % 2]
                cnt += 1
                lo = max(t0 - 2, 0)
                hi_ = min(t0 + BL + 1, T)
                if t0 == 0:
                    P.memset(xr[s][:, 0:2], 0.0)
                if t0 + BL == T:
                    P.memset(xr[s][:, BL + 2:BL + 3], 0.0)
                P.dma(xr[s][:, lo - (t0 - 2):hi_ - (t0 - 2)], k.QT[rows, lo:hi_], "lb_x%d" % s)
                if t0 % SL == 0 and t0 > 0:
                    P.ts(xr[s][:, 0:2], xr[s][:, 0:2], k.carry1[:], None, ALU.mult, eng="pool")
                if (t0 + BL) % SL == 0 and t0 + BL < T:
                    P.ts(xr[s][:, BL + 2:BL + 3], xr[s][:, BL + 2:BL + 3], k.carry1[:], None, ALU.mult, eng="pool")
                yield
                P.ts(xb[s][:], xr[s][:, 0:BL], cw[:, j, 0:1], cb[:, j:j + 1], ALU.mult, ALU.add, eng="pool")
                yield
                P.stt(xb[s][:], xr[s][:, 1:BL + 1], cw[:, j, 1:2], xb[s][:], ALU.mult, ALU.add)
                yield
                P.stt(xb[s][:], xr[s][:, 2:BL + 2], cw[:, j, 2:3], xb[s][:], ALU.mult, ALU.add)
                yield
                P.stt(xb[s][:], xr[s][:, 3:BL + 3], cw[:, j, 3:4], xb[s][:], ALU.mult, ALU.add)
                yield
                p0 = k.ps[:, s, :]
                P.mm(p0, gws[:, d * 2 + 0, :], xb[s][:])
                yield
                P.act(rr[s][:], p0, AF.Sigmoid, bias=gb[:, d * 2 + 0, j:j + 1])
                yield
                P.mm(p0, gws[:, d * 2 + 1, :], xb[s][:])
                yield
                P.act(ii[s][:], p0, AF.Sigmoid, bias=gb[:, d * 2 + 1, j:j + 1])
                yield
                P.act(aa[s][:], rr[s][:], AF.Exp, scale=m8[:, d, j:j + 1])
                P.tt(bx[s][:], ii[s][:], xb[s][:], ALU.mult)
                yield
                P.tt(a2[s][:], aa[s][:], aa[s][:], ALU.mult, eng="pool")
                yield
                P.act(sq[s][:], a2[s][:], AF.Sqrt, bias=k.ones[:, 0:1], scale=-1.0)
                yield
                P.tt(bb[s][:], sq[s][:], bx[s][:], ALU.mult)
                yield
                if prev is None:
                    init = 0.0
                else:
                    bnd = (t0 % SL == 0) if d == 0 else ((t0 + BL) % SL == 0)
                    if bnd:
                        P.ts(hinit[s][:], prev, k.carry1[:], None, ALU.mult)
                        init = hinit[s][:]
                        yield
                    else:
                        init = prev
                if d == 0:
                    P.scan(hcur[:], aa[s][:], bb[s][:], init)
                    prev = hcur[:, BL - 1:BL]
                    yield
                    P.dma(k.OT[0, rows, t0:t0 + BL], hcur[:], "lb_sh%d" % s)
                else:
                    P.dma(hf[s][:], k.OT[0, rows, t0:t0 + BL], "lb_lf%d" % s)
                    P.dma(zz[s][:], k.ZT[rows, t0:t0 + BL], "lb_lz%d" % s)
                    P.scan(hcur[:, ::-1], aa[s][:, ::-1], bb[s][:, ::-1], init)
                    prev = hcur[:, 0:1]
                    yield
                    P.tt(hf[s][:], hf[s][:], hcur[:], ALU.add, eng="pool")
                    yield
                    P.tt(gt[s][:], hf[s][:], zz[s][:], ALU.mult)
                    yield
                    P.dma(k.QK[0, 0, rows, t0:t0 + BL], gt[s][:], "lb_sg%d" % s)
                yield

    lanes = Lanes(NL)
    for j in range(16):
        lanes.push(chain(j, j % NL))
    lanes.drain()
    P.barrier()
    P.reset("rglru_C")
    st = outproj_setup(k, l, k.lru_w_out, "lc_")
    GT = [P.sbuf("lc_GT%d" % i, [128, 16, TB], BF16) for i in range(2)]
    for tb in range(NTB):
        P.dma(GT[tb % 2][:], k.QK[0, 0, :, tb * TB:(tb + 1) * TB].rearrange("(e p) t -> p e t", p=128), "lc_g%d" % (tb % 2))
        outproj_block(k, l, tb, GT[tb % 2], st, xsrc, "lc_")


def host_consts_hy(cfg, Lc, real_slots):
    NA, NB, T = cfg.NA, cfg.NB, cfg.T
    NF = 2 * T
    c = {}
    a = np.arange(NA, dtype=np.float64)
    b = np.arange(NB, dtype=np.float64)
    tha = 2 * np.pi * np.outer(a, a) / NA
    thb = 2 * np.pi * np.outer(b, b) / NB
    FAr, FAi = np.cos(tha), -np.sin(tha)
    FBr, FBi = np.cos(thb), -np.sin(thb)
    c["k_fa"] = np.concatenate([FAr, FAi], 1).astype(np.float32)
    c["k_fb"] = np.concatenate([FBr, FBi, -FBi], 1).astype(np.float32)
    c["k_fc"] = np.concatenate([FBr, -FBi, FBi, FBr], 1).astype(np.float32)
    th = 2 * np.pi * np.outer(b, a) / NF
    c["k_tw1"] = np.concatenate([np.cos(th), -np.sin(th)], 1).astype(np.float32)
    c["k_tw2"] = np.concatenate([np.cos(th.T), np.sin(th.T)], 1).astype(np.float32)
    n = np.arange(NF)
    mf = (n < Lc)
    mb = (n > NF - Lc) | (n == 0)
    pos = np.where(mf, n, np.where(mb, NF - n, 0)).astype(np.float32)
    pos[0] = 0.0
    t = (pos / np.float32(Lc - 1)).astype(np.float32)
    wv = (np.float32(2.0 * np.pi) * pos / np.float32(Lc)).astype(np.float32)
    bands = np.linspace(1e-4, 15, 16, dtype=np.float32)
    ang = (bands[:, None] * wv[None, :]).astype(np.float32)
    z = np.concatenate([t[None, :], np.cos(ang), -np.sin(ang)], 0).astype(np.float32)
    valid = (mf | mb)
    c["k_zT"] = (z * valid[None, :]).astype(np.float32)
    c["k_fmask"] = np.stack([mf, mb]).astype(np.float32)
    c["k_trow"] = (t * valid).astype(np.float32).reshape(1, NF)
    import math
    max_decay = math.log(1e-2) / 0.3
    min_decay = math.log(1e-2) / 1.5
    deltas = np.abs(np.linspace(min_decay, max_decay, E, dtype=np.float32))
    c["k_delta"] = np.ascontiguousarray(deltas.reshape(16, 128).T).astype(np.float32)
    sm = np.zeros((128, cfg.NS), np.float32)
    for s_, r in enumerate(real_slots):
        sm[:, s_] = 1.0 if r else 0.0
    c["k_smask"] = sm
    return c


def hy_fft(k, src, K1, pfx, sink):
    P, cfg = k.P, k.cfg
    NA, NB = cfg.NA, cfg.NB
    FT = k.FT
    CG, LG = 4, 16
    xin = [P.sbuf(pfx + "xin%d" % i, [K1, LG, NB], F32) for i in range(2)]
    if FT != F32:
        xrr = [k.r_xrr[i][0:K1] for i in range(2)]
        Bt = k.r_B
    else:
        xrr = xin
        Bt = [P.sbuf(pfx + "B%d" % i, [NB, 2, CG, NA], F32) for i in range(2)]
    mt = [P.sbuf(pfx + "m%d" % i, [NB, CG, NA], F32) for i in range(8)]
    psA = k.ps[0:NB, 0:2, :].rearrange("p a b -> p (a b)")
    psXr = k.ps[0:NB, 2, 0:CG * NA]
    psXi = k.ps[0:NB, 3, 0:CG * NA]
    twr = k.tw1[:, 0:NA].rearrange("p (o k) -> p o k", o=1).to_broadcast([NB, CG, NA])
    twi = k.tw1[:, NA:2 * NA].rearrange("p (o k) -> p o k", o=1).to_broadcast([NB, CG, NA])

    def grp(lg, gi, it, xs):
        s = it % 2
        m = mt[s * 4:s * 4 + 4]
        for cc in range(CG):
            P.mm(psA[:, cc * 2 * NA:(cc + 1) * 2 * NA], xs[:, gi * CG + cc, :], k.fa[0:K1, :])
        yield
        A4 = psA[:, 0:CG * 2 * NA].rearrange("p (c r k) -> p c r k", c=CG, r=2)
        Ar, Ai = A4[:, :, 0, :], A4[:, :, 1, :]
        P.tt(m[0][:], Ar, twr, ALU.mult)
        yield
        P.tt(m[1][:], Ai, twi, ALU.mult)
        yield
        P.tt(Bt[s][:, 0, :, :], m[0][:], m[1][:], ALU.subtract, eng="pool")
        P.tt(m[2][:], Ar, twi, ALU.mult)
        yield
        P.tt(m[3][:], Ai, twr, ALU.mult)
        yield
        P.tt(Bt[s][:, 1, :, :], m[2][:], m[3][:], ALU.add, eng="pool")
        yield
        Brf = Bt[s][:, 0, :, :].rearrange("p c k -> p (c k)")
        Bif = Bt[s][:, 1, :, :].rearrange("p c k -> p (c k)")
        P.mm(psXr, k.fb[:, 0:NB], Brf, start=True, stop=False)
        P.mm(psXr, k.fb[:, 2 * NB:3 * NB], Bif, start=False, stop=True)
        P.mm(psXi, k.fb[:, NB:2 * NB], Brf, start=True, stop=False)
        P.mm(psXi, k.fb[:, 0:NB], Bif, start=False, stop=True)
        yield
        yield from sink(lg, gi, lg * LG + gi * CG, psXr, psXi, s)

    lanes = Lanes(2, stagger=k.fft_stagger)
    it = 0
    for lg in range(E // LG):
        xs = xin[lg % 2]
        P.dma(xs[:], src[lg * LG:(lg + 1) * LG, 0:K1 * NB].rearrange("c (n1 n2) -> n1 c n2", n2=NB), pfx + "x%d" % (lg % 2))
        if FT != F32:
            P.copy(xrr[lg % 2][:], xs[:], eng="act")
            xs = xrr[lg % 2]
        for gi in range(LG // CG):
            lanes.push(grp(lg, gi, it, xs))
            it += 1
    lanes.drain()


import os as _os
_HYSTOP = _os.environ.get("HY_STOP", "")
_USE_F32R = _os.environ.get("HY_F32R", "1") == "1"


def hyena(k, l, xsrc):
    P, cfg = k.P, k.cfg
    NTB, T, SL, NS = cfg.NTB, cfg.T, cfg.SL, cfg.NS
    NA, NB = cfg.NA, cfg.NB
    NF = 2 * T
    CG, LG = 4, 16
    W = k.hy_w_in
    UT, X1, YT = k.QT, k.OT[0], k.OT[1]
    P.reset("hyena_A")
    bin_ = P.sbuf("ya_bin", [128, 64], F32)
    P.dma(bin_[:], k.hy_b_in.rearrange("o (j p) -> p (o j)", p=128), "ya_b", allow_slow_non_contiguous=True)
    oq = [P.sbuf("ya_o%d" % i, [128, TB], F32) for i in range(4)]
    sgq = [P.sbuf("ya_s%d" % i, [128, TB], F32) for i in range(4)]
    zb = [P.sbuf("ya_z%d" % i, [128, TB], F32) for i in range(4)]

    def epiA(tag, col, tb, pb, it):
        s = it % 4
        j = col // 128
        tsl = slice(tb * TB, (tb + 1) * TB)
        if tag == "x":
            P.act(oq[s][:], pb, AF.Identity, bias=bin_[:, j:j + 1])
            yield
            P.dma(k.PR[col:col + 128, tsl], oq[s][:], "ya_o%d" % s)
        else:
            P.act(zb[s][:], pb, AF.Identity, bias=bin_[:, j:j + 1])
            P.act(sgq[s][:], pb, AF.Sigmoid, bias=bin_[:, j:j + 1])
            yield
            P.tt(oq[s][:], zb[s][:], sgq[s][:], ALU.mult, eng="pool")
            yield
            P.dma(k.ZT[col - 3 * E:col - 3 * E + 128, tsl], oq[s][:], "ya_o%d" % s)

    cols = [(j * 128, "x") for j in range(48)] + [(3 * E + j * 128, "z") for j in range(16)]
    groups = [cols[i:i + 4] for i in range(0, 64, 4)]
    gemm_fm(k, W, groups, epiA, "ya_")
    P.barrier()
    if _HYSTOP == "B":
        return
    P.reset("hyena_B")
    BL = 512
    NBK = T // BL
    NL = 4
    cw = P.sbuf("yb_cw", [128, 48, 3], F32)
    cb = P.sbuf("yb_cb", [128, 48], F32)
    smask = P.sbuf("yb_sm", [128, NS], F32)
    for jj in range(3):
        P.dma(cw[:, :, jj], k.hy_conv_w[jj, :].rearrange("(j p) -> p j", p=128), "yb_c", allow_slow_non_contiguous=True)
    P.dma(cb[:], k.hy_conv_b.rearrange("o (j p) -> p (o j)", p=128), "yb_c", allow_slow_non_contiguous=True)
    P.dma(smask[:], k.k_smask, "yb_c")
    xr = [[P.sbuf("yb_xr%d_%d" % (i, b_), [128, BL + 2], F32) for b_ in range(3)] for i in range(NL)]
    xc = [[P.sbuf("yb_xc%d_%d" % (i, b_), [128, BL], F32) for b_ in range(3)] for i in range(NL)]
    tmpc = [P.sbuf("yb_tm%d" % i, [128, BL], F32) for i in range(NL)]
    uu = [P.sbuf("yb_u%d" % i, [128, BL], F32) for i in range(NL)]

    def bodyB(j, bi, s):
        t0 = bi * BL
        lo = max(t0 - 1, 0)
        hi_ = min(t0 + BL + 1, T)
        for b_ in range(3):
            jt = b_ * 16 + j
            xx = xr[s][b_]
            if t0 == 0:
                P.memset(xx[:, 0:1], 0.0)
            if t0 + BL == T:
                P.memset(xx[:, BL + 1:BL + 2], 0.0)
            P.dma(xx[:, lo - (t0 - 1):hi_ - (t0 - 1)], k.PR[jt * 128:(jt + 1) * 128, lo:hi_], "yb_x%d_%d" % (s, b_))
            if t0 % SL == 0 and t0 > 0:
                P.ts(xx[:, 0:1], xx[:, 0:1], k.carry1[:], None, ALU.mult, eng="pool")
            if (t0 + BL) % SL == 0 and t0 + BL < T:
                P.ts(xx[:, BL + 1:BL + 2], xx[:, BL + 1:BL + 2], k.carry1[:], None, ALU.mult, eng="pool")
        yield
        for b_ in (0, 2):
            jt = b_ * 16 + j
            xx = xr[s][b_]
            o = xc[s][b_]
            P.ts(o[:], xx[:, 0:BL], cw[:, jt, 0:1], cb[:, jt:jt + 1], ALU.mult, ALU.add)
            yield
            P.stt(o[:], xx[:, 1:BL + 1], cw[:, jt, 1:2], o[:], ALU.mult, ALU.add)
            yield
            P.stt(o[:], xx[:, 2:BL + 2], cw[:, jt, 2:3], o[:], ALU.mult, ALU.add)
            if b_ == 0:
                jt1 = 16 + j
                x1_, o1_ = xr[s][1], xc[s][1]
                P.ts(o1_[:], x1_[:, 0:BL], cw[:, jt1, 0:1], cb[:, jt1:jt1 + 1], ALU.mult, ALU.add, eng="pool")
                yield
                P.ts(tmpc[s][:], x1_[:, 1:BL + 1], cw[:, jt1, 1:2], None, ALU.mult, eng="pool")
                yield
                P.tt(o1_[:], o1_[:], tmpc[s][:], ALU.add, eng="pool")
                yield
                P.ts(tmpc[s][:], x1_[:, 2:BL + 2], cw[:, jt1, 2:3], None, ALU.mult, eng="pool")
                yield
                P.tt(o1_[:], o1_[:], tmpc[s][:], ALU.add, eng="pool")
            yield
        slot = t0 // SL
        P.stt(uu[s][:], xc[s][0][:], smask[:, slot:slot + 1], xc[s][2][:], ALU.mult, ALU.mult)
        yield
        P.dma(UT[j * 128:(j + 1) * 128, t0:t0 + BL], uu[s][:], "yb_su%d" % s)
        P.dma(X1[j * 128:(j + 1) * 128, t0:t0 + BL], xc[s][1][:], "yb_s1%d" % s)

    lanes = Lanes(NL)
    it = 0
    for j in range(16):
        for bi in range(NBK):
            lanes.push(bodyB(j, bi, it % NL))
            it += 1
    lanes.drain()
    P.barrier()
    if _HYSTOP == "C":
        return
    P.reset("hyena_C")
    w1 = P.sbuf("yc_w1", [33, 64], F32)
    w2 = P.sbuf("yc_w2", [64, 2, 64], F32)
    wout = P.sbuf("yc_wo", [64, 2 * E], F32)
    fq = P.sbuf("yc_fq", [64, 1], F32)
    fbr = P.sbuf("yc_fbr", [64, 3], F32)
    fbs = P.sbuf("yc_fbs", [64, 3], F32)
    ndel = P.sbuf("yc_nd", [128, 16], F32)
    P.dma(w1[:], k.hy_f_w1, "yc_c")
    for jj in range(2):
        P.dma(w2[:, jj, :], k.hy_f_w2[jj, :, :], "yc_c")
        P.dma(fbr[:, 1 + jj:2 + jj], k.hy_f_b2[jj:jj + 1, :].rearrange("o p -> p o"), "yc_c", allow_slow_non_contiguous=True)
    P.dma(wout[:], k.hy_f_wout, "yc_c")
    P.dma(fq[:], k.hy_f_freq.rearrange("o p -> p o"), "yc_c", allow_slow_non_contiguous=True)
    P.dma(fbr[:, 0:1], k.hy_f_b1.rearrange("o p -> p o"), "yc_c", allow_slow_non_contiguous=True)
    P.dma(ndel[:], k.k_delta, "yc_c")
    P.ts(fq[:], fq[:], float(1.0 / (2.0 * np.pi)), None, ALU.mult)
    P.ts(fbs[:], fbr[:], fq[:], None, ALU.mult)
    P.ts(ndel[:], ndel[:], -1.0, None, ALU.mult, eng="pool")
    tl = lambda nm, sh, dt=F32, n=2: [P.sbuf("yc_%s%d" % (nm, i), sh, dt) for i in range(n)]
    zt = tl("zt", [33, TB])
    mrow = tl("mr", [64, 2, TB])
    trow = tl("tr", [128, TB])
    uf = tl("uf", [64, TB], F32, 3)
    ui = tl("ui", [64, TB], I32, 3)
    aa = tl("aa", [64, TB], F32, 3)
    k.FT = mybir.dt.float32r if (NA == 128 and _USE_F32R) else F32
    afw = tl("afw", [64, TB])
    abw = tl("abw", [64, TB])
    woutr = wout
    win = tl("win", [128, TB])
    tp = tl("tp", [128, TB])
    SC2PI = float(2.0 * np.pi * (1.0 - 1e-6))
    it = 0
    for nb in range(NF // TB):
        s = nb % 2
        nsl = slice(nb * TB, (nb + 1) * TB)
        P.dma(zt[s][:], k.k_zT[:, nsl], "yc_z%d" % s)
        P.dma(mrow[s][:, 0, :], bc_rows(k.k_fmask[0:1, nsl], 64), "yc_m%d" % s)
        P.dma(mrow[s][:, 1, :], bc_rows(k.k_fmask[1:2, nsl], 64), "yc_m%d" % s)
        P.dma(trow[s][:], bc_rows(k.k_trow[0:1, nsl], 128), "yc_t%d" % s)
        a_in = zt[s][:]
        for ly in range(3):
            pb = k.ps[0:64, 4 + (ly % 2), :]
            lw = w1[:] if ly == 0 else w2[:, ly - 1, :]
            P.mm(pb, lw, a_in)
            P.ts(uf[ly][:], pb, fq[:], fbs[:, ly:ly + 1], ALU.mult, ALU.add)
            P.copy(ui[ly][:], uf[ly][:])
            P.tt(uf[ly][:], uf[ly][:], ui[ly][:], ALU.subtract)
            P.act(aa[ly][:], uf[ly][:], AF.Sin, scale=SC2PI)
            a_in = aa[ly][:]
        P.tt(afw[s][:], aa[2][:], mrow[s][:, 0, :], ALU.mult, eng="pool")
        P.tt(abw[s][:], aa[2][:], mrow[s][:, 1, :], ALU.mult, eng="pool")
        for j in range(16):
            q = it % 2
            pb = k.ps[:, it % 4, :]
            P.mm(pb, woutr[:, j * 128:(j + 1) * 128], afw[s][:], start=True, stop=False)
            P.mm(pb, woutr[:, E + j * 128:E + (j + 1) * 128], abw[s][:], start=False, stop=True)
            P.act(win[q][:], trow[s][:], AF.Exp, scale=ndel[:, j:j + 1])
            P.tt(tp[q][:], pb, win[q][:], ALU.mult)
            P.dma(k.TAPS[j * 128:(j + 1) * 128, nsl], tp[q][:], "yc_s%d" % q)
            it += 1
    P.barrier()
    if _HYSTOP == "D":
        return
    P.reset("hyena_D")
    FT = k.FT
    k.tw1 = P.sbuf("yd_tw1", [NB, 2 * NA], F32)
    k.tw2 = P.sbuf("yd_tw2", [NA, 2 * NB], F32)
    fa0 = P.sbuf("yd_fa0", [NA, 2 * NA], F32)
    fb0 = P.sbuf("yd_fb0", [NB, 3 * NB], F32)
    fc0 = P.sbuf("yd_fc0", [NB, 4 * NB], F32)
    P.dma(fa0[:], k.k_fa, "yd_c")
    P.dma(fb0[:], k.k_fb, "yd_c")
    P.dma(fc0[:], k.k_fc, "yd_c")
    if FT != F32:
        k.fa = P.sbuf_r("yd_fa", [NA, 2 * NA], FT)
        k.fb = P.sbuf_r("yd_fb", [NB, 3 * NB], FT)
        k.fc = P.sbuf_r("yd_fc", [NB, 4 * NB], FT)
        k.r_xrr = [P.sbuf_r("yd_xrr%d" % i, [NA, LG, NB], FT) for i in range(2)]
        k.r_B = [P.sbuf_r("yd_rB%d" % i, [NB, 2, CG, NA], FT) for i in range(2)]
        k.r_Y = [P.sbuf_r("yd_rY%d" % i, [NB, 2, CG, NA], FT) for i in range(2)]
        k.r_D = [P.sbuf_r("yd_rD%d" % i, [NA, 2, CG, NB], FT) for i in range(2)]
        P.copy(k.fa[:], fa0[:])
        P.copy(k.fb[:], fb0[:])
        P.copy(k.fc[:], fc0[:], eng="pool")
    else:
        k.fa, k.fb, k.fc = fa0, fb0, fc0
    P.dma(k.tw1[:], k.k_tw1, "yd_c")
    P.dma(k.tw2[:], k.k_tw2, "yd_c")
    wm = P.bump
    rwm = P.rbump
    hsb = [P.sbuf("yd_h%d" % i, [NB, 2, CG, NA], F32) for i in range(2)]
    cnt = {"i": 0}

    def sink1(lg, gi, c0, psXr, psXi, s):
        P.copy(hsb[s][:, 0, :, :], psXr.rearrange("p (c k) -> p c k", c=CG), eng="act")
        yield
        P.copy(hsb[s][:, 1, :, :], psXi.rearrange("p (c k) -> p c k", c=CG), eng="act")
        yield
        for ri in range(2):
            P.dma(k.HS[ri, :, c0:c0 + CG, :], hsb[s][:, ri, :, :], "yd_sh%d" % s)

    k.fft_stagger = 5
    hy_fft(k, k.TAPS, NA, "yd1_", sink1)
    P.barrier()
    P.bump = wm
    P.rbump = rwm
    P.nphase += 1
    P.cur_phase = "%02d_hyena_D2" % P.nphase
    Hh = [P.sbuf("yd_H%d" % i, [NB, 2, CG, NA], F32) for i in range(2)]
    if FT != F32:
        Yt = k.r_Y
        Dt = k.r_D
    else:
        Yt = [P.sbuf("yd_Y%d" % i, [NB, 2, CG, NA], F32) for i in range(2)]
        Dt = [P.sbuf("yd_D%d" % i, [NA, 2, CG, NB], F32) for i in range(2)]
    MY = NA if FT != F32 else NA // 2
    m2 = [P.sbuf("yd_mm%d" % i, [128, CG * 128], F32) for i in range(8)]
    yout = [P.sbuf("yd_yo%d" % i, [NA // 2, LG, NB], F32) for i in range(2)]
    psD = k.ps[0:NA, 4:6, :].rearrange("p a b -> p (a b)")
    psY = k.psb[0:MY, 0, :].bitcast(F32)[:, 0:CG * NB]
    t2r = k.tw2[:, 0:NB].rearrange("p (o n) -> p o n", o=1).to_broadcast([NA, CG, NB])
    t2i = k.tw2[:, NB:2 * NB].rearrange("p (o n) -> p o n", o=1).to_broadcast([NA, CG, NB])
    cnt["i"] = 0

    def sink2(lg, gi, c0, psXr, psXi, s):
        for ri in range(2):
            P.dma(Hh[s][:, ri, :, :], k.HS[ri, :, c0:c0 + CG, :], "yd_lh%d" % s)
        Xr = psXr.rearrange("p (c k) -> p c k", c=CG)
        Xi = psXi.rearrange("p (c k) -> p c k", c=CG)
        mv = [m2[s * 4 + i][0:NB, 0:CG * NA].rearrange("p (c k) -> p c k", c=CG) for i in range(4)]
        P.tt(mv[0], Xr, Hh[s][:, 0, :, :], ALU.mult)
        yield
        P.tt(mv[1], Xi, Hh[s][:, 1, :, :], ALU.mult)
        yield
        P.tt(Yt[s][:, 0, :, :], mv[0], mv[1], ALU.subtract, eng="pool")
        P.tt(mv[2], Xr, Hh[s][:, 1, :, :], ALU.mult)
        yield
        P.tt(mv[3], Xi, Hh[s][:, 0, :, :], ALU.mult)
        yield
        P.tt(Yt[s][:, 1, :, :], mv[2], mv[3], ALU.add, eng="pool")
        yield
        for cc in range(CG):
            po = psD[:, cc * 2 * NB:(cc + 1) * 2 * NB]
            P.mm(po, Yt[s][:, 0, cc, :], k.fc[:, 0:2 * NB], start=True, stop=False)
            P.mm(po, Yt[s][:, 1, cc, :], k.fc[:, 2 * NB:4 * NB], start=False, stop=True)
        yield
        D4 = psD[:, 0:CG * 2 * NB].rearrange("p (c r n) -> p c r n", c=CG, r=2)
        Dr, Di = D4[:, :, 0, :], D4[:, :, 1, :]
        nv = [m2[s * 4 + i][0:NA, 0:CG * NB].rearrange("p (c n) -> p c n", c=CG) for i in range(4)]
        P.tt(nv[0], Dr, t2r, ALU.mult)
        yield
        P.tt(nv[1], Di, t2i, ALU.mult)
        yield
        P.tt(Dt[s][:, 0, :, :], nv[0], nv[1], ALU.subtract, eng="pool")
        P.tt(nv[2], Dr, t2i, ALU.mult)
        yield
        P.tt(nv[3], Di, t2r, ALU.mult)
        yield
        P.tt(Dt[s][:, 1, :, :], nv[2], nv[3], ALU.add, eng="pool")
        yield
        P.mm(psY, k.fa[:, 0:MY], Dt[s][:, 0, :, :].rearrange("p c n -> p (c n)"), start=True, stop=False)
        P.mm(psY, k.fa[:, NA:NA + MY], Dt[s][:, 1, :, :].rearrange("p c n -> p (c n)"), start=False, stop=True)
        yield
        yo = yout[lg % 2]
        P.act(yo[:, gi * CG:(gi + 1) * CG, :], psY[0:NA // 2, :].rearrange("p (c n) -> p c n", c=CG), AF.Copy, scale=float(1.0 / NF))
        if gi == LG // CG - 1:
            yield
            P.dma(YT[lg * LG:(lg + 1) * LG, :].rearrange("c (n1 n2) -> n1 c n2", n2=NB), yo[:], "yd_sy%d" % (lg % 2))

    k.fft_stagger = 9
    hy_fft(k, UT, NA // 2, "yd2_", sink2)
    P.barrier()
    if _HYSTOP == "E":
        return
    P.reset("hyena_E")
    skp = P.sbuf("ye_sk", [128, 16], F32)
    P.dma(skp[:], k.hy_skip.rearrange("o (j p) -> p (o j)", p=128), "ye_c", allow_slow_non_contiguous=True)
    tl = lambda nm, sh, dt=F32, n=2: [P.sbuf("ye_%s%d" % (nm, i), sh, dt) for i in range(n)]
    NL = 4
    tl = lambda nm, sh, dt=F32, n=NL: [P.sbuf("ye_%s%d" % (nm, i), sh, dt) for i in range(n)]
    yy, u2, x1t, zz, tt_ = (tl(n, [128, BL]) for n in ("yy", "u2", "x1", "zz", "tt"))
    gt = tl("gt", [128, BL], BF16)

    def bodyE(j, bi, s):
        rows = slice(j * 128, (j + 1) * 128)
        tsl = slice(bi * BL, (bi + 1) * BL)
        P.dma(yy[s][:], YT[rows, tsl], "ye_y%d" % s)
        P.dma(u2[s][:], UT[rows, tsl], "ye_u%d" % s)
        P.dma(x1t[s][:], X1[rows, tsl], "ye_x%d" % s)
        P.dma(zz[s][:], k.ZT[rows, tsl], "ye_z%d" % s)
        yield
        P.stt(tt_[s][:], u2[s][:], skp[:, j:j + 1], yy[s][:], ALU.mult, ALU.add)
        yield
        P.tt(tt_[s][:], tt_[s][:], x1t[s][:], ALU.mult, eng="pool")
        yield
        P.tt(gt[s][:], tt_[s][:], zz[s][:], ALU.mult, eng=("dve" if (j + bi) % 2 == 0 else "pool"))
        yield
        P.dma(k.QK[0, 0, rows, tsl], gt[s][:], "ye_g%d" % s)

    lanes = Lanes(NL)
    it = 0
    for j in range(16):
        for bi in range(NBK):
            lanes.push(bodyE(j, bi, it % NL))
            it += 1
    lanes.drain()
    P.barrier()
    if _HYSTOP == "F":
        return
    P.reset("hyena_F")
    st = outproj_setup(k, l, k.hy_w_out, "yf_")
    GT = [P.sbuf("yf_GT%d" % i, [128, 16, TB], BF16) for i in range(2)]
    for tb in range(NTB):
        P.dma(GT[tb % 2][:], k.QK[0, 0, :, tb * TB:(tb + 1) * TB].rearrange("(e p) t -> p e t", p=128), "yf_g%d" % (tb % 2))
        outproj_block(k, l, tb, GT[tb % 2], st, xsrc, "yf_")


_CACHE = {}


def kernel(**inputs):
    cfg = Cfg(NS=4, SL=2048, kinds=(0, 1, 2, 3))
    if "nc" not in _CACHE:
        _CACHE["nc"] = build(cfg)[0]
    nc = _CACHE["nc"]
    w = {n: np.asarray(v) for n, v in inputs.items() if n not in ("x_prompt", "x_sample", "c_prompt", "c_sample")}
    xp = np.asarray(inputs["x_prompt"], np.float32)
    xs = np.asarray(inputs["x_sample"], np.float32)
    cp = np.asarray(inputs["c_prompt"], np.float32)
    cs = np.asarray(inputs["c_sample"], np.float32)
    SL, T = cfg.SL, cfg.T
    in_maps = []
    for i in range(4):
        in_maps.append(core_inputs(cfg, [xp[i, s * SL:(s + 1) * SL] for s in range(4)], [cp[i]] * 4, 1.0, w,
                                   np.arange(T), T))
    for j in range(4):
        a, b = 2 * j, 2 * j + 1
        in_maps.append(core_inputs(cfg, [xs[a], None, xs[b], None], [cs[a], None, cs[b], None], 0.0, w,
                                   np.arange(T) % SL, SL))
    res = run_bass_kernel_spmd(nc, in_maps, core_ids=list(range(8)))
    yp = np.stack([np.asarray(res.results[i]["yout"], np.float32) for i in range(4)], 0)
    ys = np.zeros(xs.shape, np.float32)
    for j in range(4):
        yo = np.asarray(res.results[4 + j]["yout"], np.float32)
        ys[2 * j] = yo[0:SL]
        ys[2 * j + 1] = yo[2 * SL:3 * SL]
    return (yp, ys)
```

```python
import numpy as np
from contextlib import ExitStack
import concourse.bass as bass
import concourse.mybir as mybir
from concourse.bass_utils import run_bass_kernel_spmd

F32 = mybir.dt.float32
BF16 = mybir.dt.bfloat16
I32 = mybir.dt.int32
ALU = mybir.AluOpType
AF = mybir.ActivationFunctionType

D = 1024
E = 2048
CH = 64
TB = 512
EPS = 1e-6

def _isz(dt):
    return mybir.dt.size(dt)


def _region(ap):
    name = ap.tensor.name
    pat = ap.ap
    off = int(ap.offset)
    space = str(ap.space)
    z = _isz(ap.dtype)
    if space in ("SB", "PSUM"):
        pstride = pat[0][0]
        p_lo = off // pstride
        p_hi = p_lo + pat[0][1]
        base = off % pstride
        lo = base
        hi = base
        for st, cn in pat[1:]:
            ext = st * (cn - 1)
            if ext < 0:
                lo += ext
            else:
                hi += ext
        return (name, p_lo, p_hi, lo * z, (hi + 1) * z)
    pitch = int(ap.tensor.shape[-1])
    r_lo = r_hi = off // pitch
    c_lo = c_hi = off % pitch
    for st, cn in pat:
        ext = st * (cn - 1)
        if abs(st) >= pitch and st % pitch == 0:
            e = ext // pitch
            if e < 0:
                r_lo += e
            else:
                r_hi += e
        else:
            if ext < 0:
                c_lo += ext
            else:
                c_hi += ext
    if c_lo < 0 or c_hi >= pitch:
        r_lo += c_lo // pitch
        r_hi += c_hi // pitch
        c_lo, c_hi = 0, pitch - 1
    return (name, r_lo, r_hi + 1, c_lo, c_hi + 1)


def _overlap(a, b):
    return a[1] < b[2] and b[1] < a[2] and a[3] < b[4] and b[3] < a[4]


def _contains(a, b):
    return a[1] <= b[1] and b[2] <= a[2] and a[3] <= b[3] and b[4] <= a[4]


class Op:
    __slots__ = ("eng", "fn", "deps", "stream", "signal", "val", "sem", "dma_snap", "phase")

    def __init__(self, eng, fn, deps, stream):
        self.eng = eng
        self.fn = fn
        self.deps = deps
        self.stream = stream
        self.signal = False
        self.val = 0
        self.sem = None
        self.dma_snap = None


class Prog:
    ENGS = ("pe", "dve", "act", "pool", "sp")

    def __init__(self, nc):
        self.nc = nc
        self.ops = []
        self.acc = {}
        self.stack = ExitStack()
        self.n = 0
        self.barrier_at = []

    ARENA = 150 * 1024
    RSV = 40 * 1024

    def sbuf_r(self, name, shape, dt):
        t = self.stack.enter_context(self.nc.sbuf_tensor(name, list(shape), dt))
        return t[tuple(slice(None) for _ in shape)]

    def sbuf(self, name, shape, dt):
        if not hasattr(self, "arena"):
            self.arena = self.stack.enter_context(self.nc.sbuf_tensor("arena", [128, self.ARENA], mybir.dt.uint8))
            self.bump = 0
            self.water = 0
            self.rbump = self.ARENA
        n = 1
        for d in shape[1:]:
            n *= d
        size = (n * _isz(dt) + 63) // 64 * 64
        off = self.bump
        self.bump += size
        assert self.bump <= self.ARENA, ("SBUF overflow", name, self.bump)
        v = self.arena[0:shape[0], off:off + n * _isz(dt)].bitcast(dt)
        if len(shape) == 3:
            v = v.rearrange("p (a b) -> p a b", a=shape[1])
        elif len(shape) == 4:
            v = v.rearrange("p (a b c) -> p a b c", a=shape[1], b=shape[2])
        return v

    def mark(self):
        self.water = self.bump

    def reset(self, label=None):
        self.bump = self.water
        self.rbump = self.ARENA
        self.nphase = getattr(self, "nphase", 0) + 1
        self.cur_phase = "%02d_%s" % (self.nphase, label or "")

    def psum(self, name, shape, dt=F32):
        return self.stack.enter_context(self.nc.psum_tensor(name, list(shape), dt))

    def dram(self, name, shape, dt, kind="Internal"):
        return self.nc.dram_tensor(name, list(shape), dt, kind=kind).ap()

    def _deps(self, reg, is_write, idx, eng, is_dma):
        lst = self.acc.setdefault(reg[0], [])
        deps = []
        keep = []
        for (r, j, w, e, d) in lst:
            if _overlap(r, reg):
                if is_write or w:
                    same = (e == eng) and not d and not is_dma
                    if same:
                        if w and not is_write and eng != "pe":
                            deps.append(j)
                    else:
                        deps.append(j)
                if is_write and _contains(reg, r):
                    continue
                if (not is_write) and (not w) and e == eng and not d and not is_dma and r == reg:
                    continue
            keep.append((r, j, w, e, d))
        keep.append((reg, idx, is_write, eng, is_dma))
        self.acc[reg[0]] = keep
        return deps

    def add(self, eng, fn, reads=(), writes=(), stream=None, extra_deps=()):
        idx = len(self.ops)
        is_dma = stream is not None
        deps = set(extra_deps)
        for ap in reads:
            if ap is None:
                continue
            deps.update(self._deps(_region(ap), False, idx, eng, is_dma))
        for ap in writes:
            if ap is None:
                continue
            deps.update(self._deps(_region(ap), True, idx, eng, is_dma))
        deps.discard(idx)
        self.ops.append(Op(eng, fn, deps, stream))
        self.ops[-1].phase = getattr(self, "cur_phase", "00")
        return idx

    def barrier(self):
        self.barrier_at.append(len(self.ops))
        self.acc = {}

    def dma(self, out, in_, stream, q="sp", **kw):
        return self.add(q, lambda e: e.dma_start(out=out, in_=in_, **kw), [in_], [out], stream=stream)

    def mm(self, out, lhsT, rhs, start=True, stop=True, extra_reads=()):
        return self.add("pe", lambda e: e.matmul(out, lhsT, rhs, start=start, stop=stop),
                        [lhsT, rhs] + list(extra_reads) + ([] if start else [out]), [out])

    def transpose(self, out, in_, ident):
        return self.add("pe", lambda e: e.transpose(out, in_, ident), [in_, ident], [out])

    def act(self, out, in_, func, bias=0.0, scale=1.0, accum_out=None, eng="act"):
        rd = [in_]
        if not isinstance(bias, (int, float)):
            rd.append(bias)
        if not isinstance(scale, (int, float)):
            rd.append(scale)
        wr = [out]
        kw = {}
        if accum_out is not None:
            wr.append(accum_out)
            kw["accum_out"] = accum_out
        return self.add(eng, lambda e: e.activation(out, in_, func, bias=bias, scale=scale, **kw), rd, wr)

    def tt(self, out, a, b, op, eng="dve"):
        return self.add(eng, lambda e: e.tensor_tensor(out, a, b, op), [a, b], [out])

    def ts(self, out, a, s1, s2, op0, op1=None, eng="dve", accum_out=None):
        rd = [a]
        if not isinstance(s1, (int, float)):
            rd.append(s1)
        if s2 is not None and not isinstance(s2, (int, float)):
            rd.append(s2)
        wr = [out]
        kw = {}
        if accum_out is not None:
            wr.append(accum_out)
            kw["accum_out"] = accum_out
        if op1 is None:
            return self.add(eng, lambda e: e.tensor_scalar(out, a, s1, None, op0, **kw), rd, wr)
        return self.add(eng, lambda e: e.tensor_scalar(out, a, s1, s2, op0, op1, **kw), rd, wr)

    def stt(self, out, a, s, b, op0, op1, eng="dve"):
        rd = [a, b]
        if not isinstance(s, (int, float)):
            rd.append(s)
        return self.add("dve", lambda e: e.scalar_tensor_tensor(out, a, s, b, op0, op1), rd, [out])

    def copy(self, out, in_, eng="dve"):
        if eng == "act":
            return self.add(eng, lambda e: e.copy(out, in_), [in_], [out])
        return self.add(eng, lambda e: e.tensor_copy(out, in_), [in_], [out])

    def memset(self, out, v, eng="pool"):
        return self.add(eng, lambda e: e.memset(out, v), [], [out])

    def scan(self, out, d0, d1, init, op0=ALU.mult, op1=ALU.add):
        rd = [d0, d1]
        if not isinstance(init, (int, float)):
            rd.append(init)
        return self.add("dve", lambda e: e.tensor_tensor_scan(out, d0, d1, init, op0, op1), rd, [out])

    def finalize(self, final_streams_wait=True):
        nc = self.nc
        engs = {"pe": nc.tensor, "dve": nc.vector, "act": nc.scalar, "pool": nc.gpsimd, "sp": nc.sync}
        ops = self.ops
        for op in ops:
            for j in op.deps:
                if ops[j].stream is None:
                    ops[j].signal = True
        last_before = []
        for b in self.barrier_at:
            lb = {}
            for i in range(b - 1, -1, -1):
                o = ops[i]
                if o.stream is None and o.eng not in lb:
                    lb[o.eng] = i
                    if len(lb) == 5:
                        break
            for i in lb.values():
                ops[i].signal = True
            last_before.append(lb)
        sems = {}

        def getsem(key):
            if key not in sems:
                sems[key] = self.stack.enter_context(nc.semaphore("s_" + key))
            return sems[key]

        cnt = {}
        stream_hist = {}
        active = {}
        freep = []
        nphys = 0
        bset = set(self.barrier_at)
        for i, op in enumerate(ops):
            if i in bset:
                freep.extend(sorted(active.values()))
                active = {}
            if op.stream is not None:
                if op.stream not in active:
                    if freep:
                        active[op.stream] = freep.pop(0)
                    else:
                        active[op.stream] = nphys
                        nphys += 1
                k = "d%d" % active[op.stream]
                cnt[k] = cnt.get(k, 0) + 16
                op.sem = k
                op.val = cnt[k]
            elif op.signal:
                k = "e_" + op.eng
                cnt[k] = cnt.get(k, 0) + 1
                op.sem = k
                op.val = cnt[k]
        waited = {e: {} for e in self.ENGS}
        self.iname = {}
        stream_cnt = {}
        bi = 0
        barrier_pending = {e: None for e in self.ENGS}
        for i, op in enumerate(ops):
            while bi < len(self.barrier_at) and self.barrier_at[bi] <= i:
                snap = {}
                for e, j in last_before[bi].items():
                    snap[ops[j].sem] = ops[j].val
                for k, v in stream_cnt.items():
                    snap[k] = v
                for e in self.ENGS:
                    barrier_pending[e] = dict(snap) if barrier_pending[e] is None else {**barrier_pending[e], **snap}
                bi += 1
            eng = engs[op.eng]
            need = {}
            if barrier_pending[op.eng] is not None:
                need.update(barrier_pending[op.eng])
                barrier_pending[op.eng] = None
            for j in op.deps:
                d = ops[j]
                if d.stream is not None:
                    v = stream_cnt[d.sem]
                else:
                    v = d.val
                if need.get(d.sem, 0) < v:
                    need[d.sem] = v
            w = waited[op.eng]
            for k, v in need.items():
                if k == "e_" + op.eng and op.eng == "pe":
                    continue
                if w.get(k, 0) < v:
                    eng.wait_ge(getsem(k), v)
                    w[k] = v
            ins = op.fn(eng)
            try:
                self.iname[ins.ins.name] = op.phase
            except Exception:
                pass
            if op.stream is not None:
                ins.then_inc(getsem(op.sem), 16)
                stream_cnt[op.sem] = op.val
            elif op.signal:
                ins.then_inc(getsem(op.sem), 1)
        if final_streams_wait:
            for k, v in stream_cnt.items():
                if waited["sp"].get(k, 0) < v:
                    nc.sync.wait_ge(getsem(k), v)
        self.nsems = len(sems)
        self.counts = cnt


class Lanes:
    def __init__(self, width, stagger=0):
        self.width = width
        self.stagger = stagger
        self.active = []

    def step(self):
        for g in list(self.active):
            try:
                next(g)
            except StopIteration:
                self.active.remove(g)

    def push(self, gen):
        if gen is None:
            return
        while len(self.active) >= self.width:
            self.step()
        self.active.append(gen)
        for _ in range(self.stagger):
            self.step()

    def drain(self):
        while self.active:
            self.step()


class Cfg:
    def __init__(self, NS=4, SL=2048, kinds=(0, 1, 2, 3)):
        self.NS = NS
        self.SL = SL
        self.T = NS * SL
        self.NCH = self.T // CH
        self.NTB = self.T // TB
        self.kinds = tuple(kinds)
        self.depth = len(kinds)
        self.NB = 128
        self.NA = (2 * self.T) // 128


class K:
    pass


def bc_rows(ap_row, n):
    return ap_row.to_broadcast([n, ap_row.shape[-1]])


def build(cfg):
    nc = bass.Bass("TRN2", target_bir_lowering=False)
    P = Prog(nc)
    k = K()
    k.P = P
    k.cfg = cfg
    T, NS, SL, NCH, NTB = cfg.T, cfg.NS, cfg.SL, cfg.NCH, cfg.NTB
    dp = cfg.depth
    din = lambda n, s, dt=F32: P.dram(n, s, dt, "ExternalInput")
    k.xin = din("xin", [T, D])
    k.cin = din("cin", [NS, D])
    k.ada_w = din("ada_w", [dp, D, 3 * D])
    k.ada_b = din("ada_b", [dp, 3 * D])
    k.norm_g = din("norm_g", [dp, D])
    k.final_g = din("final_g", [1, D])
    k.hg_lb = din("hg_lb", [5, E])
    k.hg_w_in = din("hg_w_in", [D, 5 * E])
    k.hg_norm_g = din("hg_norm_g", [1, 128])
    k.hg_w_out = din("hg_w_out", [E, D])
    if 2 in cfg.kinds:
        k.rt_w_in = din("rt_w_in", [D, 6144])
        k.rt_gn_g = din("rt_gn_g", [1, E])
        k.rt_w_out = din("rt_w_out", [E, D])
        k.k_cos = din("k_cos", [128, T])
        k.k_sin = din("k_sin", [128, T])
        k.k_dtab = din("k_dtab", [128, 16 * 64])
        k.k_rtsc = din("k_rtsc", [2, 3, 1024, NCH])
    if 3 in cfg.kinds:
        k.lru_w_in = din("lru_w_in", [D, 2 * E])
        k.lru_conv_w = din("lru_conv_w", [4, E])
        k.lru_conv_b = din("lru_conv_b", [1, E])
        k.lru_gate_w = din("lru_gate_w", [2, 2, 16, 128, 128])
        k.lru_gate_b = din("lru_gate_b", [2, 2, E])
        k.lru_lambda = din("lru_lambda", [2, E])
        k.lru_w_out = din("lru_w_out", [E, D])
    if 1 in cfg.kinds:
        NA, NB = cfg.NA, cfg.NB
        NF = 2 * T
        k.hy_w_in = din("hy_w_in", [D, 4 * E])
        k.hy_b_in = din("hy_b_in", [1, 4 * E])
        k.hy_conv_w = din("hy_conv_w", [3, 3 * E])
        k.hy_conv_b = din("hy_conv_b", [1, 3 * E])
        k.hy_f_w1 = din("hy_f_w1", [33, 64])
        k.hy_f_b1 = din("hy_f_b1", [1, 64])
        k.hy_f_w2 = din("hy_f_w2", [2, 64, 64])
        k.hy_f_b2 = din("hy_f_b2", [2, 64])
        k.hy_f_wout = din("hy_f_wout", [64, 2 * E])
        k.hy_f_freq = din("hy_f_freq", [1, 64])
        k.hy_skip = din("hy_skip", [1, E])
        k.hy_w_out = din("hy_w_out", [E, D])
        k.k_fa = din("k_fa", [NA, 2 * NA])
        k.k_fb = din("k_fb", [NB, 3 * NB])
        k.k_fc = din("k_fc", [NB, 4 * NB])
        k.k_tw1 = din("k_tw1", [NB, 2 * NA])
        k.k_tw2 = din("k_tw2", [NA, 2 * NB])
        k.k_zT = din("k_zT", [33, NF])
        k.k_fmask = din("k_fmask", [2, NF])
        k.k_trow = din("k_trow", [1, NF])
        k.k_delta = din("k_delta", [128, 16])
        k.k_smask = din("k_smask", [128, NS])
        k.PR = P.dram("PR", [3 * E, T], F32)
        k.TAPS = P.dram("TAPS", [E, NF], F32)
        k.HS = P.dram("HS", [2, NB, E, NA], F32)
    k.k_carry1 = din("k_carry1", [128, 1])
    k.k_ident = din("k_ident", [128, 128])
    k.k_tri = din("k_tri", [64, 128])
    k.k_rmask = din("k_rmask", [128, TB])
    k.k_carry = din("k_carry", [128, 2 * NCH])
    k.yout = P.dram("yout", [T, D], F32, "ExternalOutput")
    k.X = P.dram("X", [T, D], F32)
    k.MOD = P.dram("MOD", [dp, NS, 3 * D], F32)
    k.HT = P.dram("HT", [D, T], BF16)
    k.QT = P.dram("QT", [E, T], F32)
    k.ZT = P.dram("ZT", [E, T], F32)
    k.QK = P.dram("QK", [2, 2, E, T], BF16)
    k.SC = P.dram("SC", [2, 3, E, NCH], F32)
    k.V = P.dram("V", [T, E], BF16)
    k.OT = P.dram("OT", [2, E, T], F32)
    k.ident = P.sbuf("ident", [128, 128], F32)
    k.identb = P.sbuf("identb", [128, 128], BF16)
    k.ones = P.sbuf("ones", [128, 128], F32)
    k.tri = P.sbuf("tri", [64, 128], F32)
    k.rmask = P.sbuf("rmask", [128, TB], F32)
    k.carry = P.sbuf("carry", [128, 2 * NCH], F32)
    k.epsT = P.sbuf("epsT", [128, 1], F32)
    P.dma(k.ident[:], k.k_ident, "c0")
    P.dma(k.tri[:], k.k_tri, "c1")
    P.dma(k.rmask[:], k.k_rmask, "c2")
    P.dma(k.carry[:], k.k_carry, "c3")
    k.carry1 = P.sbuf("carry1", [128, 1], F32)
    P.dma(k.carry1[:], k.k_carry1, "c4")
    P.copy(k.identb[:], k.ident[:])
    P.memset(k.ones[:], 1.0)
    P.memset(k.epsT[:], EPS)
    k.ps = P.psum("ps", [128, 6, 512], F32)
    k.psb = P.psum("psb", [128, 2, 1024], BF16)

    P.mark()
    phase_mod(k)
    P.barrier()
    for l, kind in enumerate(cfg.kinds):
        xsrc = k.xin if l == 0 else k.X
        phase_norm(k, l, xsrc)
        P.barrier()
        if kind == 0:
            hgrn2(k, l, xsrc)
        elif kind == 1:
            hyena(k, l, xsrc)
        elif kind == 2:
            retention(k, l, xsrc)
        elif kind == 3:
            rglru(k, l, xsrc)
        else:
            raise NotImplementedError
        P.barrier()
    phase_final(k, k.xin if dp == 0 else k.X)
    P.finalize()
    return nc, P


def phase_mod(k):
    P, cfg = k.P, k.cfg
    P.reset("phase_mod_")
    NS = cfg.NS
    cT = P.sbuf("m_cT", [128, 8, NS], F32)
    sg = P.sbuf("m_sg", [128, 8, NS], F32)
    csT = P.sbuf("m_csT", [128, 8, NS], F32)
    for kt in range(8):
        P.dma(cT[:, kt, :], k.cin[:, kt * 128:(kt + 1) * 128].rearrange("s p -> p s"), "m_c",
              allow_slow_non_contiguous=True)
    P.act(sg[:], cT[:], AF.Sigmoid)
    P.tt(csT[:], cT[:], sg[:], ALU.mult)
    w = [P.sbuf("m_w%d" % i, [128, 8, 512], F32) for i in range(2)]
    bb = [P.sbuf("m_b%d" % i, [NS, 512], F32) for i in range(2)]
    ob = [P.sbuf("m_o%d" % i, [NS, 512], F32) for i in range(2)]
    it = 0
    for l in range(cfg.depth):
        for cb in range(6):
            s = it % 2
            P.dma(w[s][:], k.ada_w[l, :, cb * 512:(cb + 1) * 512].rearrange("(kt p) n -> p kt n", p=128), "m_w%d" % s)
            P.dma(bb[s][:], bc_rows(k.ada_b[l:l + 1, cb * 512:(cb + 1) * 512], NS), "m_b%d" % s)
            pb = k.ps[0:NS, it % 2, :]
            for kt in range(8):
                P.mm(pb, csT[:, kt, :], w[s][:, kt, :], start=(kt == 0), stop=(kt == 7))
            P.tt(ob[s][:], pb, bb[s][:], ALU.add)
            P.dma(k.MOD[l, :, cb * 512:(cb + 1) * 512], ob[s][:], "m_o%d" % s)
            it += 1


def rms_rstd(P, k, xt, junk, ss, rstd):
    P.act(junk, xt, AF.Square, accum_out=ss)
    P.act(rstd, ss, AF.Sqrt, bias=k.epsT[:], scale=1.0 / D)
    P.add("dve", lambda e: e.reciprocal(rstd, rstd), [rstd], [rstd])


def phase_norm(k, l, xsrc):
    P, cfg = k.P, k.cfg
    P.reset("phase_norm_")
    T, NS, SL = cfg.T, cfg.NS, cfg.SL
    g_bc = P.sbuf("n_g", [128, D], F32)
    A_bc = P.sbuf("n_A", [128, D], F32)
    sh_bc = P.sbuf("n_sh", [128, D], F32)
    xt = [P.sbuf("n_x%d" % i, [128, D], F32) for i in range(2)]
    junk = P.sbuf("n_junk", [128, D], F32)
    hb = [P.sbuf("n_h%d" % i, [128, D], BF16) for i in range(2)]
    hT = [P.sbuf("n_hT%d" % i, [128, 8, TB], BF16) for i in range(2)]
    ss = [P.sbuf("n_ss%d" % i, [128, 1], F32) for i in range(2)]
    rstd = [P.sbuf("n_rs%d" % i, [128, 1], F32) for i in range(2)]
    P.dma(g_bc[:], bc_rows(k.norm_g[l:l + 1, :], 128), "n_g")
    ntile = T // 128
    P.dma(xt[0][:], xsrc[0:128, :], "n_x0")
    for i in range(ntile):
        s = i % 2
        slot = (i * 128) // SL
        if (i * 128) % SL == 0:
            P.dma(A_bc[:], bc_rows(k.MOD[l, slot:slot + 1, D:2 * D], 128), "n_A")
            P.dma(sh_bc[:], bc_rows(k.MOD[l, slot:slot + 1, 0:D], 128), "n_sh")
            P.stt(A_bc[:], A_bc[:], 1.0, g_bc[:], ALU.add, ALU.mult)
        if i + 1 < ntile:
            P.dma(xt[1 - s][:], xsrc[(i + 1) * 128:(i + 2) * 128, :], "n_x%d" % (1 - s))
        rms_rstd(P, k, xt[s][:], junk[:], ss[s][:], rstd[s][:])
        P.stt(junk[:], xt[s][:], rstd[s][:], A_bc[:], ALU.mult, ALU.mult)
        P.tt(hb[s][:], junk[:], sh_bc[:], ALU.add, eng="pool")
        tb, sub = divmod(i, 4)
        hs = tb % 2
        for kt in range(8):
            pst = k.psb[:, kt % 2, (kt // 2) * 128:(kt // 2) * 128 + 128] if False else k.psb[:, kt // 4, (kt % 4) * 128:(kt % 4) * 128 + 128]
            P.transpose(pst, hb[s][:, kt * 128:(kt + 1) * 128], k.identb[:])
        for half in range(2):
            src = k.psb[:, half, 0:512].rearrange("p (a b) -> p a b", a=4)
            dst = hT[hs][:, half * 4:half * 4 + 4, sub * 128:(sub + 1) * 128]
            P.copy(dst, src, eng=("act" if half == 0 else "dve"))
        if sub == 3:
            P.dma(k.HT[:, tb * TB:(tb + 1) * TB].rearrange("(kt p) t -> p kt t", p=128), hT[hs][:], "n_hT%d" % hs)


def phase_final(k, xsrc):
    P, cfg = k.P, k.cfg
    P.reset("phase_final_")
    T = cfg.T
    g_bc = P.sbuf("f_g", [128, D], F32)
    xt = [P.sbuf("f_x%d" % i, [128, D], F32) for i in range(2)]
    yt = [P.sbuf("f_y%d" % i, [128, D], F32) for i in range(2)]
    junk = P.sbuf("f_junk", [128, D], F32)
    ss = [P.sbuf("f_ss%d" % i, [128, 1], F32) for i in range(2)]
    rstd = [P.sbuf("f_rs%d" % i, [128, 1], F32) for i in range(2)]
    P.dma(g_bc[:], bc_rows(k.final_g[0:1, :], 128), "f_g")
    ntile = T // 128
    P.dma(xt[0][:], xsrc[0:128, :], "f_x0")
    for i in range(ntile):
        s = i % 2
        if i + 1 < ntile:
            P.dma(xt[1 - s][:], xsrc[(i + 1) * 128:(i + 2) * 128, :], "f_x%d" % (1 - s))
        rms_rstd(P, k, xt[s][:], junk[:], ss[s][:], rstd[s][:])
        P.stt(yt[s][:], xt[s][:], rstd[s][:], g_bc[:], ALU.mult, ALU.mult)
        P.dma(k.yout[i * 128:(i + 1) * 128, :], yt[s][:], "f_y%d" % s)


def gemm_fm(k, W, groups, epi, pfx):
    P, cfg = k.P, k.cfg
    NTB = cfg.NTB
    wst = [P.sbuf(pfx + "wst%d" % i, [128, 8, 512], F32) for i in range(2)]
    wbf = [P.sbuf(pfx + "wbf%d" % i, [128, 8, 512], BF16) for i in range(2)]
    hT = [P.sbuf(pfx + "hT%d" % i, [128, 8, TB], BF16) for i in range(2)]

    def loadw(gi):
        s = gi % 2
        for j, (col, tag) in enumerate(groups[gi]):
            P.dma(wst[s][:, :, j * 128:(j + 1) * 128], W[:, col:col + 128].rearrange("(kt p) n -> p kt n", p=128),
                  pfx + "w%d" % s)
        ncol = 128 * len(groups[gi])
        P.copy(wbf[s][:, :, 0:ncol], wst[s][:, :, 0:ncol], eng="pool")

    items = [(gi, tb) for gi in range(len(groups)) for tb in range(NTB)]

    def loadh(ii):
        gi, tb = items[ii]
        P.dma(hT[ii % 2][:], k.HT[:, tb * TB:(tb + 1) * TB].rearrange("(kt p) t -> p kt t", p=128), pfx + "h%d" % (ii % 2))

    loadw(0)
    loadh(0)
    it = 0
    lanes = Lanes(3)
    for ii, (gi, tb) in enumerate(items):
        if ii + 1 < len(items):
            loadh(ii + 1)
        if tb == 0 and gi + 1 < len(groups):
            loadw(gi + 1)
        for j, (col, tag) in enumerate(groups[gi]):
            pb = k.ps[:, it % 4, :]
            for kt in range(8):
                P.mm(pb, wbf[gi % 2][:, kt, j * 128:(j + 1) * 128], hT[ii % 2][:, kt, :], start=(kt == 0), stop=(kt == 7))
            lanes.push(epi(tag, col, tb, pb, it))
            it += 1
    lanes.drain()


def gemm_tm(k, W, col0, ncols, dst, pfx):
    P, cfg = k.P, k.cfg
    NTB = cfg.NTB
    wst = [P.sbuf(pfx + "wst%d" % i, [128, 8, 512], F32) for i in range(2)]
    wbf = [P.sbuf(pfx + "wbf%d" % i, [128, 8, 512], BF16) for i in range(2)]
    hT = [P.sbuf(pfx + "hT%d" % i, [128, 8, TB], BF16) for i in range(2)]
    vt = [P.sbuf(pfx + "vt%d" % i, [128, 512], BF16) for i in range(2)]
    ng = ncols // 512

    def loadw(gi):
        s = gi % 2
        P.dma(wst[s][:], W[:, col0 + gi * 512:col0 + (gi + 1) * 512].rearrange("(kt p) n -> p kt n", p=128), pfx + "w%d" % s)
        P.copy(wbf[s][:], wst[s][:], eng="pool")

    items = [(gi, tb) for gi in range(ng) for tb in range(NTB)]

    def loadh(ii):
        gi, tb = items[ii]
        P.dma(hT[ii % 2][:], k.HT[:, tb * TB:(tb + 1) * TB].rearrange("(kt p) t -> p kt t", p=128), pfx + "h%d" % (ii % 2))

    loadw(0)
    loadh(0)
    it = 0
    for ii, (gi, tb) in enumerate(items):
        if ii + 1 < len(items):
            loadh(ii + 1)
        if tb == 0 and gi + 1 < ng:
            loadw(gi + 1)
        for sub in range(4):
            pb = k.ps[:, 4 + it % 2, :]
            for kt in range(8):
                P.mm(pb, hT[ii % 2][:, kt, sub * 128:(sub + 1) * 128], wbf[gi % 2][:, kt, :], start=(kt == 0), stop=(kt == 7))
            P.copy(vt[it % 2][:], pb, eng=("act" if it % 2 == 0 else "dve"))
            r0 = tb * TB + sub * 128
            P.dma(dst[r0:r0 + 128, gi * 512:(gi + 1) * 512], vt[it % 2][:], pfx + "v%d" % (it % 2))
            it += 1


def chunk_engine(k, ND, NV, NU, G, pfx, SCsrc=None):
    P, cfg = k.P, k.cfg
    NCH = cfg.NCH
    SCsrc = k.SC if SCsrc is None else SCsrc
    CB = 4
    BW = CB * CH
    NTB = NCH // CB
    NG = NU // G
    GD = G * ND
    GV = G * NV
    VW = NV * 128
    S = [P.sbuf(pfx + "S%d" % g, [128, GD, VW], F32) for g in range(NG)]
    Sbf = [P.sbuf(pfx + "Sb%d" % g, [128, GD, VW], BF16) for g in range(NG)]
    t1 = [P.sbuf(pfx + "t1%d" % i, [128, GD, VW], F32) for i in range(2)]
    t2 = [P.sbuf(pfx + "t2%d" % i, [128, GD, VW], F32) for i in range(2)]
    PT = [P.sbuf(pfx + "PT%d" % i, [64, G, 64], BF16) for i in range(2)]
    ktok = [P.sbuf(pfx + "kt%d" % i, [64, GD * 128], BF16) for i in range(2)]
    qT = [[P.sbuf(pfx + "q%d_%d" % (b, g), [128, GD, BW], BF16) for g in range(NG)] for b in range(2)]
    kT = [[P.sbuf(pfx + "k%d_%d" % (b, g), [128, GD, BW], BF16) for g in range(NG)] for b in range(2)]
    vb = [[P.sbuf(pfx + "v%d_%d" % (b, g), [64, CB, GV * 128], BF16) for g in range(NG)] for b in range(2)]
    sc = [[P.sbuf(pfx + "s%d_%d" % (b, g), [128, 3, GD, CB], F32) for g in range(NG)] for b in range(2)]
    ob = [[P.sbuf(pfx + "o%d_%d" % (b, g), [128, GV, BW], F32) for g in range(NG)] for b in range(2)]

    def load(dr, bi, blk):
        b = bi % 2
        for g in range(NG):
            r0 = g * GD * 128
            tsl = slice(blk * BW, (blk + 1) * BW)
            P.dma(qT[b][g][:], k.QK[dr, 0, r0:r0 + GD * 128, tsl].rearrange("(j p) t -> p j t", p=128), pfx + "lq%d_%d" % (b, g))
            P.dma(kT[b][g][:], k.QK[dr, 1, r0:r0 + GD * 128, tsl].rearrange("(j p) t -> p j t", p=128), pfx + "lk%d_%d" % (b, g))
            c0 = g * GV * 128
            P.dma(vb[b][g][:], k.V[tsl, c0:c0 + GV * 128].rearrange("(c s) n -> s c n", s=64), pfx + "lv%d_%d" % (b, g))
            for j3 in range(3):
                P.dma(sc[b][g][:, j3, :, :], SCsrc[dr, j3, r0:r0 + GD * 128, blk * CB:(blk + 1) * CB].rearrange("(j p) c -> p j c", p=128),
                      pfx + "ls%d_%d" % (b, g), allow_slow_non_contiguous=True)

    def body(dr, b, c, g, it):
        csl = slice(c * 64, (c + 1) * 64)
        p2 = it % 2
        psA = k.ps[:, p2, :]
        psS = k.ps[:, 2 + 2 * p2:4 + 2 * p2, :] if GD * VW > 512 else k.ps[:, 2 + p2:3 + p2, :]
        psS = psS.rearrange("p a b -> p (a b)")
        pstr = k.psb[0:64, p2, 0:GD * 128]
        for u in range(G):
            for j in range(ND):
                P.mm(psA[0:64, u * 64:(u + 1) * 64], kT[b][g][:, u * ND + j, csl], qT[b][g][:, u * ND + j, csl],
                     start=(j == 0), stop=(j == ND - 1))
        for uj in range(GD):
            P.transpose(pstr[:, uj * 128:(uj + 1) * 128], kT[b][g][:, uj, csl], k.identb[:])
        yield
        trim = k.tri[:, dr * 64:(dr + 1) * 64]
        P.tt(PT[p2][:], psA[0:64, 0:G * 64].rearrange("p (u t) -> p u t", u=G),
             trim.rearrange("p (o t) -> p o t", o=1).to_broadcast([64, G, 64]), ALU.mult)
        P.copy(ktok[p2][:], pstr, eng="act")
        if ND == 1:
            P.tt(Sbf[g][:], S[g][:], sc[b][g][:, 0, :, c:c + 1].to_broadcast([128, GD, VW]), ALU.mult, eng="pool")
        else:
            for uj in range(GD):
                P.act(Sbf[g][:, uj, :], S[g][:, uj, :], AF.Copy, scale=sc[b][g][:, 0, uj, c:c + 1])
        yield
        for u in range(G):
            for i in range(NV):
                po = psA[:, 256 + (u * NV + i) * 64:256 + (u * NV + i + 1) * 64]
                P.mm(po, vb[b][g][:, c, (u * NV + i) * 128:(u * NV + i + 1) * 128], PT[p2][:, u, :], start=True, stop=False)
                for j in range(ND):
                    P.mm(po, Sbf[g][:, u * ND + j, i * 128:(i + 1) * 128], qT[b][g][:, u * ND + j, csl],
                         start=False, stop=(j == ND - 1))
        for uj in range(GD):
            u = uj // ND
            P.mm(psS[:, uj * VW:(uj + 1) * VW], ktok[p2][:, uj * 128:(uj + 1) * 128],
                 vb[b][g][:, c, u * VW:(u + 1) * VW], start=True, stop=True)
        yield
        P.copy(ob[b][g][:, :, csl], psA[:, 256:256 + GV * 64].rearrange("p (a t) -> p a t", a=GV), eng="act")
        if ND == 1:
            P.tt(t1[p2][:], S[g][:], sc[b][g][:, 1, :, c:c + 1].to_broadcast([128, GD, VW]), ALU.mult, eng="pool")
            yield
            P.tt(t2[p2][:], psS.rearrange("p (a v) -> p a v", a=GD),
                 sc[b][g][:, 2, :, c:c + 1].to_broadcast([128, GD, VW]), ALU.mult)
            yield
            P.tt(S[g][:], t1[p2][:], t2[p2][:], ALU.add)
        else:
            for uj in range(GD):
                P.tt(t1[p2][:, uj, :], S[g][:, uj, :], sc[b][g][:, 1, uj, c:c + 1].to_broadcast([128, VW]), ALU.mult, eng="pool")
            yield
            for uj in range(GD):
                P.stt(S[g][:, uj, :], psS[:, uj * VW:(uj + 1) * VW], sc[b][g][:, 2, uj, c:c + 1], t1[p2][:, uj, :], ALU.mult, ALU.add)
                yield

    it = 0
    for dr in range(2):
        for g in range(NG):
            P.memset(S[g][:], 0.0)
        blks = list(range(NTB)) if dr == 0 else list(range(NTB - 1, -1, -1))
        load(dr, 0, blks[0])
        for bi, blk in enumerate(blks):
            b = bi % 2
            if bi + 1 < len(blks):
                load(dr, bi + 1, blks[bi + 1])
            cs = list(range(CB)) if dr == 0 else list(range(CB - 1, -1, -1))
            for c in cs:
                lanes = Lanes(2)
                for g in range(NG):
                    lanes.push(body(dr, b, c, g, it))
                    it += 1
                lanes.drain()
            for g in range(NG):
                r0 = g * GV * 128
                P.dma(k.OT[dr, r0:r0 + GV * 128, blk * BW:(blk + 1) * BW].rearrange("(i p) t -> p i t", p=128), ob[b][g][:],
                      pfx + "so%d_%d" % (b, g))


def outproj_setup(k, l, Wout, pfx):
    P = k.P
    wo = P.sbuf(pfx + "wo", [128, 16, D], BF16)
    st = K()
    st.wo = wo
    st.gate = P.sbuf(pfx + "gate", [128, D], F32)
    st.xt = [P.sbuf(pfx + "x%d" % i, [128, D], F32) for i in range(2)]
    st.ty = [P.sbuf(pfx + "ty%d" % i, [128, D], F32) for i in range(2)]
    st.it = 0
    mark = P.bump
    wst = [P.sbuf(pfx + "wos%d" % i, [128, 16, 256], F32) for i in range(2)]
    for q in range(4):
        P.dma(wst[q % 2][:], Wout[:, q * 256:(q + 1) * 256].rearrange("(e p) n -> p e n", p=128), pfx + "wo%d" % (q % 2))
        P.copy(wo[:, :, q * 256:(q + 1) * 256], wst[q % 2][:], eng="pool")
    P.bump = mark
    return st


def outproj_block(k, l, tb, GT, st, xsrc, pfx):
    P, cfg = k.P, k.cfg
    if (tb * TB) % cfg.SL == 0:
        slot = (tb * TB) // cfg.SL
        P.dma(st.gate[:], bc_rows(k.MOD[l, slot:slot + 1, 2 * D:3 * D], 128), pfx + "gate")
    for sub in range(4):
        s = st.it % 2
        r0 = tb * TB + sub * 128
        P.dma(st.xt[s][:], xsrc[r0:r0 + 128, :], pfx + "x%d" % s)
        for dh in range(2):
            pb = k.ps[:, 4 + dh, :]
            for e in range(16):
                P.mm(pb, GT[:, e, sub * 128:(sub + 1) * 128], st.wo[:, e, dh * 512:(dh + 1) * 512], start=(e == 0), stop=(e == 15))
            P.tt(st.ty[s][:, dh * 512:(dh + 1) * 512], pb, st.gate[:, dh * 512:(dh + 1) * 512], ALU.mult)
        P.tt(st.ty[s][:], st.ty[s][:], st.xt[s][:], ALU.add, eng="pool")
        P.dma(k.X[r0:r0 + 128, :], st.ty[s][:], pfx + "y%d" % s)
        st.it += 1


def hgrn2(k, l, xsrc):
    P, cfg = k.P, k.cfg
    NTB, NCH = cfg.NTB, cfg.NCH
    W = k.hg_w_in
    P.reset("hgrn2_A")
    oq = [P.sbuf("ha_o%d" % i, [128, TB], F32) for i in range(4)]
    sgq = [P.sbuf("ha_s%d" % i, [128, TB], F32) for i in range(4)]

    def epiA(tag, col, tb, pb, it):
        s = it % 4
        dst = k.QT if tag == "q" else k.ZT
        row = col if tag == "q" else col - 4 * E
        P.act(sgq[s][:], pb, AF.Sigmoid)
        yield
        P.tt(oq[s][:], pb, sgq[s][:], ALU.mult, eng=("dve" if it % 2 == 0 else "dve"))
        yield
        P.dma(dst[row:row + 128, tb * TB:(tb + 1) * TB], oq[s][:], "ha_o%d" % s)

    cols = [(h * 128, "q") for h in range(16)] + [(4 * E + h * 128, "z") for h in range(16)]
    groups = [cols[i:i + 4] for i in range(0, len(cols), 4)]
    gemm_fm(k, W, groups, epiA, "ha_")
    P.barrier()
    P.reset("hgrn2_A2")
    gemm_tm(k, W, 3 * E, E, k.V, "hv_")
    P.barrier()
    P.reset("hgrn2_B")
    lbr = P.sbuf("hb_lbr", [128, 16, 5], F32)
    lbe = P.sbuf("hb_lbe", [128, 16, 5], F32)
    den = P.sbuf("hb_den", [128, 16], F32)
    num = P.sbuf("hb_num", [128, 16], F32)
    lbv = P.sbuf("hb_lb", [128, 16], F32)
    oml = P.sbuf("hb_oml", [128, 16], F32)
    for r in range(5):
        P.dma(lbr[:, :, r], k.hg_lb[r, :].rearrange("(j p) -> p j", p=128), "hb_lb", allow_slow_non_contiguous=True)
    P.act(lbe[:], lbr[:], AF.Exp)
    P.add("dve", lambda e: e.reduce_sum(den[:], lbe[:], mybir.AxisListType.X), [lbe[:]], [den[:]])
    P.add("dve", lambda e: e.reduce_sum(num[:], lbe[:, :, 0:l + 1], mybir.AxisListType.X), [lbe[:]], [num[:]])
    P.add("dve", lambda e: e.reciprocal(den[:], den[:]), [den[:]], [den[:]])
    P.tt(lbv[:], num[:], den[:], ALU.mult)
    P.ts(oml[:], lbv[:], -1.0, 1.0, ALU.mult, ALU.add)
    nb = 3
    tl = lambda nm, sh, dt=F32: [P.sbuf("hb_%s%d" % (nm, i), sh, dt) for i in range(nb)]
    qb, sg, ff, gg, kk, bb, bc, e1, e2 = (tl(n, [128, TB]) for n in ("qb", "sg", "ff", "gg", "kk", "bb", "bc", "e1", "e2"))
    qt = tl("qt", [128, TB], BF16)
    kt_ = tl("kt", [128, TB], BF16)
    scs = tl("sc", [128, 3, 8])
    dtmp = tl("dt", [128, 8])

    def epiB(tag, col, tb, pb, it):
        _, dr, h = tag
        s = it % nb
        tsl = slice(tb * TB, (tb + 1) * TB)
        P.dma(qb[s][:], k.QT[h * 128:(h + 1) * 128, tsl], "hb_q%d" % s)
        P.act(sg[s][:], pb, AF.Sigmoid)
        yield
        P.ts(ff[s][:], sg[s][:], oml[:, h:h + 1], lbv[:, h:h + 1], ALU.mult, ALU.add)
        yield
        P.act(gg[s][:], ff[s][:], AF.Ln)
        P.ts(kk[s][:], ff[s][:], -1.0, 1.0, ALU.mult, ALU.add, eng="pool")
        yield
        if dr == 0:
            P.scan(bb[s][:], k.rmask[:], gg[s][:], 0.0)
        else:
            P.scan(bb[s][:, ::-1], k.rmask[:], gg[s][:, ::-1], 0.0)
        yield
        b3 = bb[s][:].rearrange("p (c t) -> p c t", t=64)
        P.tt(bc[s][:].rearrange("p (c t) -> p c t", t=64), b3, b3[:, :, 32:33].to_broadcast([128, 8, 64]), ALU.subtract, eng="pool")
        yield
        P.act(e1[s][:], bc[s][:], AF.Exp)
        yield
        P.act(e2[s][:], bc[s][:], AF.Exp, scale=-1.0)
        P.tt(qt[s][:], qb[s][:], e1[s][:], ALU.mult)
        yield
        P.tt(kt_[s][:], kk[s][:], e2[s][:], ALU.mult, eng="pool")
        refc = b3[:, :, 32]
        lastc = b3[:, :, 63] if dr == 0 else b3[:, :, 0]
        P.act(scs[s][:, 0, :], refc, AF.Exp)
        yield
        P.act(scs[s][:, 1, :], lastc, AF.Exp)
        P.tt(dtmp[s][:], lastc, refc, ALU.subtract)
        yield
        P.act(scs[s][:, 2, :], dtmp[s][:], AF.Exp)
        cr = k.carry[:, dr * NCH + tb * 8:dr * NCH + tb * 8 + 8]
        yield
        P.tt(scs[s][:, 0:2, :], scs[s][:, 0:2, :], cr.rearrange("p (o c) -> p o c", o=1).to_broadcast([128, 2, 8]), ALU.mult)
        rows = slice(h * 128, (h + 1) * 128)
        P.dma(k.QK[dr, 0, rows, tsl], qt[s][:], "hb_sq%d" % s)
        yield
        P.dma(k.QK[dr, 1, rows, tsl], kt_[s][:], "hb_sk%d" % s)
        P.dma(k.SC[dr, :, rows, tb * 8:(tb + 1) * 8].rearrange("j p c -> p j c"), scs[s][:], "hb_ss%d" % s,
              allow_slow_non_contiguous=True)

    cols = [(E + dr * E + h * 128, ("f", dr, h)) for dr in range(2) for h in range(16)]
    groups = [cols[i:i + 4] for i in range(0, len(cols), 4)]
    gemm_fm(k, W, groups, epiB, "hb_")
    P.barrier()
    P.reset("hgrn2_C")
    chunk_engine(k, 1, 1, 16, 4, "hc_")
    P.barrier()
    P.reset("hgrn2_D")
    st = outproj_setup(k, l, k.hg_w_out, "hd_")
    ngc = P.sbuf("hd_ng", [128, 1], F32)
    P.dma(ngc[:], k.hg_norm_g.rearrange("o p -> p o"), "hd_ng", allow_slow_non_contiguous=True)
    GT = [P.sbuf("hd_GT%d" % i, [128, 16, TB], BF16) for i in range(2)]
    tl = lambda nm, sh, dt=F32: [P.sbuf("hd_%s%d" % (nm, i), sh, dt) for i in range(2)]
    NL = 4
    tl = lambda nm, sh, dt=F32: [P.sbuf("hd_%s%d" % (nm, i), sh, dt) for i in range(NL)]
    of, obk, zz, oo, sq, rs = (tl(n, [128, TB]) for n in ("of", "ob", "zz", "oo", "sq", "rs"))

    def body(tb, h, it):
        s = it % NL
        tsl = slice(tb * TB, (tb + 1) * TB)
        rows = slice(h * 128, (h + 1) * 128)
        P.dma(of[s][:], k.OT[0, rows, tsl], "hd_of%d" % s)
        P.dma(obk[s][:], k.OT[1, rows, tsl], "hd_ob%d" % s)
        P.dma(zz[s][:], k.ZT[rows, tsl], "hd_zz%d" % s)
        yield
        P.tt(oo[s][:], of[s][:], obk[s][:], ALU.add, eng="pool")
        yield
        P.act(sq[s][:], oo[s][:], AF.Square)
        yield
        pb = k.ps[:, it % 4, :]
        P.mm(pb, k.ones[:], sq[s][:])
        yield
        P.act(rs[s][:], pb, AF.Sqrt, bias=k.epsT[:], scale=1.0 / 128)
        yield
        P.add("dve", lambda e, a=rs[s][:]: e.reciprocal(a, a), [rs[s][:]], [rs[s][:]])
        yield
        P.tt(oo[s][:], oo[s][:], rs[s][:], ALU.mult, eng="pool")
        yield
        P.stt(GT[tb % 2][:, h, :], oo[s][:], ngc[:], zz[s][:], ALU.mult, ALU.mult)

    it = 0
    for tb in range(NTB):
        lanes = Lanes(NL)
        for h in range(16):
            lanes.push(body(tb, h, it))
            it += 1
        lanes.drain()
        outproj_block(k, l, tb, GT[tb % 2], st, xsrc, "hd_")


def host_consts(cfg, carry):
    NCH = cfg.NCH
    cps = cfg.SL // CH
    c = {}
    c["k_ident"] = np.eye(128, dtype=np.float32)
    ii = np.arange(64)
    fw = (ii[:, None] <= ii[None, :]).astype(np.float32)
    bw = (ii[:, None] >= ii[None, :]).astype(np.float32)
    c["k_tri"] = np.concatenate([fw, bw], axis=1)
    rm = np.ones((128, TB), np.float32)
    rm[:, ::64] = 0.0
    c["k_rmask"] = rm
    cr = np.ones((128, 2 * NCH), np.float32)
    for ch in range(NCH):
        if ch % cps == 0 and ch > 0:
            cr[:, ch] = carry
        if (ch + 1) % cps == 0 and ch < NCH - 1:
            cr[:, NCH + ch] = carry
    c["k_carry"] = cr
    c["k_carry1"] = np.full((128, 1), carry, np.float32)
    return c


def host_consts_rt(cfg, carry, seq_pos):
    NCH = cfg.NCH
    cps = cfg.SL // CH
    c = {}
    inv = (10000.0 ** (-np.arange(0, 256, 2, dtype=np.float32) / np.float32(256))).astype(np.float32)
    ang = (seq_pos.astype(np.float32)[None, :] * inv[:, None]).astype(np.float32)
    c["k_cos"] = np.cos(ang).astype(np.float32)
    c["k_sin"] = np.sin(ang).astype(np.float32)
    hidx = np.arange(4, dtype=np.float32)
    lg = [np.log1p(-np.exp2(-5.0 - hidx)).astype(np.float32), np.log1p(-np.exp2(-5.5 - hidx)).astype(np.float32)]
    pos = np.arange(64, dtype=np.float64)
    dt = np.zeros((2, 4, 2, 64), np.float64)
    sc = np.ones((2, 3, 1024, NCH), np.float64)
    for dr in range(2):
        cnt = (pos + 1.0) if dr == 0 else (64.0 - pos)
        for hd in range(4):
            g = float(lg[dr][hd])
            dt[dr, hd, 0] = np.exp(g * cnt)
            dt[dr, hd, 1] = np.exp(-g * cnt) * (256.0 ** -0.5)
            rows = slice(hd * 256, (hd + 1) * 256)
            sc[dr, 1, rows, :] = np.exp(g * 64.0)
            sc[dr, 2, rows, :] = np.exp(g * 64.0)
        for ch in range(NCH):
            bnd = (ch % cps == 0 and ch > 0) if dr == 0 else ((ch + 1) % cps == 0 and ch < NCH - 1)
            if bnd:
                sc[dr, 0, :, ch] *= carry
                sc[dr, 1, :, ch] *= carry
    c["k_dtab"] = np.broadcast_to(dt.reshape(1, -1), (128, 16 * 64)).astype(np.float32).copy()
    c["k_rtsc"] = sc.astype(np.float32)
    return c


WNAMES = ["ada_w", "ada_b", "norm_g", "hg_lb", "hg_w_in", "hg_norm_g", "hg_w_out"]


def core_inputs(cfg, x_slots, c_slots, carry, w, seq_pos=None, Lc=None):
    SL = cfg.SL
    xin = np.zeros((cfg.T, D), np.float32)
    cin = np.zeros((cfg.NS, D), np.float32)
    for s in range(cfg.NS):
        if x_slots[s] is not None:
            xin[s * SL:(s + 1) * SL] = x_slots[s]
            cin[s] = c_slots[s]
    m = {"xin": xin, "cin": cin}
    m.update(host_consts(cfg, carry))
    dp = cfg.depth
    m["ada_w"] = np.ascontiguousarray(w["ada_w"][:dp])
    m["ada_b"] = np.ascontiguousarray(w["ada_b"][:dp])
    m["norm_g"] = np.ascontiguousarray(w["norm_g"][:dp])
    m["final_g"] = np.ascontiguousarray(w["final_g"]).reshape(1, D)
    m["hg_lb"] = np.ascontiguousarray(w["hg_lb"])
    m["hg_w_in"] = np.ascontiguousarray(w["hg_w_in"][0])
    m["hg_norm_g"] = np.ascontiguousarray(w["hg_norm_g"][0]).reshape(1, 128)
    m["hg_w_out"] = np.ascontiguousarray(w["hg_w_out"][0])
    if 2 in cfg.kinds:
        m.update(host_consts_rt(cfg, carry, seq_pos))
        m["rt_w_in"] = np.ascontiguousarray(w["rt_w_in"][0])
        m["rt_gn_g"] = np.ascontiguousarray(w["rt_gn_g"][0]).reshape(1, E)
        m["rt_w_out"] = np.ascontiguousarray(w["rt_w_out"][0])
    if 1 in cfg.kinds:
        m.update(host_consts_hy(cfg, Lc, [xs is not None for xs in x_slots]))
        for nm in ["hy_w_in", "hy_conv_w", "hy_f_w1", "hy_f_w2", "hy_f_b2", "hy_f_wout", "hy_w_out"]:
            m[nm] = np.ascontiguousarray(w[nm][0])
        for nm in ["hy_b_in", "hy_conv_b", "hy_f_b1", "hy_f_freq", "hy_skip"]:
            m[nm] = np.ascontiguousarray(w[nm][0]).reshape(1, -1)
    if 3 in cfg.kinds:
        m["lru_w_in"] = np.ascontiguousarray(w["lru_w_in"][0])
        m["lru_conv_w"] = np.ascontiguousarray(w["lru_conv_w"][0])
        m["lru_conv_b"] = np.ascontiguousarray(w["lru_conv_b"][0]).reshape(1, E)
        m["lru_gate_w"] = np.ascontiguousarray(w["lru_gate_w"][0])
        m["lru_gate_b"] = np.ascontiguousarray(w["lru_gate_b"][0])
        m["lru_lambda"] = np.ascontiguousarray(w["lru_lambda"][0])
        m["lru_w_out"] = np.ascontiguousarray(w["lru_w_out"][0])
    return m


def retention(k, l, xsrc):
    P, cfg = k.P, k.cfg
    NTB, NCH = cfg.NTB, cfg.NCH
    W = k.rt_w_in
    P.reset("retention_A")
    dtab = P.sbuf("ra_dtab", [128, 16, 64], F32)
    P.dma(dtab[:], k.k_dtab.rearrange("p (a b) -> p a b", a=16), "ra_dt")
    NP = 3
    tl = lambda nm, sh, dt=F32, n=NP: [P.sbuf("ra_%s%d" % (nm, i), sh, dt) for i in range(n)]
    x1, x2, cs_, sn_, o1, o2, ta, tb_ = (tl(n, [128, TB]) for n in ("x1", "x2", "cs", "sn", "o1", "o2", "ta", "tb"))
    obf = tl("obf", [128, TB], BF16, 8)
    oz = tl("oz", [128, TB], F32, 4)
    sgz = tl("sgz", [128, TB], F32, 4)
    cnt = {"p": 0, "o": 0}

    def epiA(tag, col, tb, pb, it):
        kind, hd, a = tag
        tsl = slice(tb * TB, (tb + 1) * TB)
        if kind == "g":
            s = it % 4
            P.act(sgz[s][:], pb, AF.Sigmoid)
            yield
            P.tt(oz[s][:], pb, sgz[s][:], ALU.mult)
            yield
            P.dma(k.ZT[col - 4096:col - 4096 + 128, tsl], oz[s][:], "ra_oz%d" % s)
            return
        s = cnt["p"] % NP
        if a == 0:
            P.copy(x1[s][:], pb, eng="act")
            return
        P.copy(x2[s][:], pb, eng="act")
        cnt["p"] += 1
        qk = 0 if kind == "q" else 1
        P.dma(cs_[s][:], k.k_cos[:, tsl], "ra_c%d" % s)
        P.dma(sn_[s][:], k.k_sin[:, tsl], "ra_s%d" % s)
        yield
        P.tt(o1[s][:], x1[s][:], cs_[s][:], ALU.mult)
        P.tt(ta[s][:], x2[s][:], sn_[s][:], ALU.mult, eng="pool")
        yield
        P.tt(o2[s][:], x1[s][:], sn_[s][:], ALU.mult, eng="pool")
        P.tt(tb_[s][:], x2[s][:], cs_[s][:], ALU.mult)
        yield
        P.tt(o1[s][:], o1[s][:], ta[s][:], ALU.subtract)
        P.tt(o2[s][:], o2[s][:], tb_[s][:], ALU.add, eng="pool")
        yield
        for dr in range(2):
            dsl = dtab[:, (dr * 4 + hd) * 2 + qk, :].rearrange("p (o t) -> p o t", o=1).to_broadcast([128, TB // 64, 64])
            for half, src in enumerate((o1[s], o2[s])):
                oi = cnt["o"] % 8
                ob_ = obf[oi]
                P.tt(ob_[:].rearrange("p (c t) -> p c t", t=64), src[:].rearrange("p (c t) -> p c t", t=64), dsl, ALU.mult,
                     eng=("dve" if half == 0 else "pool"))
                r0 = (hd * 2 + half) * 128
                cnt["o"] += 1
                yield
                P.dma(k.QK[dr, qk, r0:r0 + 128, tsl], ob_[:], "ra_so%d" % oi)

    groups = []
    for hd in range(4):
        groups.append([(hd * 256, ("q", hd, 0)), (hd * 256 + 128, ("q", hd, 1)),
                       (1024 + hd * 256, ("k", hd, 0)), (1024 + hd * 256 + 128, ("k", hd, 1))])
    gcols = [(4096 + j * 128, ("g", 0, 0)) for j in range(16)]
    groups += [gcols[i:i + 4] for i in range(0, 16, 4)]
    gemm_fm(k, W, groups, epiA, "ra_")
    P.barrier()
    P.reset("retention_A2")
    gemm_tm(k, W, 2048, E, k.V, "rv_")
    P.barrier()
    P.reset("retention_B")
    chunk_engine(k, 2, 4, 4, 1, "rc_", SCsrc=k.k_rtsc)
    P.barrier()
    P.reset("retention_C")
    st = outproj_setup(k, l, k.rt_w_out, "rd_")
    gn = P.sbuf("rd_gn", [128, 16], F32)
    P.dma(gn[:], k.rt_gn_g.rearrange("o (j p) -> p (o j)", p=128), "rd_gn", allow_slow_non_contiguous=True)
    GT = [P.sbuf("rd_GT%d" % i, [128, 16, TB], BF16) for i in range(2)]
    NL = 2
    tl = lambda nm, sh, dt=F32, n=2: [P.sbuf("rd_%s%d" % (nm, i), sh, dt) for i in range(n)]
    of, obk, zz = (tl(n, [128, TB], F32, NL * 4) for n in ("of", "ob", "zz"))
    sq = obk
    oo = of
    rs = tl("rs", [128, TB], F32, NL)

    def body(tb, hd, hi):
        tsl = slice(tb * TB, (tb + 1) * TB)
        ln = hi % NL
        pb = k.ps[:, hi % 4, :]
        for i in range(4):
            e = hd * 4 + i
            s = ln * 4 + i
            rows = slice(e * 128, (e + 1) * 128)
            P.dma(of[s][:], k.OT[0, rows, tsl], "rd_of%d" % s)
            P.dma(obk[s][:], k.OT[1, rows, tsl], "rd_ob%d" % s)
            P.dma(zz[s][:], k.ZT[rows, tsl], "rd_zz%d" % s)
        yield
        for i in range(4):
            s = ln * 4 + i
            P.tt(oo[s][:], of[s][:], obk[s][:], ALU.add, eng=("pool" if i % 2 == 0 else "dve"))
            yield
            P.act(sq[s][:], oo[s][:], AF.Square)
            yield
        for i in range(4):
            s = ln * 4 + i
            P.mm(pb, k.ones[:], sq[s][:], start=(i == 0), stop=(i == 3))
        yield
        r_ = rs[ln]
        P.act(r_[:], pb, AF.Sqrt, bias=k.epsT[:], scale=1.0 / 512)
        yield
        P.add("dve", lambda e_, a=r_[:]: e_.reciprocal(a, a), [r_[:]], [r_[:]])
        yield
        for i in range(4):
            e = hd * 4 + i
            s = ln * 4 + i
            P.tt(oo[s][:], oo[s][:], r_[:], ALU.mult, eng="pool")
            yield
            P.stt(GT[tb % 2][:, e, :], oo[s][:], gn[:, e:e + 1], zz[s][:], ALU.mult, ALU.mult)
            yield

    hi = 0
    for tb in range(NTB):
        lanes = Lanes(NL)
        for hd in range(4):
            lanes.push(body(tb, hd, hi))
            hi += 1
        lanes.drain()
        outproj_block(k, l, tb, GT[tb % 2], st, xsrc, "rd_")


def rglru(k, l, xsrc):
    P, cfg = k.P, k.cfg
    NTB, T, SL = cfg.NTB, cfg.T, cfg.SL
    W = k.lru_w_in
    P.reset("rglru_A")
    oq = [P.sbuf("la_o%d" % i, [128, TB], F32) for i in range(4)]
    sgq = [P.sbuf("la_s%d" % i, [128, TB], F32) for i in range(4)]

    def epiA(tag, col, tb, pb, it):
        s = it % 4
        tsl = slice(tb * TB, (tb + 1) * TB)
        if tag == "x":
            P.copy(oq[s][:], pb, eng="act")
            yield
            P.dma(k.QT[col:col + 128, tsl], oq[s][:], "la_o%d" % s)
        else:
            P.act(sgq[s][:], pb, AF.Sigmoid)
            yield
            P.tt(oq[s][:], pb, sgq[s][:], ALU.mult)
            yield
            P.dma(k.ZT[col - E:col - E + 128, tsl], oq[s][:], "la_o%d" % s)

    cols = [(j * 128, "x") for j in range(16)] + [(E + j * 128, "z") for j in range(16)]
    groups = [cols[i:i + 4] for i in range(0, 32, 4)]
    gemm_fm(k, W, groups, epiA, "la_")
    P.barrier()
    P.reset("rglru_B")
    BL = 512
    NB = T // BL
    NL = 6
    cw = P.sbuf("lb_cw", [128, 16, 4], F32)
    cb = P.sbuf("lb_cb", [128, 16], F32)
    gb = P.sbuf("lb_gb", [128, 4, 16], F32)
    lam = P.sbuf("lb_lam", [128, 2, 16], F32)
    m8 = P.sbuf("lb_m8", [128, 2, 16], F32)
    for jj in range(4):
        P.dma(cw[:, :, jj], k.lru_conv_w[jj, :].rearrange("(j p) -> p j", p=128), "lb_c", allow_slow_non_contiguous=True)
    P.dma(cb[:], k.lru_conv_b.rearrange("o (j p) -> p (o j)", p=128), "lb_c", allow_slow_non_contiguous=True)
    for d in range(2):
        P.dma(lam[:, d, :], k.lru_lambda[d, :].rearrange("(j p) -> p j", p=128), "lb_c", allow_slow_non_contiguous=True)
        for g in range(2):
            P.dma(gb[:, d * 2 + g, :], k.lru_gate_b[d, g, :].rearrange("(j p) -> p j", p=128), "lb_c", allow_slow_non_contiguous=True)
    P.act(m8[:], lam[:], AF.Exp, scale=-1.0)
    P.act(m8[:], m8[:], AF.Ln, bias=k.ones[:, 0:1], scale=1.0)
    P.ts(m8[:], m8[:], -8.0, None, ALU.mult)
    gw = [P.sbuf("lb_gw%d" % i, [128, 4, 128], F32) for i in range(NL)]
    tl = lambda nm, sh, dt=F32, n=NL: [P.sbuf("lb_%s%d" % (nm, i), sh, dt) for i in range(n)]
    xr = tl("xr", [128, BL + 3])
    xb, rr, ii, aa, hf, zz = (tl(n, [128, BL]) for n in ("xb", "rr", "ii", "aa", "hf", "zz"))
    a2, sq, bx, bb = rr, rr, ii, ii
    hh = [[P.sbuf("lb_hh%d_%d" % (i, q), [128, BL], F32) for q in range(2)] for i in range(NL)]
    gt = tl("gt", [128, BL], BF16)
    hinit = tl("hi", [128, 1])

    def chain(j, s):
        rows = slice(j * 128, (j + 1) * 128)
        gws = gw[s]
        for d in range(2):
            for g in range(2):
                P.dma(gws[:, d * 2 + g, :], k.lru_gate_w[d, g, j, :, :], "lb_gw%d" % s)
        yield
        cnt = 0
        for d in range(2):
            order = list(range(NB)) if d == 0 else list(range(NB - 1, -1, -1))
            prev = None
            for bi in order:
                t0 = bi * BL
                hcur = hh[s][cnt % 2]
                cnt += 1
                lo = max(t0 - 2, 0)
                hi_ = min(t0 + BL + 1, T)
                if t0 == 0:
                    P.memset(xr[s][:, 0:2], 0.0)
                if t0 + BL == T:
                    P.memset(xr[s][:, BL + 2:BL + 3], 0.0)
                P.dma(xr[s][:, lo - (t0 - 2):hi_ - (t0 - 2)], k.QT[rows, lo:hi_], "lb_x%d" % s)
                if t0 % SL == 0 and t0 > 0:
                    P.ts(xr[s][:, 0:2], xr[s][:, 0:2], k.carry1[:], None, ALU.mult, eng="pool")
                if (t0 + BL) % SL == 0 and t0 + BL < T:
                    P.ts(xr[s][:, BL + 2:BL + 3], xr[s][:, BL + 2:BL + 3], k.carry1[:], None, ALU.mult, eng="pool")
                yield
                P.act(xb[s][:], xr[s][:, 0:BL], AF.Identity, bias=cb[:, j:j + 1], scale=cw[:, j, 0:1])
                yield
                P.stt(xb[s][:], xr[s][:, 1:BL + 1], cw[:, j, 1:2], xb[s][:], ALU.mult, ALU.add)
                yield
                P.stt(xb[s][:], xr[s][:, 2:BL + 2], cw[:, j, 2:3], xb[s][:], ALU.mult, ALU.add)
                yield
                P.stt(xb[s][:], xr[s][:, 3:BL + 3], cw[:, j, 3:4], xb[s][:], ALU.mult, ALU.add)
                yield
                p0 = k.ps[:, s, :]
                P.mm(p0, gws[:, d * 2 + 0, :], xb[s][:])
                yield
                P.act(rr[s][:], p0, AF.Sigmoid, bias=gb[:, d * 2 + 0, j:j + 1])
                yield
                P.mm(p0, gws[:, d * 2 + 1, :], xb[s][:])
                yield
                P.act(ii[s][:], p0, AF.Sigmoid, bias=gb[:, d * 2 + 1, j:j + 1])
                yield
                P.act(aa[s][:], rr[s][:], AF.Exp, scale=m8[:, d, j:j + 1])
                P.tt(bx[s][:], ii[s][:], xb[s][:], ALU.mult, eng="pool")
                yield
                P.tt(a2[s][:], aa[s][:], aa[s][:], ALU.mult, eng="pool")
                yield
                P.act(sq[s][:], a2[s][:], AF.Sqrt, bias=k.ones[:, 0:1], scale=-1.0)
                yield
                P.tt(bb[s][:], sq[s][:], bx[s][:], ALU.mult)
                yield
                if prev is None:
                    init = 0.0
                else:
                    bnd = (t0 % SL == 0) if d == 0 else ((t0 + BL) % SL == 0)
                    if bnd:
                        P.ts(hinit[s][:], prev, k.carry1[:], None, ALU.mult)
                        init = hinit[s][:]
                        yield
                    else:
                        init = prev
                if d == 0:
                    P.scan(hcur[:], aa[s][:], bb[s][:], init)
                    prev = hcur[:, BL - 1:BL]
                    yield
                    P.dma(k.OT[0, rows, t0:t0 + BL], hcur[:], "lb_sh%d" % s)
                else:
                    P.dma(hf[s][:], k.OT[0, rows, t0:t0 + BL], "lb_lf%d" % s)
                    P.dma(zz[s][:], k.ZT[rows, t0:t0 + BL], "lb_lz%d" % s)
                    P.scan(hcur[:, ::-1], aa[s][:, ::-1], bb[s][:, ::-1], init)
                    prev = hcur[:, 0:1]
                    yield
                    P.tt(hf[s][:], hf[s][:], hcur[:], ALU.add, eng="pool")
                    yield
                    P.tt(gt[s][:], hf[s][:], zz[s][:], ALU.mult)
                    yield
                    P.dma(k.QK[0, 0, rows, t0:t0 + BL], gt[s][:], "lb_sg%d" % s)
                yield

    lanes = Lanes(NL)
    for j in range(16):
        lanes.push(chain(j, j % NL))
    lanes.drain()
    P.barrier()
    P.reset("rglru_C")
    st = outproj_setup(k, l, k.lru_w_out, "lc_")
    GT = [P.sbuf("lc_GT%d" % i, [128, 16, TB], BF16) for i in range(2)]
    for tb in range(NTB):
        P.dma(GT[tb % 2][:], k.QK[0, 0, :, tb * TB:(tb + 1) * TB].rearrange("(e p) t -> p e t", p=128), "lc_g%d" % (tb % 2))
        outproj_block(k, l, tb, GT[tb % 2], st, xsrc, "lc_")


def host_consts_hy(cfg, Lc, real_slots):
    NA, NB, T = cfg.NA, cfg.NB, cfg.T
    NF = 2 * T
    c = {}
    a = np.arange(NA, dtype=np.float64)
    b = np.arange(NB, dtype=np.float64)
    tha = 2 * np.pi * np.outer(a, a) / NA
    thb = 2 * np.pi * np.outer(b, b) / NB
    FAr, FAi = np.cos(tha), -np.sin(tha)
    FBr, FBi = np.cos(thb), -np.sin(thb)
    c["k_fa"] = np.concatenate([FAr, FAi], 1).astype(np.float32)
    c["k_fb"] = np.concatenate([FBr, FBi, -FBi], 1).astype(np.float32)
    c["k_fc"] = np.concatenate([FBr, -FBi, FBi, FBr], 1).astype(np.float32)
    th = 2 * np.pi * np.outer(b, a) / NF
    c["k_tw1"] = np.concatenate([np.cos(th), -np.sin(th)], 1).astype(np.float32)
    c["k_tw2"] = np.concatenate([np.cos(th.T), np.sin(th.T)], 1).astype(np.float32)
    n = np.arange(NF)
    mf = (n < Lc)
    mb = (n > NF - Lc) | (n == 0)
    pos = np.where(mf, n, np.where(mb, NF - n, 0)).astype(np.float32)
    pos[0] = 0.0
    t = (pos / np.float32(Lc - 1)).astype(np.float32)
    wv = (np.float32(2.0 * np.pi) * pos / np.float32(Lc)).astype(np.float32)
    bands = np.linspace(1e-4, 15, 16, dtype=np.float32)
    ang = (bands[:, None] * wv[None, :]).astype(np.float32)
    z = np.concatenate([t[None, :], np.cos(ang), -np.sin(ang)], 0).astype(np.float32)
    valid = (mf | mb)
    c["k_zT"] = (z * valid[None, :]).astype(np.float32)
    c["k_fmask"] = np.stack([mf, mb]).astype(np.float32)
    c["k_trow"] = (t * valid).astype(np.float32).reshape(1, NF)
    import math
    max_decay = math.log(1e-2) / 0.3
    min_decay = math.log(1e-2) / 1.5
    deltas = np.abs(np.linspace(min_decay, max_decay, E, dtype=np.float32))
    c["k_delta"] = np.ascontiguousarray(deltas.reshape(16, 128).T).astype(np.float32)
    sm = np.zeros((128, cfg.NS), np.float32)
    for s_, r in enumerate(real_slots):
        sm[:, s_] = 1.0 if r else 0.0
    c["k_smask"] = sm
    return c


def hy_fft(k, src, K1, pfx, sink):
    P, cfg = k.P, k.cfg
    NA, NB = cfg.NA, cfg.NB
    FT = k.FT
    CG, LG = 4, 16
    xin = [P.sbuf(pfx + "xin%d" % i, [K1, LG, NB], F32) for i in range(2)]
    if FT != F32:
        xrr = [k.r_xrr[i][0:K1] for i in range(2)]
        Bt = k.r_B
    else:
        xrr = xin
        Bt = [P.sbuf(pfx + "B%d" % i, [NB, 2, CG, NA], F32) for i in range(2)]
    mt = [P.sbuf(pfx + "m%d" % i, [NB, CG, NA], F32) for i in range(8)]
    psA = k.ps[0:NB, 0:2, :].rearrange("p a b -> p (a b)")
    psXr = k.ps[0:NB, 2, 0:CG * NA]
    psXi = k.ps[0:NB, 3, 0:CG * NA]
    twr = k.tw1[:, 0:NA].rearrange("p (o k) -> p o k", o=1).to_broadcast([NB, CG, NA])
    twi = k.tw1[:, NA:2 * NA].rearrange("p (o k) -> p o k", o=1).to_broadcast([NB, CG, NA])

    def grp(lg, gi, it, xs):
        s = it % 2
        m = mt[s * 4:s * 4 + 4]
        for cc in range(CG):
            P.mm(psA[:, cc * 2 * NA:(cc + 1) * 2 * NA], xs[:, gi * CG + cc, :], k.fa[0:K1, :])
        yield
        A4 = psA[:, 0:CG * 2 * NA].rearrange("p (c r k) -> p c r k", c=CG, r=2)
        Ar, Ai = A4[:, :, 0, :], A4[:, :, 1, :]
        P.tt(m[0][:], Ar, twr, ALU.mult)
        yield
        P.tt(m[1][:], Ai, twi, ALU.mult)
        yield
        P.tt(Bt[s][:, 0, :, :], m[0][:], m[1][:], ALU.subtract, eng="pool")
        P.tt(m[2][:], Ar, twi, ALU.mult)
        yield
        P.tt(m[3][:], Ai, twr, ALU.mult)
        yield
        P.tt(Bt[s][:, 1, :, :], m[2][:], m[3][:], ALU.add, eng="pool")
        yield
        Brf = Bt[s][:, 0, :, :].rearrange("p c k -> p (c k)")
        Bif = Bt[s][:, 1, :, :].rearrange("p c k -> p (c k)")
        P.mm(psXr, k.fb[:, 0:NB], Brf, start=True, stop=False)
        P.mm(psXr, k.fb[:, 2 * NB:3 * NB], Bif, start=False, stop=True)
        P.mm(psXi, k.fb[:, NB:2 * NB], Brf, start=True, stop=False)
        P.mm(psXi, k.fb[:, 0:NB], Bif, start=False, stop=True)
        yield
        yield from sink(lg, gi, lg * LG + gi * CG, psXr, psXi, s)

    lanes = Lanes(2, stagger=k.fft_stagger)
    it = 0
    for lg in range(E // LG):
        xs = xin[lg % 2]
        P.dma(xs[:], src[lg * LG:(lg + 1) * LG, 0:K1 * NB].rearrange("c (n1 n2) -> n1 c n2", n2=NB), pfx + "x%d" % (lg % 2))
        if FT != F32:
            P.copy(xrr[lg % 2][:], xs[:], eng="act")
            xs = xrr[lg % 2]
        for gi in range(LG // CG):
            lanes.push(grp(lg, gi, it, xs))
            it += 1
    lanes.drain()


import os as _os
_HYSTOP = _os.environ.get("HY_STOP", "")
_USE_F32R = _os.environ.get("HY_F32R", "1") == "1"


def hyena(k, l, xsrc):
    P, cfg = k.P, k.cfg
    NTB, T, SL, NS = cfg.NTB, cfg.T, cfg.SL, cfg.NS
    NA, NB = cfg.NA, cfg.NB
    NF = 2 * T
    CG, LG = 4, 16
    W = k.hy_w_in
    UT, X1, YT = k.QT, k.OT[0], k.OT[1]
    P.reset("hyena_A")
    bin_ = P.sbuf("ya_bin", [128, 64], F32)
    P.dma(bin_[:], k.hy_b_in.rearrange("o (j p) -> p (o j)", p=128), "ya_b", allow_slow_non_contiguous=True)
    oq = [P.sbuf("ya_o%d" % i, [128, TB], F32) for i in range(4)]
    sgq = [P.sbuf("ya_s%d" % i, [128, TB], F32) for i in range(4)]
    zb = [P.sbuf("ya_z%d" % i, [128, TB], F32) for i in range(4)]

    def epiA(tag, col, tb, pb, it):
        s = it % 4
        j = col // 128
        tsl = slice(tb * TB, (tb + 1) * TB)
        if tag == "x":
            P.act(oq[s][:], pb, AF.Identity, bias=bin_[:, j:j + 1])
            yield
            P.dma(k.PR[col:col + 128, tsl], oq[s][:], "ya_o%d" % s)
        else:
            P.act(zb[s][:], pb, AF.Identity, bias=bin_[:, j:j + 1])
            P.act(sgq[s][:], pb, AF.Sigmoid, bias=bin_[:, j:j + 1])
            yield
            P.tt(oq[s][:], zb[s][:], sgq[s][:], ALU.mult, eng="pool")
            yield
            P.dma(k.ZT[col - 3 * E:col - 3 * E + 128, tsl], oq[s][:], "ya_o%d" % s)

    cols = [(j * 128, "x") for j in range(48)] + [(3 * E + j * 128, "z") for j in range(16)]
    groups = [cols[i:i + 4] for i in range(0, 64, 4)]
    gemm_fm(k, W, groups, epiA, "ya_")
    P.barrier()
    if _HYSTOP == "B":
        return
    P.reset("hyena_B")
    BL = 512
    NBK = T // BL
    NL = 4
    cw = P.sbuf("yb_cw", [128, 48, 3], F32)
    cb = P.sbuf("yb_cb", [128, 48], F32)
    smask = P.sbuf("yb_sm", [128, NS], F32)
    for jj in range(3):
        P.dma(cw[:, :, jj], k.hy_conv_w[jj, :].rearrange("(j p) -> p j", p=128), "yb_c", allow_slow_non_contiguous=True)
    P.dma(cb[:], k.hy_conv_b.rearrange("o (j p) -> p (o j)", p=128), "yb_c", allow_slow_non_contiguous=True)
    P.dma(smask[:], k.k_smask, "yb_c")
    xr = [[P.sbuf("yb_xr%d_%d" % (i, b_), [128, BL + 2], F32) for b_ in range(3)] for i in range(NL)]
    xc = [[P.sbuf("yb_xc%d_%d" % (i, b_), [128, BL], F32) for b_ in range(3)] for i in range(NL)]
    tmpc = [P.sbuf("yb_tm%d" % i, [128, BL], F32) for i in range(NL)]
    uu = [P.sbuf("yb_u%d" % i, [128, BL], F32) for i in range(NL)]

    def bodyB(j, bi, s):
        t0 = bi * BL
        lo = max(t0 - 1, 0)
        hi_ = min(t0 + BL + 1, T)
        for b_ in range(3):
            jt = b_ * 16 + j
            xx = xr[s][b_]
            if t0 == 0:
                P.memset(xx[:, 0:1], 0.0)
            if t0 + BL == T:
                P.memset(xx[:, BL + 1:BL + 2], 0.0)
            P.dma(xx[:, lo - (t0 - 1):hi_ - (t0 - 1)], k.PR[jt * 128:(jt + 1) * 128, lo:hi_], "yb_x%d_%d" % (s, b_))
            if t0 % SL == 0 and t0 > 0:
                P.ts(xx[:, 0:1], xx[:, 0:1], k.carry1[:], None, ALU.mult, eng="pool")
            if (t0 + BL) % SL == 0 and t0 + BL < T:
                P.ts(xx[:, BL + 1:BL + 2], xx[:, BL + 1:BL + 2], k.carry1[:], None, ALU.mult, eng="pool")
        yield
        for b_ in (0, 2):
            jt = b_ * 16 + j
            xx = xr[s][b_]
            o = xc[s][b_]
            P.ts(o[:], xx[:, 0:BL], cw[:, jt, 0:1], cb[:, jt:jt + 1], ALU.mult, ALU.add)
            yield
            P.stt(o[:], xx[:, 1:BL + 1], cw[:, jt, 1:2], o[:], ALU.mult, ALU.add)
            yield
            P.stt(o[:], xx[:, 2:BL + 2], cw[:, jt, 2:3], o[:], ALU.mult, ALU.add)
            if b_ == 0:
                jt1 = 16 + j
                x1_, o1_ = xr[s][1], xc[s][1]
                P.act(o1_[:], x1_[:, 0:BL], AF.Identity, bias=cb[:, jt1:jt1 + 1], scale=cw[:, jt1, 0:1])
                yield
                P.act(tmpc[s][:], x1_[:, 1:BL + 1], AF.Copy, scale=cw[:, jt1, 1:2])
                yield
                P.tt(o1_[:], o1_[:], tmpc[s][:], ALU.add, eng="pool")
                yield
                P.act(tmpc[s][:], x1_[:, 2:BL + 2], AF.Copy, scale=cw[:, jt1, 2:3])
                yield
                P.tt(o1_[:], o1_[:], tmpc[s][:], ALU.add, eng="pool")
            yield
        slot = t0 // SL
        P.stt(uu[s][:], xc[s][0][:], smask[:, slot:slot + 1], xc[s][2][:], ALU.mult, ALU.mult)
        yield
        P.dma(UT[j * 128:(j + 1) * 128, t0:t0 + BL], uu[s][:], "yb_su%d" % s)
        P.dma(X1[j * 128:(j + 1) * 128, t0:t0 + BL], xc[s][1][:], "yb_s1%d" % s)

    lanes = Lanes(NL)
    it = 0
    for j in range(16):
        for bi in range(NBK):
            lanes.push(bodyB(j, bi, it % NL))
            it += 1
    lanes.drain()
    P.barrier()
    if _HYSTOP == "C":
        return
    P.reset("hyena_C")
    w1 = P.sbuf("yc_w1", [33, 64], F32)
    w2 = P.sbuf("yc_w2", [64, 2, 64], F32)
    wout = P.sbuf("yc_wo", [64, 2 * E], F32)
    fq = P.sbuf("yc_fq", [64, 1], F32)
    fbr = P.sbuf("yc_fbr", [64, 3], F32)
    fbs = P.sbuf("yc_fbs", [64, 3], F32)
    ndel = P.sbuf("yc_nd", [128, 16], F32)
    P.dma(w1[:], k.hy_f_w1, "yc_c")
    for jj in range(2):
        P.dma(w2[:, jj, :], k.hy_f_w2[jj, :, :], "yc_c")
        P.dma(fbr[:, 1 + jj:2 + jj], k.hy_f_b2[jj:jj + 1, :].rearrange("o p -> p o"), "yc_c", allow_slow_non_contiguous=True)
    P.dma(wout[:], k.hy_f_wout, "yc_c")
    P.dma(fq[:], k.hy_f_freq.rearrange("o p -> p o"), "yc_c", allow_slow_non_contiguous=True)
    P.dma(fbr[:, 0:1], k.hy_f_b1.rearrange("o p -> p o"), "yc_c", allow_slow_non_contiguous=True)
    P.dma(ndel[:], k.k_delta, "yc_c")
    P.ts(fq[:], fq[:], float(1.0 / (2.0 * np.pi)), None, ALU.mult)
    P.ts(fbs[:], fbr[:], fq[:], None, ALU.mult)
    P.ts(ndel[:], ndel[:], -1.0, None, ALU.mult, eng="pool")
    tl = lambda nm, sh, dt=F32, n=2: [P.sbuf("yc_%s%d" % (nm, i), sh, dt) for i in range(n)]
    zt = tl("zt", [33, TB])
    mrow = tl("mr", [64, 2, TB])
    trow = tl("tr", [128, TB])
    uf = tl("uf", [64, TB], F32, 3)
    ui = tl("ui", [64, TB], I32, 3)
    aa = tl("aa", [64, TB], F32, 3)
    k.FT = mybir.dt.float32r if (NA == 128 and _USE_F32R) else F32
    afw = tl("afw", [64, TB])
    abw = tl("abw", [64, TB])
    woutr = wout
    win = tl("win", [128, TB])
    tp = tl("tp", [128, TB])
    SC2PI = float(2.0 * np.pi * (1.0 - 1e-6))
    it = 0
    for nb in range(NF // TB):
        s = nb % 2
        nsl = slice(nb * TB, (nb + 1) * TB)
        P.dma(zt[s][:], k.k_zT[:, nsl], "yc_z%d" % s)
        P.dma(mrow[s][:, 0, :], bc_rows(k.k_fmask[0:1, nsl], 64), "yc_m%d" % s)
        P.dma(mrow[s][:, 1, :], bc_rows(k.k_fmask[1:2, nsl], 64), "yc_m%d" % s)
        P.dma(trow[s][:], bc_rows(k.k_trow[0:1, nsl], 128), "yc_t%d" % s)
        a_in = zt[s][:]
        for ly in range(3):
            pb = k.ps[0:64, 4 + (ly % 2), :]
            lw = w1[:] if ly == 0 else w2[:, ly - 1, :]
            P.mm(pb, lw, a_in)
            P.ts(uf[ly][:], pb, fq[:], fbs[:, ly:ly + 1], ALU.mult, ALU.add)
            P.copy(ui[ly][:], uf[ly][:])
            P.tt(uf[ly][:], uf[ly][:], ui[ly][:], ALU.subtract)
            P.act(aa[ly][:], uf[ly][:], AF.Sin, scale=SC2PI)
            a_in = aa[ly][:]
        P.tt(afw[s][:], aa[2][:], mrow[s][:, 0, :], ALU.mult, eng="pool")
        P.tt(abw[s][:], aa[2][:], mrow[s][:, 1, :], ALU.mult, eng="pool")
        for j in range(16):
            q = it % 2
            pb = k.ps[:, it % 4, :]
            P.mm(pb, woutr[:, j * 128:(j + 1) * 128], afw[s][:], start=True, stop=False)
            P.mm(pb, woutr[:, E + j * 128:E + (j + 1) * 128], abw[s][:], start=False, stop=True)
            P.act(win[q][:], trow[s][:], AF.Exp, scale=ndel[:, j:j + 1])
            P.tt(tp[q][:], pb, win[q][:], ALU.mult)
            P.dma(k.TAPS[j * 128:(j + 1) * 128, nsl], tp[q][:], "yc_s%d" % q)
            it += 1
    P.barrier()
    if _HYSTOP == "D":
        return
    P.reset("hyena_D")
    FT = k.FT
    k.tw1 = P.sbuf("yd_tw1", [NB, 2 * NA], F32)
    k.tw2 = P.sbuf("yd_tw2", [NA, 2 * NB], F32)
    fa0 = P.sbuf("yd_fa0", [NA, 2 * NA], F32)
    fb0 = P.sbuf("yd_fb0", [NB, 3 * NB], F32)
    fc0 = P.sbuf("yd_fc0", [NB, 4 * NB], F32)
    P.dma(fa0[:], k.k_fa, "yd_c")
    P.dma(fb0[:], k.k_fb, "yd_c")
    P.dma(fc0[:], k.k_fc, "yd_c")
    if FT != F32:
        k.fa = P.sbuf_r("yd_fa", [NA, 2 * NA], FT)
        k.fb = P.sbuf_r("yd_fb", [NB, 3 * NB], FT)
        k.fc = P.sbuf_r("yd_fc", [NB, 4 * NB], FT)
        k.r_xrr = [P.sbuf_r("yd_xrr%d" % i, [NA, LG, NB], FT) for i in range(2)]
        k.r_B = [P.sbuf_r("yd_rB%d" % i, [NB, 2, CG, NA], FT) for i in range(2)]
        k.r_Y = [P.sbuf_r("yd_rY%d" % i, [NB, 2, CG, NA], FT) for i in range(2)]
        k.r_D = [P.sbuf_r("yd_rD%d" % i, [NA, 2, CG, NB], FT) for i in range(2)]
        P.copy(k.fa[:], fa0[:])
        P.copy(k.fb[:], fb0[:])
        P.copy(k.fc[:], fc0[:], eng="pool")
    else:
        k.fa, k.fb, k.fc = fa0, fb0, fc0
    P.dma(k.tw1[:], k.k_tw1, "yd_c")
    P.dma(k.tw2[:], k.k_tw2, "yd_c")
    wm = P.bump
    rwm = P.rbump
    hsb = [P.sbuf("yd_h%d" % i, [NB, 2, CG, NA], F32) for i in range(2)]
    cnt = {"i": 0}

    def sink1(lg, gi, c0, psXr, psXi, s):
        P.copy(hsb[s][:, 0, :, :], psXr.rearrange("p (c k) -> p c k", c=CG), eng="act")
        yield
        P.copy(hsb[s][:, 1, :, :], psXi.rearrange("p (c k) -> p c k", c=CG), eng="act")
        yield
        for ri in range(2):
            P.dma(k.HS[ri, :, c0:c0 + CG, :], hsb[s][:, ri, :, :], "yd_sh%d" % s)

    k.fft_stagger = 5
    hy_fft(k, k.TAPS, NA, "yd1_", sink1)
    P.barrier()
    P.bump = wm
    P.rbump = rwm
    P.nphase += 1
    P.cur_phase = "%02d_hyena_D2" % P.nphase
    Hh = [P.sbuf("yd_H%d" % i, [NB, 2, CG, NA], F32) for i in range(2)]
    if FT != F32:
        Yt = k.r_Y
        Dt = k.r_D
    else:
        Yt = [P.sbuf("yd_Y%d" % i, [NB, 2, CG, NA], F32) for i in range(2)]
        Dt = [P.sbuf("yd_D%d" % i, [NA, 2, CG, NB], F32) for i in range(2)]
    MY = NA if FT != F32 else NA // 2
    m2 = [P.sbuf("yd_mm%d" % i, [128, CG * 128], F32) for i in range(8)]
    yout = [P.sbuf("yd_yo%d" % i, [NA // 2, LG, NB], F32) for i in range(2)]
    psD = k.ps[0:NA, 4:6, :].rearrange("p a b -> p (a b)")
    psY = k.psb[0:MY, 0, :].bitcast(F32)[:, 0:CG * NB]
    t2r = k.tw2[:, 0:NB].rearrange("p (o n) -> p o n", o=1).to_broadcast([NA, CG, NB])
    t2i = k.tw2[:, NB:2 * NB].rearrange("p (o n) -> p o n", o=1).to_broadcast([NA, CG, NB])
    cnt["i"] = 0

    def sink2(lg, gi, c0, psXr, psXi, s):
        for ri in range(2):
            P.dma(Hh[s][:, ri, :, :], k.HS[ri, :, c0:c0 + CG, :], "yd_lh%d" % s)
        Xr = psXr.rearrange("p (c k) -> p c k", c=CG)
        Xi = psXi.rearrange("p (c k) -> p c k", c=CG)
        mv = [m2[s * 4 + i][0:NB, 0:CG * NA].rearrange("p (c k) -> p c k", c=CG) for i in range(4)]
        P.tt(mv[0], Xr, Hh[s][:, 0, :, :], ALU.mult)
        yield
        P.tt(mv[1], Xi, Hh[s][:, 1, :, :], ALU.mult)
        yield
        P.tt(Yt[s][:, 0, :, :], mv[0], mv[1], ALU.subtract, eng="pool")
        P.tt(mv[2], Xr, Hh[s][:, 1, :, :], ALU.mult)
        yield
        P.tt(mv[3], Xi, Hh[s][:, 0, :, :], ALU.mult)
        yield
        P.tt(Yt[s][:, 1, :, :], mv[2], mv[3], ALU.add, eng="pool")
        yield
        for cc in range(CG):
            po = psD[:, cc * 2 * NB:(cc + 1) * 2 * NB]
            P.mm(po, Yt[s][:, 0, cc, :], k.fc[:, 0:2 * NB], start=True, stop=False)
            P.mm(po, Yt[s][:, 1, cc, :], k.fc[:, 2 * NB:4 * NB], start=False, stop=True)
        yield
        D4 = psD[:, 0:CG * 2 * NB].rearrange("p (c r n) -> p c r n", c=CG, r=2)
        Dr, Di = D4[:, :, 0, :], D4[:, :, 1, :]
        nv = [m2[s * 4 + i][0:NA, 0:CG * NB].rearrange("p (c n) -> p c n", c=CG) for i in range(4)]
        P.tt(nv[0], Dr, t2r, ALU.mult)
        yield
        P.tt(nv[1], Di, t2i, ALU.mult)
        yield
        P.tt(Dt[s][:, 0, :, :], nv[0], nv[1], ALU.subtract, eng="pool")
        P.tt(nv[2], Dr, t2i, ALU.mult)
        yield
        P.tt(nv[3], Di, t2r, ALU.mult)
        yield
        P.tt(Dt[s][:, 1, :, :], nv[2], nv[3], ALU.add, eng="pool")
        yield
        P.mm(psY, k.fa[:, 0:MY], Dt[s][:, 0, :, :].rearrange("p c n -> p (c n)"), start=True, stop=False)
        P.mm(psY, k.fa[:, NA:NA + MY], Dt[s][:, 1, :, :].rearrange("p c n -> p (c n)"), start=False, stop=True)
        yield
        yo = yout[lg % 2]
        P.act(yo[:, gi * CG:(gi + 1) * CG, :], psY[0:NA // 2, :].rearrange("p (c n) -> p c n", c=CG), AF.Copy, scale=float(1.0 / NF))
        if gi == LG // CG - 1:
            yield
            P.dma(YT[lg * LG:(lg + 1) * LG, :].rearrange("c (n1 n2) -> n1 c n2", n2=NB), yo[:], "yd_sy%d" % (lg % 2))

    k.fft_stagger = 9
    hy_fft(k, UT, NA // 2, "yd2_", sink2)
    P.barrier()
    if _HYSTOP == "E":
        return
    P.reset("hyena_E")
    skp = P.sbuf("ye_sk", [128, 16], F32)
    P.dma(skp[:], k.hy_skip.rearrange("o (j p) -> p (o j)", p=128), "ye_c", allow_slow_non_contiguous=True)
    tl = lambda nm, sh, dt=F32, n=2: [P.sbuf("ye_%s%d" % (nm, i), sh, dt) for i in range(n)]
    NL = 4
    tl = lambda nm, sh, dt=F32, n=NL: [P.sbuf("ye_%s%d" % (nm, i), sh, dt) for i in range(n)]
    yy, u2, x1t, zz, tt_ = (tl(n, [128, BL]) for n in ("yy", "u2", "x1", "zz", "tt"))
    gt = tl("gt", [128, BL], BF16)

    def bodyE(j, bi, s):
        rows = slice(j * 128, (j + 1) * 128)
        tsl = slice(bi * BL, (bi + 1) * BL)
        P.dma(yy[s][:], YT[rows, tsl], "ye_y%d" % s)
        P.dma(u2[s][:], UT[rows, tsl], "ye_u%d" % s)
        P.dma(x1t[s][:], X1[rows, tsl], "ye_x%d" % s)
        P.dma(zz[s][:], k.ZT[rows, tsl], "ye_z%d" % s)
        yield
        P.stt(tt_[s][:], u2[s][:], skp[:, j:j + 1], yy[s][:], ALU.mult, ALU.add)
        yield
        P.tt(tt_[s][:], tt_[s][:], x1t[s][:], ALU.mult, eng="pool")
        yield
        P.tt(gt[s][:], tt_[s][:], zz[s][:], ALU.mult, eng=("dve" if (j + bi) % 2 == 0 else "pool"))
        yield
        P.dma(k.QK[0, 0, rows, tsl], gt[s][:], "ye_g%d" % s)

    lanes = Lanes(NL)
    it = 0
    for j in range(16):
        for bi in range(NBK):
            lanes.push(bodyE(j, bi, it % NL))
            it += 1
    lanes.drain()
    P.barrier()
    if _HYSTOP == "F":
        return
    P.reset("hyena_F")
    st = outproj_setup(k, l, k.hy_w_out, "yf_")
    GT = [P.sbuf("yf_GT%d" % i, [128, 16, TB], BF16) for i in range(2)]
    for tb in range(NTB):
        P.dma(GT[tb % 2][:], k.QK[0, 0, :, tb * TB:(tb + 1) * TB].rearrange("(e p) t -> p e t", p=128), "yf_g%d" % (tb % 2))
        outproj_block(k, l, tb, GT[tb % 2], st, xsrc, "yf_")


_CACHE = {}


def kernel(**inputs):
    cfg = Cfg(NS=4, SL=2048, kinds=(0, 1, 2, 3))
    if "nc" not in _CACHE:
        _CACHE["nc"] = build(cfg)[0]
    nc = _CACHE["nc"]
    w = {n: np.asarray(v) for n, v in inputs.items() if n not in ("x_prompt", "x_sample", "c_prompt", "c_sample")}
    xp = np.asarray(inputs["x_prompt"], np.float32)
    xs = np.asarray(inputs["x_sample"], np.float32)
    cp = np.asarray(inputs["c_prompt"], np.float32)
    cs = np.asarray(inputs["c_sample"], np.float32)
    SL, T = cfg.SL, cfg.T
    in_maps = []
    for i in range(4):
        in_maps.append(core_inputs(cfg, [xp[i, s * SL:(s + 1) * SL] for s in range(4)], [cp[i]] * 4, 1.0, w,
                                   np.arange(T), T))
    for j in range(4):
        a, b = 2 * j, 2 * j + 1
        in_maps.append(core_inputs(cfg, [xs[a], None, xs[b], None], [cs[a], None, cs[b], None], 0.0, w,
                                   np.arange(T) % SL, SL))
    res = run_bass_kernel_spmd(nc, in_maps, core_ids=list(range(8)))
    yp = np.stack([np.asarray(res.results[i]["yout"], np.float32) for i in range(4)], 0)
    ys = np.zeros(xs.shape, np.float32)
    for j in range(4):
        yo = np.asarray(res.results[4 + j]["yout"], np.float32)
        ys[2 * j] = yo[0:SL]
        ys[2 * j + 1] = yo[2 * SL:3 * SL]
    return (yp, ys)
```

```python
import numpy as np
from contextlib import ExitStack
import concourse.bass as bass
import concourse.mybir as mybir
from concourse.bass_utils import run_bass_kernel_spmd

F32 = mybir.dt.float32
BF16 = mybir.dt.bfloat16
I32 = mybir.dt.int32
ALU = mybir.AluOpType
AF = mybir.ActivationFunctionType

D = 1024
E = 2048
CH = 64
TB = 512
EPS = 1e-6

def _isz(dt):
    return mybir.dt.size(dt)


def _region(ap):
    name = ap.tensor.name
    pat = ap.ap
    off = int(ap.offset)
    space = str(ap.space)
    z = _isz(ap.dtype)
    if space in ("SB", "PSUM"):
        pstride = pat[0][0]
        p_lo = off // pstride
        p_hi = p_lo + pat[0][1]
        base = off % pstride
        lo = base
        hi = base
        for st, cn in pat[1:]:
            ext = st * (cn - 1)
            if ext < 0:
                lo += ext
            else:
                hi += ext
        return (name, p_lo, p_hi, lo * z, (hi + 1) * z)
    pitch = int(ap.tensor.shape[-1])
    r_lo = r_hi = off // pitch
    c_lo = c_hi = off % pitch
    for st, cn in pat:
        ext = st * (cn - 1)
        if abs(st) >= pitch and st % pitch == 0:
            e = ext // pitch
            if e < 0:
                r_lo += e
            else:
                r_hi += e
        else:
            if ext < 0:
                c_lo += ext
            else:
                c_hi += ext
    if c_lo < 0 or c_hi >= pitch:
        r_lo += c_lo // pitch
        r_hi += c_hi // pitch
        c_lo, c_hi = 0, pitch - 1
    return (name, r_lo, r_hi + 1, c_lo, c_hi + 1)


def _overlap(a, b):
    return a[1] < b[2] and b[1] < a[2] and a[3] < b[4] and b[3] < a[4]


def _contains(a, b):
    return a[1] <= b[1] and b[2] <= a[2] and a[3] <= b[3] and b[4] <= a[4]


class Op:
    __slots__ = ("eng", "fn", "deps", "stream", "signal", "val", "sem", "dma_snap", "phase")

    def __init__(self, eng, fn, deps, stream):
        self.eng = eng
        self.fn = fn
        self.deps = deps
        self.stream = stream
        self.signal = False
        self.val = 0
        self.sem = None
        self.dma_snap = None


class Prog:
    ENGS = ("pe", "dve", "act", "pool", "sp")

    def __init__(self, nc):
        self.nc = nc
        self.ops = []
        self.acc = {}
        self.stack = ExitStack()
        self.n = 0
        self.barrier_at = []

    ARENA = 150 * 1024
    RSV = 40 * 1024

    def sbuf_r(self, name, shape, dt):
        t = self.stack.enter_context(self.nc.sbuf_tensor(name, list(shape), dt))
        return t[tuple(slice(None) for _ in shape)]

    def sbuf(self, name, shape, dt):
        if not hasattr(self, "arena"):
            self.arena = self.stack.enter_context(self.nc.sbuf_tensor("arena", [128, self.ARENA], mybir.dt.uint8))
            self.bump = 0
            self.water = 0
            self.rbump = self.ARENA
        n = 1
        for d in shape[1:]:
            n *= d
        size = (n * _isz(dt) + 63) // 64 * 64
        off = self.bump
        self.bump += size
        assert self.bump <= self.ARENA, ("SBUF overflow", name, self.bump)
        v = self.arena[0:shape[0], off:off + n * _isz(dt)].bitcast(dt)
        if len(shape) == 3:
            v = v.rearrange("p (a b) -> p a b", a=shape[1])
        elif len(shape) == 4:
            v = v.rearrange("p (a b c) -> p a b c", a=shape[1], b=shape[2])
        return v

    def mark(self):
        self.water = self.bump

    def reset(self, label=None):
        self.bump = self.water
        self.rbump = self.ARENA
        self.nphase = getattr(self, "nphase", 0) + 1
        self.cur_phase = "%02d_%s" % (self.nphase, label or "")

    def psum(self, name, shape, dt=F32):
        return self.stack.enter_context(self.nc.psum_tensor(name, list(shape), dt))

    def dram(self, name, shape, dt, kind="Internal"):
        return self.nc.dram_tensor(name, list(shape), dt, kind=kind).ap()

    def _deps(self, reg, is_write, idx, eng, is_dma):
        lst = self.acc.setdefault(reg[0], [])
        deps = []
        keep = []
        for (r, j, w, e, d) in lst:
            if _overlap(r, reg):
                if is_write or w:
                    same = (e == eng) and not d and not is_dma
                    if same:
                        if w and not is_write and eng != "pe":
                            deps.append(j)
                    else:
                        deps.append(j)
                if is_write and _contains(reg, r):
                    continue
                if (not is_write) and (not w) and e == eng and not d and not is_dma and r == reg:
                    continue
            keep.append((r, j, w, e, d))
        keep.append((reg, idx, is_write, eng, is_dma))
        self.acc[reg[0]] = keep
        return deps

    def add(self, eng, fn, reads=(), writes=(), stream=None, extra_deps=()):
        idx = len(self.ops)
        is_dma = stream is not None
        deps = set(extra_deps)
        for ap in reads:
            if ap is None:
                continue
            deps.update(self._deps(_region(ap), False, idx, eng, is_dma))
        for ap in writes:
            if ap is None:
                continue
            deps.update(self._deps(_region(ap), True, idx, eng, is_dma))
        deps.discard(idx)
        self.ops.append(Op(eng, fn, deps, stream))
        self.ops[-1].phase = getattr(self, "cur_phase", "00")
        return idx

    def barrier(self):
        self.barrier_at.append(len(self.ops))
        self.acc = {}

    def dma(self, out, in_, stream, q="sp", **kw):
        return self.add(q, lambda e: e.dma_start(out=out, in_=in_, **kw), [in_], [out], stream=stream)

    def mm(self, out, lhsT, rhs, start=True, stop=True, extra_reads=()):
        return self.add("pe", lambda e: e.matmul(out, lhsT, rhs, start=start, stop=stop),
                        [lhsT, rhs] + list(extra_reads) + ([] if start else [out]), [out])

    def transpose(self, out, in_, ident):
        return self.add("pe", lambda e: e.transpose(out, in_, ident), [in_, ident], [out])

    def act(self, out, in_, func, bias=0.0, scale=1.0, accum_out=None, eng="act"):
        rd = [in_]
        if not isinstance(bias, (int, float)):
            rd.append(bias)
        if not isinstance(scale, (int, float)):
            rd.append(scale)
        wr = [out]
        kw = {}
        if accum_out is not None:
            wr.append(accum_out)
            kw["accum_out"] = accum_out
        return self.add(eng, lambda e: e.activation(out, in_, func, bias=bias, scale=scale, **kw), rd, wr)

    def tt(self, out, a, b, op, eng="dve"):
        return self.add(eng, lambda e: e.tensor_tensor(out, a, b, op), [a, b], [out])

    def ts(self, out, a, s1, s2, op0, op1=None, eng="dve", accum_out=None):
        rd = [a]
        if not isinstance(s1, (int, float)):
            rd.append(s1)
        if s2 is not None and not isinstance(s2, (int, float)):
            rd.append(s2)
        wr = [out]
        kw = {}
        if accum_out is not None:
            wr.append(accum_out)
            kw["accum_out"] = accum_out
        if op1 is None:
            return self.add(eng, lambda e: e.tensor_scalar(out, a, s1, None, op0, **kw), rd, wr)
        return self.add(eng, lambda e: e.tensor_scalar(out, a, s1, s2, op0, op1, **kw), rd, wr)

    def stt(self, out, a, s, b, op0, op1, eng="dve"):
        rd = [a, b]
        if not isinstance(s, (int, float)):
            rd.append(s)
        return self.add("dve", lambda e: e.scalar_tensor_tensor(out, a, s, b, op0, op1), rd, [out])

    def copy(self, out, in_, eng="dve"):
        if eng == "act":
            return self.add(eng, lambda e: e.copy(out, in_), [in_], [out])
        return self.add(eng, lambda e: e.tensor_copy(out, in_), [in_], [out])

    def memset(self, out, v, eng="pool"):
        return self.add(eng, lambda e: e.memset(out, v), [], [out])

    def scan(self, out, d0, d1, init, op0=ALU.mult, op1=ALU.add):
        rd = [d0, d1]
        if not isinstance(init, (int, float)):
            rd.append(init)
        return self.add("dve", lambda e: e.tensor_tensor_scan(out, d0, d1, init, op0, op1), rd, [out])

    def finalize(self, final_streams_wait=True):
        nc = self.nc
        engs = {"pe": nc.tensor, "dve": nc.vector, "act": nc.scalar, "pool": nc.gpsimd, "sp": nc.sync}
        ops = self.ops
        for op in ops:
            for j in op.deps:
                if ops[j].stream is None:
                    ops[j].signal = True
        last_before = []
        for b in self.barrier_at:
            lb = {}
            for i in range(b - 1, -1, -1):
                o = ops[i]
                if o.stream is None and o.eng not in lb:
                    lb[o.eng] = i
                    if len(lb) == 5:
                        break
            for i in lb.values():
                ops[i].signal = True
            last_before.append(lb)
        sems = {}

        def getsem(key):
            if key not in sems:
                sems[key] = self.stack.enter_context(nc.semaphore("s_" + key))
            return sems[key]

        cnt = {}
        stream_hist = {}
        active = {}
        freep = []
        nphys = 0
        bset = set(self.barrier_at)
        for i, op in enumerate(ops):
            if i in bset:
                freep.extend(sorted(active.values()))
                active = {}
            if op.stream is not None:
                if op.stream not in active:
                    if freep:
                        active[op.stream] = freep.pop(0)
                    else:
                        active[op.stream] = nphys
                        nphys += 1
                k = "d%d" % active[op.stream]
                cnt[k] = cnt.get(k, 0) + 16
                op.sem = k
                op.val = cnt[k]
            elif op.signal:
                k = "e_" + op.eng
                cnt[k] = cnt.get(k, 0) + 1
                op.sem = k
                op.val = cnt[k]
        waited = {e: {} for e in self.ENGS}
        self.iname = {}
        stream_cnt = {}
        bi = 0
        barrier_pending = {e: None for e in self.ENGS}
        for i, op in enumerate(ops):
            while bi < len(self.barrier_at) and self.barrier_at[bi] <= i:
                snap = {}
                for e, j in last_before[bi].items():
                    snap[ops[j].sem] = ops[j].val
                for k, v in stream_cnt.items():
                    snap[k] = v
                for e in self.ENGS:
                    barrier_pending[e] = dict(snap) if barrier_pending[e] is None else {**barrier_pending[e], **snap}
                bi += 1
            eng = engs[op.eng]
            need = {}
            if barrier_pending[op.eng] is not None:
                need.update(barrier_pending[op.eng])
                barrier_pending[op.eng] = None
            for j in op.deps:
                d = ops[j]
                if d.stream is not None:
                    v = stream_cnt[d.sem]
                else:
                    v = d.val
                if need.get(d.sem, 0) < v:
                    need[d.sem] = v
            w = waited[op.eng]
            for k, v in need.items():
                if k == "e_" + op.eng and op.eng == "pe":
                    continue
                if w.get(k, 0) < v:
                    eng.wait_ge(getsem(k), v)
                    w[k] = v
            ins = op.fn(eng)
            try:
                self.iname[ins.ins.name] = op.phase
            except Exception:
                pass
            if op.stream is not None:
                ins.then_inc(getsem(op.sem), 16)
                stream_cnt[op.sem] = op.val
            elif op.signal:
                ins.then_inc(getsem(op.sem), 1)
        if final_streams_wait:
            for k, v in stream_cnt.items():
                if waited["sp"].get(k, 0) < v:
                    nc.sync.wait_ge(getsem(k), v)
        self.nsems = len(sems)
        self.counts = cnt


class Lanes:
    def __init__(self, width, stagger=0):
        self.width = width
        self.stagger = stagger
        self.active = []

    def step(self):
        for g in list(self.active):
            try:
                next(g)
            except StopIteration:
                self.active.remove(g)

    def push(self, gen):
        if gen is None:
            return
        while len(self.active) >= self.width:
            self.step()
        self.active.append(gen)
        for _ in range(self.stagger):
            self.step()

    def drain(self):
        while self.active:
            self.step()


class Cfg:
    def __init__(self, NS=4, SL=2048, kinds=(0, 1, 2, 3)):
        self.NS = NS
        self.SL = SL
        self.T = NS * SL
        self.NCH = self.T // CH
        self.NTB = self.T // TB
        self.kinds = tuple(kinds)
        self.depth = len(kinds)
        self.NB = 128
        self.NA = (2 * self.T) // 128


class K:
    pass


def bc_rows(ap_row, n):
    return ap_row.to_broadcast([n, ap_row.shape[-1]])


def build(cfg):
    nc = bass.Bass("TRN2", target_bir_lowering=False)
    P = Prog(nc)
    k = K()
    k.P = P
    k.cfg = cfg
    T, NS, SL, NCH, NTB = cfg.T, cfg.NS, cfg.SL, cfg.NCH, cfg.NTB
    dp = cfg.depth
    din = lambda n, s, dt=F32: P.dram(n, s, dt, "ExternalInput")
    k.xin = din("xin", [T, D])
    k.cin = din("cin", [NS, D])
    k.ada_w = din("ada_w", [dp, D, 3 * D])
    k.ada_b = din("ada_b", [dp, 3 * D])
    k.norm_g = din("norm_g", [dp, D])
    k.final_g = din("final_g", [1, D])
    k.hg_lb = din("hg_lb", [5, E])
    k.hg_w_in = din("hg_w_in", [D, 5 * E])
    k.hg_norm_g = din("hg_norm_g", [1, 128])
    k.hg_w_out = din("hg_w_out", [E, D])
    if 2 in cfg.kinds:
        k.rt_w_in = din("rt_w_in", [D, 6144])
        k.rt_gn_g = din("rt_gn_g", [1, E])
        k.rt_w_out = din("rt_w_out", [E, D])
        k.k_cos = din("k_cos", [128, T])
        k.k_sin = din("k_sin", [128, T])
        k.k_dtab = din("k_dtab", [128, 16 * 64])
        k.k_rtsc = din("k_rtsc", [2, 3, 1024, NCH])
    if 3 in cfg.kinds:
        k.lru_w_in = din("lru_w_in", [D, 2 * E])
        k.lru_conv_w = din("lru_conv_w", [4, E])
        k.lru_conv_b = din("lru_conv_b", [1, E])
        k.lru_gate_w = din("lru_gate_w", [2, 2, 16, 128, 128])
        k.lru_gate_b = din("lru_gate_b", [2, 2, E])
        k.lru_lambda = din("lru_lambda", [2, E])
        k.lru_w_out = din("lru_w_out", [E, D])
    if 1 in cfg.kinds:
        NA, NB = cfg.NA, cfg.NB
        NF = 2 * T
        k.hy_w_in = din("hy_w_in", [D, 4 * E])
        k.hy_b_in = din("hy_b_in", [1, 4 * E])
        k.hy_conv_w = din("hy_conv_w", [3, 3 * E])
        k.hy_conv_b = din("hy_conv_b", [1, 3 * E])
        k.hy_f_w1 = din("hy_f_w1", [33, 64])
        k.hy_f_b1 = din("hy_f_b1", [1, 64])
        k.hy_f_w2 = din("hy_f_w2", [2, 64, 64])
        k.hy_f_b2 = din("hy_f_b2", [2, 64])
        k.hy_f_wout = din("hy_f_wout", [64, 2 * E])
        k.hy_f_freq = din("hy_f_freq", [1, 64])
        k.hy_skip = din("hy_skip", [1, E])
        k.hy_w_out = din("hy_w_out", [E, D])
        k.k_fa = din("k_fa", [NA, 2 * NA])
        k.k_fb = din("k_fb", [NB, 3 * NB])
        k.k_fc = din("k_fc", [NB, 4 * NB])
        k.k_tw1 = din("k_tw1", [NB, 2 * NA])
        k.k_tw2 = din("k_tw2", [NA, 2 * NB])
        k.k_zT = din("k_zT", [33, NF])
        k.k_fmask = din("k_fmask", [2, NF])
        k.k_trow = din("k_trow", [1, NF])
        k.k_delta = din("k_delta", [128, 16])
        k.k_smask = din("k_smask", [128, NS])
        k.PR = P.dram("PR", [3 * E, T], F32)
        k.TAPS = P.dram("TAPS", [E, NF], F32)
        k.HS = P.dram("HS", [2, NB, E, NA], F32)
    k.k_carry1 = din("k_carry1", [128, 1])
    k.k_ident = din("k_ident", [128, 128])
    k.k_tri = din("k_tri", [64, 128])
    k.k_rmask = din("k_rmask", [128, TB])
    k.k_carry = din("k_carry", [128, 2 * NCH])
    k.yout = P.dram("yout", [T, D], F32, "ExternalOutput")
    k.X = P.dram("X", [T, D], F32)
    k.MOD = P.dram("MOD", [dp, NS, 3 * D], F32)
    k.HT = P.dram("HT", [D, T], BF16)
    k.QT = P.dram("QT", [E, T], F32)
    k.ZT = P.dram("ZT", [E, T], F32)
    k.QK = P.dram("QK", [2, 2, E, T], BF16)
    k.SC = P.dram("SC", [2, 3, E, NCH], F32)
    k.V = P.dram("V", [T, E], BF16)
    k.OT = P.dram("OT", [2, E, T], F32)
    k.ident = P.sbuf("ident", [128, 128], F32)
    k.identb = P.sbuf("identb", [128, 128], BF16)
    k.ones = P.sbuf("ones", [128, 128], F32)
    k.tri = P.sbuf("tri", [64, 128], F32)
    k.rmask = P.sbuf("rmask", [128, TB], F32)
    k.carry = P.sbuf("carry", [128, 2 * NCH], F32)
    k.epsT = P.sbuf("epsT", [128, 1], F32)
    P.dma(k.ident[:], k.k_ident, "c0")
    P.dma(k.tri[:], k.k_tri, "c1")
    P.dma(k.rmask[:], k.k_rmask, "c2")
    P.dma(k.carry[:], k.k_carry, "c3")
    k.carry1 = P.sbuf("carry1", [128, 1], F32)
    P.dma(k.carry1[:], k.k_carry1, "c4")
    P.copy(k.identb[:], k.ident[:])
    P.memset(k.ones[:], 1.0)
    P.memset(k.epsT[:], EPS)
    k.ps = P.psum("ps", [128, 6, 512], F32)
    k.psb = P.psum("psb", [128, 2, 1024], BF16)

    P.mark()
    phase_mod(k)
    P.barrier()
    for l, kind in enumerate(cfg.kinds):
        xsrc = k.xin if l == 0 else k.X
        phase_norm(k, l, xsrc)
        P.barrier()
        if kind == 0:
            hgrn2(k, l, xsrc)
        elif kind == 1:
            hyena(k, l, xsrc)
        elif kind == 2:
            retention(k, l, xsrc)
        elif kind == 3:
            rglru(k, l, xsrc)
        else:
            raise NotImplementedError
        P.barrier()
    phase_final(k, k.xin if dp == 0 else k.X)
    P.finalize()
    return nc, P


def phase_mod(k):
    P, cfg = k.P, k.cfg
    P.reset("phase_mod_")
    NS = cfg.NS
    cT = P.sbuf("m_cT", [128, 8, NS], F32)
    sg = P.sbuf("m_sg", [128, 8, NS], F32)
    csT = P.sbuf("m_csT", [128, 8, NS], F32)
    for kt in range(8):
        P.dma(cT[:, kt, :], k.cin[:, kt * 128:(kt + 1) * 128].rearrange("s p -> p s"), "m_c",
              allow_slow_non_contiguous=True)
    P.act(sg[:], cT[:], AF.Sigmoid)
    P.tt(csT[:], cT[:], sg[:], ALU.mult)
    w = [P.sbuf("m_w%d" % i, [128, 8, 512], F32) for i in range(2)]
    bb = [P.sbuf("m_b%d" % i, [NS, 512], F32) for i in range(2)]
    ob = [P.sbuf("m_o%d" % i, [NS, 512], F32) for i in range(2)]
    it = 0
    for l in range(cfg.depth):
        for cb in range(6):
            s = it % 2
            P.dma(w[s][:], k.ada_w[l, :, cb * 512:(cb + 1) * 512].rearrange("(kt p) n -> p kt n", p=128), "m_w%d" % s)
            P.dma(bb[s][:], bc_rows(k.ada_b[l:l + 1, cb * 512:(cb + 1) * 512], NS), "m_b%d" % s)
            pb = k.ps[0:NS, it % 2, :]
            for kt in range(8):
                P.mm(pb, csT[:, kt, :], w[s][:, kt, :], start=(kt == 0), stop=(kt == 7))
            P.tt(ob[s][:], pb, bb[s][:], ALU.add)
            P.dma(k.MOD[l, :, cb * 512:(cb + 1) * 512], ob[s][:], "m_o%d" % s)
            it += 1


def rms_rstd(P, k, xt, junk, ss, rstd):
    P.act(junk, xt, AF.Square, accum_out=ss)
    P.act(rstd, ss, AF.Sqrt, bias=k.epsT[:], scale=1.0 / D)
    P.add("dve", lambda e: e.reciprocal(rstd, rstd), [rstd], [rstd])


def phase_norm(k, l, xsrc):
    P, cfg = k.P, k.cfg
    P.reset("phase_norm_")
    T, NS, SL = cfg.T, cfg.NS, cfg.SL
    g_bc = P.sbuf("n_g", [128, D], F32)
    A_bc = P.sbuf("n_A", [128, D], F32)
    sh_bc = P.sbuf("n_sh", [128, D], F32)
    xt = [P.sbuf("n_x%d" % i, [128, D], F32) for i in range(2)]
    junk = P.sbuf("n_junk", [128, D], F32)
    hb = [P.sbuf("n_h%d" % i, [128, D], BF16) for i in range(2)]
    hT = [P.sbuf("n_hT%d" % i, [128, 8, TB], BF16) for i in range(2)]
    ss = [P.sbuf("n_ss%d" % i, [128, 1], F32) for i in range(2)]
    rstd = [P.sbuf("n_rs%d" % i, [128, 1], F32) for i in range(2)]
    P.dma(g_bc[:], bc_rows(k.norm_g[l:l + 1, :], 128), "n_g")
    ntile = T // 128
    P.dma(xt[0][:], xsrc[0:128, :], "n_x0")
    for i in range(ntile):
        s = i % 2
        slot = (i * 128) // SL
        if (i * 128) % SL == 0:
            P.dma(A_bc[:], bc_rows(k.MOD[l, slot:slot + 1, D:2 * D], 128), "n_A")
            P.dma(sh_bc[:], bc_rows(k.MOD[l, slot:slot + 1, 0:D], 128), "n_sh")
            P.stt(A_bc[:], A_bc[:], 1.0, g_bc[:], ALU.add, ALU.mult)
        if i + 1 < ntile:
            P.dma(xt[1 - s][:], xsrc[(i + 1) * 128:(i + 2) * 128, :], "n_x%d" % (1 - s))
        rms_rstd(P, k, xt[s][:], junk[:], ss[s][:], rstd[s][:])
        P.stt(junk[:], xt[s][:], rstd[s][:], A_bc[:], ALU.mult, ALU.mult)
        P.tt(hb[s][:], junk[:], sh_bc[:], ALU.add, eng="pool")
        tb, sub = divmod(i, 4)
        hs = tb % 2
        for kt in range(8):
            pst = k.psb[:, kt % 2, (kt // 2) * 128:(kt // 2) * 128 + 128] if False else k.psb[:, kt // 4, (kt % 4) * 128:(kt % 4) * 128 + 128]
            P.transpose(pst, hb[s][:, kt * 128:(kt + 1) * 128], k.identb[:])
        for half in range(2):
            src = k.psb[:, half, 0:512].rearrange("p (a b) -> p a b", a=4)
            dst = hT[hs][:, half * 4:half * 4 + 4, sub * 128:(sub + 1) * 128]
            P.copy(dst, src, eng=("act" if half == 0 else "dve"))
        if sub == 3:
            P.dma(k.HT[:, tb * TB:(tb + 1) * TB].rearrange("(kt p) t -> p kt t", p=128), hT[hs][:], "n_hT%d" % hs)


def phase_final(k, xsrc):
    P, cfg = k.P, k.cfg
    P.reset("phase_final_")
    T = cfg.T
    g_bc = P.sbuf("f_g", [128, D], F32)
    xt = [P.sbuf("f_x%d" % i, [128, D], F32) for i in range(2)]
    yt = [P.sbuf("f_y%d" % i, [128, D], F32) for i in range(2)]
    junk = P.sbuf("f_junk", [128, D], F32)
    ss = [P.sbuf("f_ss%d" % i, [128, 1], F32) for i in range(2)]
    rstd = [P.sbuf("f_rs%d" % i, [128, 1], F32) for i in range(2)]
    P.dma(g_bc[:], bc_rows(k.final_g[0:1, :], 128), "f_g")
    ntile = T // 128
    P.dma(xt[0][:], xsrc[0:128, :], "f_x0")
    for i in range(ntile):
        s = i % 2
        if i + 1 < ntile:
            P.dma(xt[1 - s][:], xsrc[(i + 1) * 128:(i + 2) * 128, :], "f_x%d" % (1 - s))
        rms_rstd(P, k, xt[s][:], junk[:], ss[s][:], rstd[s][:])
        P.stt(yt[s][:], xt[s][:], rstd[s][:], g_bc[:], ALU.mult, ALU.mult)
        P.dma(k.yout[i * 128:(i + 1) * 128, :], yt[s][:], "f_y%d" % s)


def gemm_fm(k, W, groups, epi, pfx):
    P, cfg = k.P, k.cfg
    NTB = cfg.NTB
    wst = [P.sbuf(pfx + "wst%d" % i, [128, 8, 512], F32) for i in range(2)]
    wbf = [P.sbuf(pfx + "wbf%d" % i, [128, 8, 512], BF16) for i in range(2)]
    hT = [P.sbuf(pfx + "hT%d" % i, [128, 8, TB], BF16) for i in range(2)]

    def loadw(gi):
        s = gi % 2
        for j, (col, tag) in enumerate(groups[gi]):
            P.dma(wst[s][:, :, j * 128:(j + 1) * 128], W[:, col:col + 128].rearrange("(kt p) n -> p kt n", p=128),
                  pfx + "w%d" % s)
        ncol = 128 * len(groups[gi])
        P.copy(wbf[s][:, :, 0:ncol], wst[s][:, :, 0:ncol], eng="pool")

    items = [(gi, tb) for gi in range(len(groups)) for tb in range(NTB)]

    def loadh(ii):
        gi, tb = items[ii]
        P.dma(hT[ii % 2][:], k.HT[:, tb * TB:(tb + 1) * TB].rearrange("(kt p) t -> p kt t", p=128), pfx + "h%d" % (ii % 2))

    loadw(0)
    loadh(0)
    it = 0
    lanes = Lanes(3)
    for ii, (gi, tb) in enumerate(items):
        if ii + 1 < len(items):
            loadh(ii + 1)
        if tb == 0 and gi + 1 < len(groups):
            loadw(gi + 1)
        for j, (col, tag) in enumerate(groups[gi]):
            pb = k.ps[:, it % 4, :]
            for kt in range(8):
                P.mm(pb, wbf[gi % 2][:, kt, j * 128:(j + 1) * 128], hT[ii % 2][:, kt, :], start=(kt == 0), stop=(kt == 7))
            lanes.push(epi(tag, col, tb, pb, it))
            it += 1
    lanes.drain()


def gemm_tm(k, W, col0, ncols, dst, pfx):
    P, cfg = k.P, k.cfg
    NTB = cfg.NTB
    wst = [P.sbuf(pfx + "wst%d" % i, [128, 8, 512], F32) for i in range(2)]
    wbf = [P.sbuf(pfx + "wbf%d" % i, [128, 8, 512], BF16) for i in range(2)]
    hT = [P.sbuf(pfx + "hT%d" % i, [128, 8, TB], BF16) for i in range(2)]
    vt = [P.sbuf(pfx + "vt%d" % i, [128, 512], BF16) for i in range(2)]
    ng = ncols // 512

    def loadw(gi):
        s = gi % 2
        P.dma(wst[s][:], W[:, col0 + gi * 512:col0 + (gi + 1) * 512].rearrange("(kt p) n -> p kt n", p=128), pfx + "w%d" % s)
        P.copy(wbf[s][:], wst[s][:], eng="pool")

    items = [(gi, tb) for gi in range(ng) for tb in range(NTB)]

    def loadh(ii):
        gi, tb = items[ii]
        P.dma(hT[ii % 2][:], k.HT[:, tb * TB:(tb + 1) * TB].rearrange("(kt p) t -> p kt t", p=128), pfx + "h%d" % (ii % 2))

    loadw(0)
    loadh(0)
    it = 0
    for ii, (gi, tb) in enumerate(items):
        if ii + 1 < len(items):
            loadh(ii + 1)
        if tb == 0 and gi + 1 < ng:
            loadw(gi + 1)
        for sub in range(4):
            pb = k.ps[:, 4 + it % 2, :]
            for kt in range(8):
                P.mm(pb, hT[ii % 2][:, kt, sub * 128:(sub + 1) * 128], wbf[gi % 2][:, kt, :], start=(kt == 0), stop=(kt == 7))
            P.copy(vt[it % 2][:], pb, eng=("act" if it % 2 == 0 else "dve"))
            r0 = tb * TB + sub * 128
            P.dma(dst[r0:r0 + 128, gi * 512:(gi + 1) * 512], vt[it % 2][:], pfx + "v%d" % (it % 2))
            it += 1


def chunk_engine(k, ND, NV, NU, G, pfx, SCsrc=None):
    P, cfg = k.P, k.cfg
    NCH = cfg.NCH
    SCsrc = k.SC if SCsrc is None else SCsrc
    CB = 4
    BW = CB * CH
    NTB = NCH // CB
    NG = NU // G
    GD = G * ND
    GV = G * NV
    VW = NV * 128
    S = [P.sbuf(pfx + "S%d" % g, [128, GD, VW], F32) for g in range(NG)]
    Sbf = [P.sbuf(pfx + "Sb%d" % g, [128, GD, VW], BF16) for g in range(NG)]
    t1 = [P.sbuf(pfx + "t1%d" % i, [128, GD, VW], F32) for i in range(2)]
    t2 = [P.sbuf(pfx + "t2%d" % i, [128, GD, VW], F32) for i in range(2)]
    PT = [P.sbuf(pfx + "PT%d" % i, [64, G, 64], BF16) for i in range(2)]
    ktok = [P.sbuf(pfx + "kt%d" % i, [64, GD * 128], BF16) for i in range(2)]
    qT = [[P.sbuf(pfx + "q%d_%d" % (b, g), [128, GD, BW], BF16) for g in range(NG)] for b in range(2)]
    kT = [[P.sbuf(pfx + "k%d_%d" % (b, g), [128, GD, BW], BF16) for g in range(NG)] for b in range(2)]
    vb = [[P.sbuf(pfx + "v%d_%d" % (b, g), [64, CB, GV * 128], BF16) for g in range(NG)] for b in range(2)]
    sc = [[P.sbuf(pfx + "s%d_%d" % (b, g), [128, 3, GD, CB], F32) for g in range(NG)] for b in range(2)]
    ob = [[P.sbuf(pfx + "o%d_%d" % (b, g), [128, GV, BW], F32) for g in range(NG)] for b in range(2)]

    def load(dr, bi, blk):
        b = bi % 2
        for g in range(NG):
            r0 = g * GD * 128
            tsl = slice(blk * BW, (blk + 1) * BW)
            P.dma(qT[b][g][:], k.QK[dr, 0, r0:r0 + GD * 128, tsl].rearrange("(j p) t -> p j t", p=128), pfx + "lq%d_%d" % (b, g))
            P.dma(kT[b][g][:], k.QK[dr, 1, r0:r0 + GD * 128, tsl].rearrange("(j p) t -> p j t", p=128), pfx + "lk%d_%d" % (b, g))
            c0 = g * GV * 128
            P.dma(vb[b][g][:], k.V[tsl, c0:c0 + GV * 128].rearrange("(c s) n -> s c n", s=64), pfx + "lv%d_%d" % (b, g))
            for j3 in range(3):
                P.dma(sc[b][g][:, j3, :, :], SCsrc[dr, j3, r0:r0 + GD * 128, blk * CB:(blk + 1) * CB].rearrange("(j p) c -> p j c", p=128),
                      pfx + "ls%d_%d" % (b, g), allow_slow_non_contiguous=True)

    def body(dr, b, c, g, it):
        csl = slice(c * 64, (c + 1) * 64)
        p2 = it % 2
        psA = k.ps[:, p2, :]
        psS = k.ps[:, 2 + 2 * p2:4 + 2 * p2, :] if GD * VW > 512 else k.ps[:, 2 + p2:3 + p2, :]
        psS = psS.rearrange("p a b -> p (a b)")
        pstr = k.psb[0:64, p2, 0:GD * 128]
        for u in range(G):
            for j in range(ND):
                P.mm(psA[0:64, u * 64:(u + 1) * 64], kT[b][g][:, u * ND + j, csl], qT[b][g][:, u * ND + j, csl],
                     start=(j == 0), stop=(j == ND - 1))
        for uj in range(GD):
            P.transpose(pstr[:, uj * 128:(uj + 1) * 128], kT[b][g][:, uj, csl], k.identb[:])
        yield
        trim = k.tri[:, dr * 64:(dr + 1) * 64]
        P.tt(PT[p2][:], psA[0:64, 0:G * 64].rearrange("p (u t) -> p u t", u=G),
             trim.rearrange("p (o t) -> p o t", o=1).to_broadcast([64, G, 64]), ALU.mult)
        P.copy(ktok[p2][:], pstr, eng="act")
        if ND == 1:
            P.tt(Sbf[g][:], S[g][:], sc[b][g][:, 0, :, c:c + 1].to_broadcast([128, GD, VW]), ALU.mult, eng="pool")
        else:
            for uj in range(GD):
                P.act(Sbf[g][:, uj, :], S[g][:, uj, :], AF.Copy, scale=sc[b][g][:, 0, uj, c:c + 1])
        yield
        for u in range(G):
            for i in range(NV):
                po = psA[:, 256 + (u * NV + i) * 64:256 + (u * NV + i + 1) * 64]
                P.mm(po, vb[b][g][:, c, (u * NV + i) * 128:(u * NV + i + 1) * 128], PT[p2][:, u, :], start=True, stop=False)
                for j in range(ND):
                    P.mm(po, Sbf[g][:, u * ND + j, i * 128:(i + 1) * 128], qT[b][g][:, u * ND + j, csl],
                         start=False, stop=(j == ND - 1))
        for uj in range(GD):
            u = uj // ND
            P.mm(psS[:, uj * VW:(uj + 1) * VW], ktok[p2][:, uj * 128:(uj + 1) * 128],
                 vb[b][g][:, c, u * VW:(u + 1) * VW], start=True, stop=True)
        yield
        P.copy(ob[b][g][:, :, csl], psA[:, 256:256 + GV * 64].rearrange("p (a t) -> p a t", a=GV), eng="act")
        if ND == 1:
            P.tt(t1[p2][:], S[g][:], sc[b][g][:, 1, :, c:c + 1].to_broadcast([128, GD, VW]), ALU.mult, eng="pool")
            yield
            P.tt(t2[p2][:], psS.rearrange("p (a v) -> p a v", a=GD),
                 sc[b][g][:, 2, :, c:c + 1].to_broadcast([128, GD, VW]), ALU.mult)
            yield
            P.tt(S[g][:], t1[p2][:], t2[p2][:], ALU.add)
        else:
            for uj in range(GD):
                P.tt(t1[p2][:, uj, :], S[g][:, uj, :], sc[b][g][:, 1, uj, c:c + 1].to_broadcast([128, VW]), ALU.mult, eng="pool")
            yield
            for uj in range(GD):
                P.stt(S[g][:, uj, :], psS[:, uj * VW:(uj + 1) * VW], sc[b][g][:, 2, uj, c:c + 1], t1[p2][:, uj, :], ALU.mult, ALU.add)
                yield

    it = 0
    for dr in range(2):
        for g in range(NG):
            P.memset(S[g][:], 0.0)
        blks = list(range(NTB)) if dr == 0 else list(range(NTB - 1, -1, -1))
        load(dr, 0, blks[0])
        for bi, blk in enumerate(blks):
            b = bi % 2
            if bi + 1 < len(blks):
                load(dr, bi + 1, blks[bi + 1])
            cs = list(range(CB)) if dr == 0 else list(range(CB - 1, -1, -1))
            for c in cs:
                lanes = Lanes(2)
                for g in range(NG):
                    lanes.push(body(dr, b, c, g, it))
                    it += 1
                lanes.drain()
            for g in range(NG):
                r0 = g * GV * 128
                P.dma(k.OT[dr, r0:r0 + GV * 128, blk * BW:(blk + 1) * BW].rearrange("(i p) t -> p i t", p=128), ob[b][g][:],
                      pfx + "so%d_%d" % (b, g))


def outproj_setup(k, l, Wout, pfx):
    P = k.P
    wo = P.sbuf(pfx + "wo", [128, 16, D], BF16)
    st = K()
    st.wo = wo
    st.gate = P.sbuf(pfx + "gate", [128, D], F32)
    st.xt = [P.sbuf(pfx + "x%d" % i, [128, D], F32) for i in range(2)]
    st.ty = [P.sbuf(pfx + "ty%d" % i, [128, D], F32) for i in range(2)]
    st.it = 0
    mark = P.bump
    wst = [P.sbuf(pfx + "wos%d" % i, [128, 16, 256], F32) for i in range(2)]
    for q in range(4):
        P.dma(wst[q % 2][:], Wout[:, q * 256:(q + 1) * 256].rearrange("(e p) n -> p e n", p=128), pfx + "wo%d" % (q % 2))
        P.copy(wo[:, :, q * 256:(q + 1) * 256], wst[q % 2][:], eng="pool")
    P.bump = mark
    return st


def outproj_gen(k, l, tb, GT, st, xsrc, pfx):
    P, cfg = k.P, k.cfg
    if (tb * TB) % cfg.SL == 0:
        slot = (tb * TB) // cfg.SL
        P.dma(st.gate[:], bc_rows(k.MOD[l, slot:slot + 1, 2 * D:3 * D], 128), pfx + "gate")
    for sub in range(4):
        s = st.it % 2
        st.it += 1
        r0 = tb * TB + sub * 128
        P.dma(st.xt[s][:], xsrc[r0:r0 + 128, :], pfx + "x%d" % s)
        yield
        for dh in range(2):
            pb = k.ps[:, 4 + dh, :]
            for e in range(16):
                P.mm(pb, GT[:, e, sub * 128:(sub + 1) * 128], st.wo[:, e, dh * 512:(dh + 1) * 512], start=(e == 0), stop=(e == 15))
                if e % 4 == 3:
                    yield
            P.tt(st.ty[s][:, dh * 512:(dh + 1) * 512], pb, st.gate[:, dh * 512:(dh + 1) * 512], ALU.mult)
            yield
        P.tt(st.ty[s][:], st.ty[s][:], st.xt[s][:], ALU.add, eng="pool")
        yield
        P.dma(k.X[r0:r0 + 128, :], st.ty[s][:], pfx + "y%d" % s)


def outproj_block(k, l, tb, GT, st, xsrc, pfx):
    for _ in outproj_gen(k, l, tb, GT, st, xsrc, pfx):
        pass


def hgrn2(k, l, xsrc):
    P, cfg = k.P, k.cfg
    NTB, NCH = cfg.NTB, cfg.NCH
    W = k.hg_w_in
    P.reset("hgrn2_A")
    oq = [P.sbuf("ha_o%d" % i, [128, TB], F32) for i in range(4)]
    sgq = [P.sbuf("ha_s%d" % i, [128, TB], F32) for i in range(4)]

    def epiA(tag, col, tb, pb, it):
        s = it % 4
        dst = k.QT if tag == "q" else k.ZT
        row = col if tag == "q" else col - 4 * E
        P.act(sgq[s][:], pb, AF.Sigmoid)
        yield
        P.tt(oq[s][:], pb, sgq[s][:], ALU.mult, eng=("dve" if it % 2 == 0 else "dve"))
        yield
        P.dma(dst[row:row + 128, tb * TB:(tb + 1) * TB], oq[s][:], "ha_o%d" % s)

    cols = [(h * 128, "q") for h in range(16)] + [(4 * E + h * 128, "z") for h in range(16)]
    groups = [cols[i:i + 4] for i in range(0, len(cols), 4)]
    gemm_fm(k, W, groups, epiA, "ha_")
    P.barrier()
    P.reset("hgrn2_A2")
    gemm_tm(k, W, 3 * E, E, k.V, "hv_")
    P.barrier()
    P.reset("hgrn2_B")
    lbr = P.sbuf("hb_lbr", [128, 16, 5], F32)
    lbe = P.sbuf("hb_lbe", [128, 16, 5], F32)
    den = P.sbuf("hb_den", [128, 16], F32)
    num = P.sbuf("hb_num", [128, 16], F32)
    lbv = P.sbuf("hb_lb", [128, 16], F32)
    oml = P.sbuf("hb_oml", [128, 16], F32)
    for r in range(5):
        P.dma(lbr[:, :, r], k.hg_lb[r, :].rearrange("(j p) -> p j", p=128), "hb_lb", allow_slow_non_contiguous=True)
    P.act(lbe[:], lbr[:], AF.Exp)
    P.add("dve", lambda e: e.reduce_sum(den[:], lbe[:], mybir.AxisListType.X), [lbe[:]], [den[:]])
    P.add("dve", lambda e: e.reduce_sum(num[:], lbe[:, :, 0:l + 1], mybir.AxisListType.X), [lbe[:]], [num[:]])
    P.add("dve", lambda e: e.reciprocal(den[:], den[:]), [den[:]], [den[:]])
    P.tt(lbv[:], num[:], den[:], ALU.mult)
    P.ts(oml[:], lbv[:], -1.0, 1.0, ALU.mult, ALU.add)
    nb = 3
    tl = lambda nm, sh, dt=F32: [P.sbuf("hb_%s%d" % (nm, i), sh, dt) for i in range(nb)]
    qb, sg, ff, gg, kk, bb, bc, e1, e2 = (tl(n, [128, TB]) for n in ("qb", "sg", "ff", "gg", "kk", "bb", "bc", "e1", "e2"))
    qt = tl("qt", [128, TB], BF16)
    kt_ = tl("kt", [128, TB], BF16)
    scs = tl("sc", [128, 3, 8])
    dtmp = tl("dt", [128, 8])

    def epiB(tag, col, tb, pb, it):
        _, dr, h = tag
        s = it % nb
        tsl = slice(tb * TB, (tb + 1) * TB)
        P.dma(qb[s][:], k.QT[h * 128:(h + 1) * 128, tsl], "hb_q%d" % s)
        P.act(sg[s][:], pb, AF.Sigmoid)
        yield
        P.ts(ff[s][:], sg[s][:], oml[:, h:h + 1], lbv[:, h:h + 1], ALU.mult, ALU.add)
        yield
        P.act(gg[s][:], ff[s][:], AF.Ln)
        P.ts(kk[s][:], ff[s][:], -1.0, 1.0, ALU.mult, ALU.add, eng="pool")
        yield
        if dr == 0:
            P.scan(bb[s][:], k.rmask[:], gg[s][:], 0.0)
        else:
            P.scan(bb[s][:, ::-1], k.rmask[:], gg[s][:, ::-1], 0.0)
        yield
        b3 = bb[s][:].rearrange("p (c t) -> p c t", t=64)
        P.tt(bc[s][:].rearrange("p (c t) -> p c t", t=64), b3, b3[:, :, 32:33].to_broadcast([128, 8, 64]), ALU.subtract, eng="pool")
        yield
        P.act(e1[s][:], bc[s][:], AF.Exp)
        yield
        P.act(e2[s][:], bc[s][:], AF.Exp, scale=-1.0)
        P.tt(qt[s][:], qb[s][:], e1[s][:], ALU.mult)
        yield
        P.tt(kt_[s][:], kk[s][:], e2[s][:], ALU.mult, eng="pool")
        refc = b3[:, :, 32]
        lastc = b3[:, :, 63] if dr == 0 else b3[:, :, 0]
        P.act(scs[s][:, 0, :], refc, AF.Exp)
        yield
        P.act(scs[s][:, 1, :], lastc, AF.Exp)
        P.tt(dtmp[s][:], lastc, refc, ALU.subtract)
        yield
        P.act(scs[s][:, 2, :], dtmp[s][:], AF.Exp)
        cr = k.carry[:, dr * NCH + tb * 8:dr * NCH + tb * 8 + 8]
        yield
        P.tt(scs[s][:, 0:2, :], scs[s][:, 0:2, :], cr.rearrange("p (o c) -> p o c", o=1).to_broadcast([128, 2, 8]), ALU.mult)
        rows = slice(h * 128, (h + 1) * 128)
        P.dma(k.QK[dr, 0, rows, tsl], qt[s][:], "hb_sq%d" % s)
        yield
        P.dma(k.QK[dr, 1, rows, tsl], kt_[s][:], "hb_sk%d" % s)
        P.dma(k.SC[dr, :, rows, tb * 8:(tb + 1) * 8].rearrange("j p c -> p j c"), scs[s][:], "hb_ss%d" % s,
              allow_slow_non_contiguous=True)

    cols = [(E + dr * E + h * 128, ("f", dr, h)) for dr in range(2) for h in range(16)]
    groups = [cols[i:i + 4] for i in range(0, len(cols), 4)]
    gemm_fm(k, W, groups, epiB, "hb_")
    P.barrier()
    P.reset("hgrn2_C")
    chunk_engine(k, 1, 1, 16, 4, "hc_")
    P.barrier()
    P.reset("hgrn2_D")
    st = outproj_setup(k, l, k.hg_w_out, "hd_")
    ngc = P.sbuf("hd_ng", [128, 1], F32)
    P.dma(ngc[:], k.hg_norm_g.rearrange("o p -> p o"), "hd_ng", allow_slow_non_contiguous=True)
    GT = [P.sbuf("hd_GT%d" % i, [128, 16, TB], BF16) for i in range(2)]
    tl = lambda nm, sh, dt=F32: [P.sbuf("hd_%s%d" % (nm, i), sh, dt) for i in range(2)]
    NL = 4
    tl = lambda nm, sh, dt=F32: [P.sbuf("hd_%s%d" % (nm, i), sh, dt) for i in range(NL)]
    of, obk, zz, oo, sq, rs = (tl(n, [128, TB]) for n in ("of", "ob", "zz", "oo", "sq", "rs"))

    def body(tb, h, it):
        s = it % NL
        tsl = slice(tb * TB, (tb + 1) * TB)
        rows = slice(h * 128, (h + 1) * 128)
        P.dma(of[s][:], k.OT[0, rows, tsl], "hd_of%d" % s)
        P.dma(obk[s][:], k.OT[1, rows, tsl], "hd_ob%d" % s)
        P.dma(zz[s][:], k.ZT[rows, tsl], "hd_zz%d" % s)
        yield
        P.tt(oo[s][:], of[s][:], obk[s][:], ALU.add, eng="pool")
        yield
        P.act(sq[s][:], oo[s][:], AF.Square)
        yield
        pb = k.ps[:, it % 4, :]
        P.mm(pb, k.ones[:], sq[s][:])
        yield
        P.act(rs[s][:], pb, AF.Sqrt, bias=k.epsT[:], scale=1.0 / 128)
        yield
        P.add("dve", lambda e, a=rs[s][:]: e.reciprocal(a, a), [rs[s][:]], [rs[s][:]])
        yield
        P.tt(oo[s][:], oo[s][:], rs[s][:], ALU.mult, eng="pool")
        yield
        P.stt(GT[tb % 2][:, h, :], oo[s][:], ngc[:], zz[s][:], ALU.mult, ALU.mult)

    it = 0
    pend = None
    for tb in range(NTB):
        lanes = Lanes(NL + 1 if pend is not None else NL)
        lanes.push(pend)
        for h in range(16):
            lanes.width = NL + (1 if any(g is pend for g in lanes.active) else 0)
            lanes.push(body(tb, h, it))
            it += 1
        lanes.drain()
        pend = outproj_gen(k, l, tb, GT[tb % 2], st, xsrc, "hd_")
    for _ in pend:
        pass


def host_consts(cfg, carry):
    NCH = cfg.NCH
    cps = cfg.SL // CH
    c = {}
    c["k_ident"] = np.eye(128, dtype=np.float32)
    ii = np.arange(64)
    fw = (ii[:, None] <= ii[None, :]).astype(np.float32)
    bw = (ii[:, None] >= ii[None, :]).astype(np.float32)
    c["k_tri"] = np.concatenate([fw, bw], axis=1)
    rm = np.ones((128, TB), np.float32)
    rm[:, ::64] = 0.0
    c["k_rmask"] = rm
    cr = np.ones((128, 2 * NCH), np.float32)
    for ch in range(NCH):
        if ch % cps == 0 and ch > 0:
            cr[:, ch] = carry
        if (ch + 1) % cps == 0 and ch < NCH - 1:
            cr[:, NCH + ch] = carry
    c["k_carry"] = cr
    c["k_carry1"] = np.full((128, 1), carry, np.float32)
    return c


def host_consts_rt(cfg, carry, seq_pos):
    NCH = cfg.NCH
    cps = cfg.SL // CH
    c = {}
    inv = (10000.0 ** (-np.arange(0, 256, 2, dtype=np.float32) / np.float32(256))).astype(np.float32)
    ang = (seq_pos.astype(np.float32)[None, :] * inv[:, None]).astype(np.float32)
    c["k_cos"] = np.cos(ang).astype(np.float32)
    c["k_sin"] = np.sin(ang).astype(np.float32)
    hidx = np.arange(4, dtype=np.float32)
    lg = [np.log1p(-np.exp2(-5.0 - hidx)).astype(np.float32), np.log1p(-np.exp2(-5.5 - hidx)).astype(np.float32)]
    pos = np.arange(64, dtype=np.float64)
    dt = np.zeros((2, 4, 2, 64), np.float64)
    sc = np.ones((2, 3, 1024, NCH), np.float64)
    for dr in range(2):
        cnt = (pos + 1.0) if dr == 0 else (64.0 - pos)
        for hd in range(4):
            g = float(lg[dr][hd])
            dt[dr, hd, 0] = np.exp(g * cnt)
            dt[dr, hd, 1] = np.exp(-g * cnt) * (256.0 ** -0.5)
            rows = slice(hd * 256, (hd + 1) * 256)
            sc[dr, 1, rows, :] = np.exp(g * 64.0)
            sc[dr, 2, rows, :] = np.exp(g * 64.0)
        for ch in range(NCH):
            bnd = (ch % cps == 0 and ch > 0) if dr == 0 else ((ch + 1) % cps == 0 and ch < NCH - 1)
            if bnd:
                sc[dr, 0, :, ch] *= carry
                sc[dr, 1, :, ch] *= carry
    c["k_dtab"] = np.broadcast_to(dt.reshape(1, -1), (128, 16 * 64)).astype(np.float32).copy()
    c["k_rtsc"] = sc.astype(np.float32)
    return c


WNAMES = ["ada_w", "ada_b", "norm_g", "hg_lb", "hg_w_in", "hg_norm_g", "hg_w_out"]


def core_inputs(cfg, x_slots, c_slots, carry, w, seq_pos=None, Lc=None):
    SL = cfg.SL
    xin = np.zeros((cfg.T, D), np.float32)
    cin = np.zeros((cfg.NS, D), np.float32)
    for s in range(cfg.NS):
        if x_slots[s] is not None:
            xin[s * SL:(s + 1) * SL] = x_slots[s]
            cin[s] = c_slots[s]
    m = {"xin": xin, "cin": cin}
    m.update(host_consts(cfg, carry))
    dp = cfg.depth
    m["ada_w"] = np.ascontiguousarray(w["ada_w"][:dp])
    m["ada_b"] = np.ascontiguousarray(w["ada_b"][:dp])
    m["norm_g"] = np.ascontiguousarray(w["norm_g"][:dp])
    m["final_g"] = np.ascontiguousarray(w["final_g"]).reshape(1, D)
    m["hg_lb"] = np.ascontiguousarray(w["hg_lb"])
    m["hg_w_in"] = np.ascontiguousarray(w["hg_w_in"][0])
    m["hg_norm_g"] = np.ascontiguousarray(w["hg_norm_g"][0]).reshape(1, 128)
    m["hg_w_out"] = np.ascontiguousarray(w["hg_w_out"][0])
    if 2 in cfg.kinds:
        m.update(host_consts_rt(cfg, carry, seq_pos))
        m["rt_w_in"] = np.ascontiguousarray(w["rt_w_in"][0])
        m["rt_gn_g"] = np.ascontiguousarray(w["rt_gn_g"][0]).reshape(1, E)
        m["rt_w_out"] = np.ascontiguousarray(w["rt_w_out"][0])
    if 1 in cfg.kinds:
        m.update(host_consts_hy(cfg, Lc, [xs is not None for xs in x_slots]))
        for nm in ["hy_w_in", "hy_conv_w", "hy_f_w1", "hy_f_w2", "hy_f_b2", "hy_f_wout", "hy_w_out"]:
            m[nm] = np.ascontiguousarray(w[nm][0])
        for nm in ["hy_b_in", "hy_conv_b", "hy_f_b1", "hy_f_freq", "hy_skip"]:
            m[nm] = np.ascontiguousarray(w[nm][0]).reshape(1, -1)
    if 3 in cfg.kinds:
        m["lru_w_in"] = np.ascontiguousarray(w["lru_w_in"][0])
        m["lru_conv_w"] = np.ascontiguousarray(w["lru_conv_w"][0])
        m["lru_conv_b"] = np.ascontiguousarray(w["lru_conv_b"][0]).reshape(1, E)
        m["lru_gate_w"] = np.ascontiguousarray(w["lru_gate_w"][0])
        m["lru_gate_b"] = np.ascontiguousarray(w["lru_gate_b"][0])
        m["lru_lambda"] = np.ascontiguousarray(w["lru_lambda"][0])
        m["lru_w_out"] = np.ascontiguousarray(w["lru_w_out"][0])
    return m


def retention(k, l, xsrc):
    P, cfg = k.P, k.cfg
    NTB, NCH = cfg.NTB, cfg.NCH
    W = k.rt_w_in
    P.reset("retention_A")
    dtab = P.sbuf("ra_dtab", [128, 16, 64], F32)
    P.dma(dtab[:], k.k_dtab.rearrange("p (a b) -> p a b", a=16), "ra_dt")
    NP = 3
    tl = lambda nm, sh, dt=F32, n=NP: [P.sbuf("ra_%s%d" % (nm, i), sh, dt) for i in range(n)]
    x1, x2, cs_, sn_, o1, o2, ta, tb_ = (tl(n, [128, TB]) for n in ("x1", "x2", "cs", "sn", "o1", "o2", "ta", "tb"))
    obf = tl("obf", [128, TB], BF16, 8)
    oz = tl("oz", [128, TB], F32, 4)
    sgz = tl("sgz", [128, TB], F32, 4)
    cnt = {"p": 0, "o": 0}

    def epiA(tag, col, tb, pb, it):
        kind, hd, a = tag
        tsl = slice(tb * TB, (tb + 1) * TB)
        if kind == "g":
            s = it % 4
            P.act(sgz[s][:], pb, AF.Sigmoid)
            yield
            P.tt(oz[s][:], pb, sgz[s][:], ALU.mult)
            yield
            P.dma(k.ZT[col - 4096:col - 4096 + 128, tsl], oz[s][:], "ra_oz%d" % s)
            return
        s = cnt["p"] % NP
        if a == 0:
            P.copy(x1[s][:], pb, eng="act")
            return
        P.copy(x2[s][:], pb, eng="act")
        cnt["p"] += 1
        qk = 0 if kind == "q" else 1
        P.dma(cs_[s][:], k.k_cos[:, tsl], "ra_c%d" % s)
        P.dma(sn_[s][:], k.k_sin[:, tsl], "ra_s%d" % s)
        yield
        P.tt(o1[s][:], x1[s][:], cs_[s][:], ALU.mult)
        P.tt(ta[s][:], x2[s][:], sn_[s][:], ALU.mult, eng="pool")
        yield
        P.tt(o2[s][:], x1[s][:], sn_[s][:], ALU.mult, eng="pool")
        P.tt(tb_[s][:], x2[s][:], cs_[s][:], ALU.mult)
        yield
        P.tt(o1[s][:], o1[s][:], ta[s][:], ALU.subtract)
        P.tt(o2[s][:], o2[s][:], tb_[s][:], ALU.add, eng="pool")
        yield
        for dr in range(2):
            dsl = dtab[:, (dr * 4 + hd) * 2 + qk, :].rearrange("p (o t) -> p o t", o=1).to_broadcast([128, TB // 64, 64])
            for half, src in enumerate((o1[s], o2[s])):
                oi = cnt["o"] % 8
                ob_ = obf[oi]
                P.tt(ob_[:].rearrange("p (c t) -> p c t", t=64), src[:].rearrange("p (c t) -> p c t", t=64), dsl, ALU.mult,
                     eng=("dve" if half == 0 else "pool"))
                r0 = (hd * 2 + half) * 128
                cnt["o"] += 1
                yield
                P.dma(k.QK[dr, qk, r0:r0 + 128, tsl], ob_[:], "ra_so%d" % oi)

    groups = []
    for hd in range(4):
        groups.append([(hd * 256, ("q", hd, 0)), (hd * 256 + 128, ("q", hd, 1)),
                       (1024 + hd * 256, ("k", hd, 0)), (1024 + hd * 256 + 128, ("k", hd, 1))])
    gcols = [(4096 + j * 128, ("g", 0, 0)) for j in range(16)]
    groups += [gcols[i:i + 4] for i in range(0, 16, 4)]
    gemm_fm(k, W, groups, epiA, "ra_")
    P.barrier()
    P.reset("retention_A2")
    gemm_tm(k, W, 2048, E, k.V, "rv_")
    P.barrier()
    P.reset("retention_B")
    chunk_engine(k, 2, 4, 4, 1, "rc_", SCsrc=k.k_rtsc)
    P.barrier()
    P.reset("retention_C")
    st = outproj_setup(k, l, k.rt_w_out, "rd_")
    gn = P.sbuf("rd_gn", [128, 16], F32)
    P.dma(gn[:], k.rt_gn_g.rearrange("o (j p) -> p (o j)", p=128), "rd_gn", allow_slow_non_contiguous=True)
    GT = [P.sbuf("rd_GT%d" % i, [128, 16, TB], BF16) for i in range(2)]
    NL = 2
    tl = lambda nm, sh, dt=F32, n=2: [P.sbuf("rd_%s%d" % (nm, i), sh, dt) for i in range(n)]
    of, obk, zz = (tl(n, [128, TB], F32, NL * 4) for n in ("of", "ob", "zz"))
    sq = obk
    oo = of
    rs = tl("rs", [128, TB], F32, NL)

    def body(tb, hd, hi):
        tsl = slice(tb * TB, (tb + 1) * TB)
        ln = hi % NL
        pb = k.ps[:, hi % 4, :]
        for i in range(4):
            e = hd * 4 + i
            s = ln * 4 + i
            rows = slice(e * 128, (e + 1) * 128)
            P.dma(of[s][:], k.OT[0, rows, tsl], "rd_of%d" % s)
            P.dma(obk[s][:], k.OT[1, rows, tsl], "rd_ob%d" % s)
            P.dma(zz[s][:], k.ZT[rows, tsl], "rd_zz%d" % s)
        yield
        for i in range(4):
            s = ln * 4 + i
            P.tt(oo[s][:], of[s][:], obk[s][:], ALU.add, eng=("pool" if i % 2 == 0 else "dve"))
            yield
            P.act(sq[s][:], oo[s][:], AF.Square)
            yield
        for i in range(4):
            s = ln * 4 + i
            P.mm(pb, k.ones[:], sq[s][:], start=(i == 0), stop=(i == 3))
        yield
        r_ = rs[ln]
        P.act(r_[:], pb, AF.Sqrt, bias=k.epsT[:], scale=1.0 / 512)
        yield
        P.add("dve", lambda e_, a=r_[:]: e_.reciprocal(a, a), [r_[:]], [r_[:]])
        yield
        for i in range(4):
            e = hd * 4 + i
            s = ln * 4 + i
            P.tt(oo[s][:], oo[s][:], r_[:], ALU.mult, eng="pool")
            yield
            P.stt(GT[tb % 2][:, e, :], oo[s][:], gn[:, e:e + 1], zz[s][:], ALU.mult, ALU.mult)
            yield

    hi = 0
    pend = None
    for tb in range(NTB):
        lanes = Lanes(NL + 1 if pend is not None else NL)
        lanes.push(pend)
        for hd in range(4):
            lanes.width = NL + (1 if any(g is pend for g in lanes.active) else 0)
            lanes.push(body(tb, hd, hi))
            hi += 1
        lanes.drain()
        pend = outproj_gen(k, l, tb, GT[tb % 2], st, xsrc, "rd_")
    for _ in pend:
        pass


def rglru(k, l, xsrc):
    P, cfg = k.P, k.cfg
    NTB, T, SL = cfg.NTB, cfg.T, cfg.SL
    W = k.lru_w_in
    P.reset("rglru_A")
    oq = [P.sbuf("la_o%d" % i, [128, TB], F32) for i in range(4)]
    sgq = [P.sbuf("la_s%d" % i, [128, TB], F32) for i in range(4)]

    def epiA(tag, col, tb, pb, it):
        s = it % 4
        tsl = slice(tb * TB, (tb + 1) * TB)
        if tag == "x":
            P.copy(oq[s][:], pb, eng="act")
            yield
            P.dma(k.QT[col:col + 128, tsl], oq[s][:], "la_o%d" % s)
        else:
            P.act(sgq[s][:], pb, AF.Sigmoid)
            yield
            P.tt(oq[s][:], pb, sgq[s][:], ALU.mult)
            yield
            P.dma(k.ZT[col - E:col - E + 128, tsl], oq[s][:], "la_o%d" % s)

    cols = [(j * 128, "x") for j in range(16)] + [(E + j * 128, "z") for j in range(16)]
    groups = [cols[i:i + 4] for i in range(0, 32, 4)]
    gemm_fm(k, W, groups, epiA, "la_")
    P.barrier()
    P.reset("rglru_B")
    BL = 512
    NB = T // BL
    NL = 6
    cw = P.sbuf("lb_cw", [128, 16, 4], F32)
    cb = P.sbuf("lb_cb", [128, 16], F32)
    gb = P.sbuf("lb_gb", [128, 4, 16], F32)
    lam = P.sbuf("lb_lam", [128, 2, 16], F32)
    m8 = P.sbuf("lb_m8", [128, 2, 16], F32)
    for jj in range(4):
        P.dma(cw[:, :, jj], k.lru_conv_w[jj, :].rearrange("(j p) -> p j", p=128), "lb_c", allow_slow_non_contiguous=True)
    P.dma(cb[:], k.lru_conv_b.rearrange("o (j p) -> p (o j)", p=128), "lb_c", allow_slow_non_contiguous=True)
    for d in range(2):
        P.dma(lam[:, d, :], k.lru_lambda[d, :].rearrange("(j p) -> p j", p=128), "lb_c", allow_slow_non_contiguous=True)
        for g in range(2):
            P.dma(gb[:, d * 2 + g, :], k.lru_gate_b[d, g, :].rearrange("(j p) -> p j", p=128), "lb_c", allow_slow_non_contiguous=True)
    P.act(m8[:], lam[:], AF.Exp, scale=-1.0)
    P.act(m8[:], m8[:], AF.Ln, bias=k.ones[:, 0:1], scale=1.0)
    P.ts(m8[:], m8[:], -8.0, None, ALU.mult)
    gw = [P.sbuf("lb_gw%d" % i, [128, 4, 128], F32) for i in range(NL)]
    tl = lambda nm, sh, dt=F32, n=NL: [P.sbuf("lb_%s%d" % (nm, i), sh, dt) for i in range(n)]
    xr = tl("xr", [128, BL + 3])
    xb, rr, ii, aa, hf, zz = (tl(n, [128, BL]) for n in ("xb", "rr", "ii", "aa", "hf", "zz"))
    a2, sq, bx, bb = rr, rr, ii, ii
    hh = [[P.sbuf("lb_hh%d_%d" % (i, q), [128, BL], F32) for q in range(2)] for i in range(NL)]
    gt = tl("gt", [128, BL], BF16)
    hinit = tl("hi", [128, 1])

    def chain(j, s):
        rows = slice(j * 128, (j + 1) * 128)
        gws = gw[s]
        for d in range(2):
            for g in range(2):
                P.dma(gws[:, d * 2 + g, :], k.lru_gate_w[d, g, j, :, :], "lb_gw%d" % s)
        yield
        cnt = 0
        for d in range(2):
            order = list(range(NB)) if d == 0 else list(range(NB - 1, -1, -1))
            prev = None
            for bi in order:
                t0 = bi * BL
                hcur = hh[s][cnt % 2]
                cnt += 1
                lo = max(t0 - 2, 0)
                hi_ = min(t0 + BL + 1, T)
                if t0 == 0:
                    P.memset(xr[s][:, 0:2], 0.0)
                if t0 + BL == T:
                    P.memset(xr[s][:, BL + 2:BL + 3], 0.0)
                P.dma(xr[s][:, lo - (t0 - 2):hi_ - (t0 - 2)], k.QT[rows, lo:hi_], "lb_x%d" % s)
                if t0 % SL == 0 and t0 > 0:
                    P.ts(xr[s][:, 0:2], xr[s][:, 0:2], k.carry1[:], None, ALU.mult, eng="pool")
                if (t0 + BL) % SL == 0 and t0 + BL < T:
                    P.ts(xr[s][:, BL + 2:BL + 3], xr[s][:, BL + 2:BL + 3], k.carry1[:], None, ALU.mult, eng="pool")
                yield
                P.act(xb[s][:], xr[s][:, 0:BL], AF.Identity, bias=cb[:, j:j + 1], scale=cw[:, j, 0:1])
                yield
                P.stt(xb[s][:], xr[s][:, 1:BL + 1], cw[:, j, 1:2], xb[s][:], ALU.mult, ALU.add)
                yield
                P.stt(xb[s][:], xr[s][:, 2:BL + 2], cw[:, j, 2:3], xb[s][:], ALU.mult, ALU.add)
                yield
                P.stt(xb[s][:], xr[s][:, 3:BL + 3], cw[:, j, 3:4], xb[s][:], ALU.mult, ALU.add)
                yield
                p0 = k.ps[:, s, :]
                P.mm(p0, gws[:, d * 2 + 0, :], xb[s][:])
                yield
                P.act(rr[s][:], p0, AF.Sigmoid, bias=gb[:, d * 2 + 0, j:j + 1])
                yield
                P.mm(p0, gws[:, d * 2 + 1, :], xb[s][:])
                yield
                P.act(ii[s][:], p0, AF.Sigmoid, bias=gb[:, d * 2 + 1, j:j + 1])
                yield
                P.act(aa[s][:], rr[s][:], AF.Exp, scale=m8[:, d, j:j + 1])
                P.tt(bx[s][:], ii[s][:], xb[s][:], ALU.mult)
                yield
                P.tt(a2[s][:], aa[s][:], aa[s][:], ALU.mult, eng="pool")
                yield
                P.act(sq[s][:], a2[s][:], AF.Sqrt, bias=k.ones[:, 0:1], scale=-1.0)
                yield
                P.tt(bb[s][:], sq[s][:], bx[s][:], ALU.mult)
                yield
                if prev is None:
                    init = 0.0
                else:
                    bnd = (t0 % SL == 0) if d == 0 else ((t0 + BL) % SL == 0)
                    if bnd:
                        P.ts(hinit[s][:], prev, k.carry1[:], None, ALU.mult)
                        init = hinit[s][:]
                        yield
                    else:
                        init = prev
                if d == 0:
                    P.scan(hcur[:], aa[s][:], bb[s][:], init)
                    prev = hcur[:, BL - 1:BL]
                    yield
                    P.dma(k.OT[0, rows, t0:t0 + BL], hcur[:], "lb_sh%d" % s)
                else:
                    P.dma(hf[s][:], k.OT[0, rows, t0:t0 + BL], "lb_lf%d" % s)
                    P.dma(zz[s][:], k.ZT[rows, t0:t0 + BL], "lb_lz%d" % s)
                    P.scan(hcur[:, ::-1], aa[s][:, ::-1], bb[s][:, ::-1], init)
                    prev = hcur[:, 0:1]
                    yield
                    P.tt(hf[s][:], hf[s][:], hcur[:], ALU.add, eng="pool")
                    yield
                    P.tt(gt[s][:], hf[s][:], zz[s][:], ALU.mult)
                    yield
                    P.dma(k.QK[0, 0, rows, t0:t0 + BL], gt[s][:], "lb_sg%d" % s)
                yield

    lanes = Lanes(NL)
    for j in range(16):
        lanes.push(chain(j, j % NL))
    lanes.drain()
    P.barrier()
    P.reset("rglru_C")
    st = outproj_setup(k, l, k.lru_w_out, "lc_")
    GT = [P.sbuf("lc_GT%d" % i, [128, 16, TB], BF16) for i in range(2)]
    for tb in range(NTB):
        P.dma(GT[tb % 2][:], k.QK[0, 0, :, tb * TB:(tb + 1) * TB].rearrange("(e p) t -> p e t", p=128), "lc_g%d" % (tb % 2))
        outproj_block(k, l, tb, GT[tb % 2], st, xsrc, "lc_")


def host_consts_hy(cfg, Lc, real_slots):
    NA, NB, T = cfg.NA, cfg.NB, cfg.T
    NF = 2 * T
    c = {}
    a = np.arange(NA, dtype=np.float64)
    b = np.arange(NB, dtype=np.float64)
    tha = 2 * np.pi * np.outer(a, a) / NA
    thb = 2 * np.pi * np.outer(b, b) / NB
    FAr, FAi = np.cos(tha), -np.sin(tha)
    FBr, FBi = np.cos(thb), -np.sin(thb)
    c["k_fa"] = np.concatenate([FAr, FAi], 1).astype(np.float32)
    c["k_fb"] = np.concatenate([FBr, FBi, -FBi], 1).astype(np.float32)
    c["k_fc"] = np.concatenate([FBr, -FBi, FBi, FBr], 1).astype(np.float32)
    th = 2 * np.pi * np.outer(b, a) / NF
    c["k_tw1"] = np.concatenate([np.cos(th), -np.sin(th)], 1).astype(np.float32)
    c["k_tw2"] = np.concatenate([np.cos(th.T), np.sin(th.T)], 1).astype(np.float32)
    n = np.arange(NF)
    mf = (n < Lc)
    mb = (n > NF - Lc) | (n == 0)
    pos = np.where(mf, n, np.where(mb, NF - n, 0)).astype(np.float32)
    pos[0] = 0.0
    t = (pos / np.float32(Lc - 1)).astype(np.float32)
    wv = (np.float32(2.0 * np.pi) * pos / np.float32(Lc)).astype(np.float32)
    bands = np.linspace(1e-4, 15, 16, dtype=np.float32)
    ang = (bands[:, None] * wv[None, :]).astype(np.float32)
    z = np.concatenate([t[None, :], np.cos(ang), -np.sin(ang)], 0).astype(np.float32)
    valid = (mf | mb)
    c["k_zT"] = (z * valid[None, :]).astype(np.float32)
    c["k_fmask"] = np.stack([mf, mb]).astype(np.float32)
    c["k_trow"] = (t * valid).astype(np.float32).reshape(1, NF)
    import math
    max_decay = math.log(1e-2) / 0.3
    min_decay = math.log(1e-2) / 1.5
    deltas = np.abs(np.linspace(min_decay, max_decay, E, dtype=np.float32))
    c["k_delta"] = np.ascontiguousarray(deltas.reshape(16, 128).T).astype(np.float32)
    sm = np.zeros((128, cfg.NS), np.float32)
    for s_, r in enumerate(real_slots):
        sm[:, s_] = 1.0 if r else 0.0
    c["k_smask"] = sm
    return c


def hy_fft(k, src, K1, pfx, sink):
    P, cfg = k.P, k.cfg
    NA, NB = cfg.NA, cfg.NB
    FT = k.FT
    CG, LG = 4, 16
    xin = [P.sbuf(pfx + "xin%d" % i, [K1, LG, NB], F32) for i in range(2)]
    if FT != F32:
        xrr = [k.r_xrr[i][0:K1] for i in range(2)]
        Bt = k.r_B
    else:
        xrr = xin
        Bt = [P.sbuf(pfx + "B%d" % i, [NB, 2, CG, NA], F32) for i in range(2)]
    mt = [P.sbuf(pfx + "m%d" % i, [NB, CG, NA], F32) for i in range(8)]
    psA = k.ps[0:NB, 0:2, :].rearrange("p a b -> p (a b)")
    psXr = k.ps[0:NB, 2, 0:CG * NA]
    psXi = k.ps[0:NB, 3, 0:CG * NA]
    twr = k.tw1[:, 0:NA].rearrange("p (o k) -> p o k", o=1).to_broadcast([NB, CG, NA])
    twi = k.tw1[:, NA:2 * NA].rearrange("p (o k) -> p o k", o=1).to_broadcast([NB, CG, NA])

    def grp(lg, gi, it, xs):
        s = it % 2
        m = mt[s * 4:s * 4 + 4]
        for cc in range(CG):
            P.mm(psA[:, cc * 2 * NA:(cc + 1) * 2 * NA], xs[:, gi * CG + cc, :], k.fa[0:K1, :])
        yield
        A4 = psA[:, 0:CG * 2 * NA].rearrange("p (c r k) -> p c r k", c=CG, r=2)
        Ar, Ai = A4[:, :, 0, :], A4[:, :, 1, :]
        P.tt(m[0][:], Ar, twr, ALU.mult)
        yield
        P.tt(m[1][:], Ai, twi, ALU.mult)
        yield
        P.tt(Bt[s][:, 0, :, :], m[0][:], m[1][:], ALU.subtract, eng="pool")
        P.tt(m[2][:], Ar, twi, ALU.mult)
        yield
        P.tt(m[3][:], Ai, twr, ALU.mult)
        yield
        P.tt(Bt[s][:, 1, :, :], m[2][:], m[3][:], ALU.add, eng="pool")
        yield
        Brf = Bt[s][:, 0, :, :].rearrange("p c k -> p (c k)")
        Bif = Bt[s][:, 1, :, :].rearrange("p c k -> p (c k)")
        P.mm(psXr, k.fb[:, 0:NB], Brf, start=True, stop=False)
        P.mm(psXr, k.fb[:, 2 * NB:3 * NB], Bif, start=False, stop=True)
        P.mm(psXi, k.fb[:, NB:2 * NB], Brf, start=True, stop=False)
        P.mm(psXi, k.fb[:, 0:NB], Bif, start=False, stop=True)
        yield
        yield from sink(lg, gi, lg * LG + gi * CG, psXr, psXi, s)

    lanes = Lanes(2, stagger=k.fft_stagger)
    it = 0
    for lg in range(E // LG):
        xs = xin[lg % 2]
        P.dma(xs[:], src[lg * LG:(lg + 1) * LG, 0:K1 * NB].rearrange("c (n1 n2) -> n1 c n2", n2=NB), pfx + "x%d" % (lg % 2))
        if FT != F32:
            P.copy(xrr[lg % 2][:], xs[:], eng="act")
            xs = xrr[lg % 2]
        for gi in range(LG // CG):
            lanes.push(grp(lg, gi, it, xs))
            it += 1
    lanes.drain()


import os as _os
_HYSTOP = _os.environ.get("HY_STOP", "")
_USE_F32R = _os.environ.get("HY_F32R", "1") == "1"


def hyena(k, l, xsrc):
    P, cfg = k.P, k.cfg
    NTB, T, SL, NS = cfg.NTB, cfg.T, cfg.SL, cfg.NS
    NA, NB = cfg.NA, cfg.NB
    NF = 2 * T
    CG, LG = 4, 16
    W = k.hy_w_in
    UT, X1, YT = k.QT, k.OT[0], k.OT[1]
    P.reset("hyena_A")
    bin_ = P.sbuf("ya_bin", [128, 64], F32)
    P.dma(bin_[:], k.hy_b_in.rearrange("o (j p) -> p (o j)", p=128), "ya_b", allow_slow_non_contiguous=True)
    oq = [P.sbuf("ya_o%d" % i, [128, TB], F32) for i in range(4)]
    sgq = [P.sbuf("ya_s%d" % i, [128, TB], F32) for i in range(4)]
    zb = [P.sbuf("ya_z%d" % i, [128, TB], F32) for i in range(4)]

    def epiA(tag, col, tb, pb, it):
        s = it % 4
        j = col // 128
        tsl = slice(tb * TB, (tb + 1) * TB)
        if tag == "x":
            P.act(oq[s][:], pb, AF.Identity, bias=bin_[:, j:j + 1])
            yield
            P.dma(k.PR[col:col + 128, tsl], oq[s][:], "ya_o%d" % s)
        else:
            P.act(zb[s][:], pb, AF.Identity, bias=bin_[:, j:j + 1])
            P.act(sgq[s][:], pb, AF.Sigmoid, bias=bin_[:, j:j + 1])
            yield
            P.tt(oq[s][:], zb[s][:], sgq[s][:], ALU.mult, eng="pool")
            yield
            P.dma(k.ZT[col - 3 * E:col - 3 * E + 128, tsl], oq[s][:], "ya_o%d" % s)

    cols = [(j * 128, "x") for j in range(48)] + [(3 * E + j * 128, "z") for j in range(16)]
    groups = [cols[i:i + 4] for i in range(0, 64, 4)]
    gemm_fm(k, W, groups, epiA, "ya_")
    P.barrier()
    if _HYSTOP == "B":
        return
    P.reset("hyena_B")
    BL = 512
    NBK = T // BL
    NL = 4
    cw = P.sbuf("yb_cw", [128, 48, 3], F32)
    cb = P.sbuf("yb_cb", [128, 48], F32)
    smask = P.sbuf("yb_sm", [128, NS], F32)
    for jj in range(3):
        P.dma(cw[:, :, jj], k.hy_conv_w[jj, :].rearrange("(j p) -> p j", p=128), "yb_c", allow_slow_non_contiguous=True)
    P.dma(cb[:], k.hy_conv_b.rearrange("o (j p) -> p (o j)", p=128), "yb_c", allow_slow_non_contiguous=True)
    P.dma(smask[:], k.k_smask, "yb_c")
    xr = [[P.sbuf("yb_xr%d_%d" % (i, b_), [128, BL + 2], F32) for b_ in range(3)] for i in range(NL)]
    xc = [[P.sbuf("yb_xc%d_%d" % (i, b_), [128, BL], F32) for b_ in range(3)] for i in range(NL)]
    tmpc = [P.sbuf("yb_tm%d" % i, [128, BL], F32) for i in range(NL)]
    uu = [P.sbuf("yb_u%d" % i, [128, BL], F32) for i in range(NL)]

    def bodyB(j, bi, s):
        t0 = bi * BL
        lo = max(t0 - 1, 0)
        hi_ = min(t0 + BL + 1, T)
        for b_ in range(3):
            jt = b_ * 16 + j
            xx = xr[s][b_]
            if t0 == 0:
                P.memset(xx[:, 0:1], 0.0)
            if t0 + BL == T:
                P.memset(xx[:, BL + 1:BL + 2], 0.0)
            P.dma(xx[:, lo - (t0 - 1):hi_ - (t0 - 1)], k.PR[jt * 128:(jt + 1) * 128, lo:hi_], "yb_x%d_%d" % (s, b_))
            if t0 % SL == 0 and t0 > 0:
                P.ts(xx[:, 0:1], xx[:, 0:1], k.carry1[:], None, ALU.mult, eng="pool")
            if (t0 + BL) % SL == 0 and t0 + BL < T:
                P.ts(xx[:, BL + 1:BL + 2], xx[:, BL + 1:BL + 2], k.carry1[:], None, ALU.mult, eng="pool")
        yield
        for b_ in (0, 2):
            jt = b_ * 16 + j
            xx = xr[s][b_]
            o = xc[s][b_]
            P.ts(o[:], xx[:, 0:BL], cw[:, jt, 0:1], cb[:, jt:jt + 1], ALU.mult, ALU.add)
            yield
            P.stt(o[:], xx[:, 1:BL + 1], cw[:, jt, 1:2], o[:], ALU.mult, ALU.add)
            yield
            P.stt(o[:], xx[:, 2:BL + 2], cw[:, jt, 2:3], o[:], ALU.mult, ALU.add)
            if b_ == 0:
                jt1 = 16 + j
                x1_, o1_ = xr[s][1], xc[s][1]
                P.act(o1_[:], x1_[:, 0:BL], AF.Identity, bias=cb[:, jt1:jt1 + 1], scale=cw[:, jt1, 0:1])
                yield
                P.act(tmpc[s][:], x1_[:, 1:BL + 1], AF.Copy, scale=cw[:, jt1, 1:2])
                yield
                P.tt(o1_[:], o1_[:], tmpc[s][:], ALU.add, eng="pool")
                yield
                P.act(tmpc[s][:], x1_[:, 2:BL + 2], AF.Copy, scale=cw[:, jt1, 2:3])
                yield
                P.tt(o1_[:], o1_[:], tmpc[s][:], ALU.add, eng="pool")
            yield
        slot = t0 // SL
        P.stt(uu[s][:], xc[s][0][:], smask[:, slot:slot + 1], xc[s][2][:], ALU.mult, ALU.mult)
        yield
        P.dma(UT[j * 128:(j + 1) * 128, t0:t0 + BL], uu[s][:], "yb_su%d" % s)
        P.dma(X1[j * 128:(j + 1) * 128, t0:t0 + BL], xc[s][1][:], "yb_s1%d" % s)

    lanes = Lanes(NL)
    it = 0
    for j in range(16):
        for bi in range(NBK):
            lanes.push(bodyB(j, bi, it % NL))
            it += 1
    lanes.drain()
    P.barrier()
    if _HYSTOP == "C":
        return
    P.reset("hyena_C")
    w1 = P.sbuf("yc_w1", [33, 64], F32)
    w2 = P.sbuf("yc_w2", [64, 2, 64], F32)
    wout = P.sbuf("yc_wo", [64, 2 * E], F32)
    fq = P.sbuf("yc_fq", [64, 1], F32)
    fbr = P.sbuf("yc_fbr", [64, 3], F32)
    fbs = P.sbuf("yc_fbs", [64, 3], F32)
    ndel = P.sbuf("yc_nd", [128, 16], F32)
    P.dma(w1[:], k.hy_f_w1, "yc_c")
    for jj in range(2):
        P.dma(w2[:, jj, :], k.hy_f_w2[jj, :, :], "yc_c")
        P.dma(fbr[:, 1 + jj:2 + jj], k.hy_f_b2[jj:jj + 1, :].rearrange("o p -> p o"), "yc_c", allow_slow_non_contiguous=True)
    P.dma(wout[:], k.hy_f_wout, "yc_c")
    P.dma(fq[:], k.hy_f_freq.rearrange("o p -> p o"), "yc_c", allow_slow_non_contiguous=True)
    P.dma(fbr[:, 0:1], k.hy_f_b1.rearrange("o p -> p o"), "yc_c", allow_slow_non_contiguous=True)
    P.dma(ndel[:], k.k_delta, "yc_c")
    P.ts(fq[:], fq[:], float(1.0 / (2.0 * np.pi)), None, ALU.mult)
    P.ts(fbs[:], fbr[:], fq[:], None, ALU.mult)
    P.ts(ndel[:], ndel[:], -1.0, None, ALU.mult, eng="pool")
    tl = lambda nm, sh, dt=F32, n=2: [P.sbuf("yc_%s%d" % (nm, i), sh, dt) for i in range(n)]
    zt = tl("zt", [33, TB])
    mrow = tl("mr", [64, 2, TB])
    trow = tl("tr", [128, TB])
    uf = tl("uf", [64, TB], F32, 3)
    ui = tl("ui", [64, TB], I32, 3)
    aa = tl("aa", [64, TB], F32, 3)
    k.FT = mybir.dt.float32r if (NA == 128 and _USE_F32R) else F32
    afw = tl("afw", [64, TB])
    abw = tl("abw", [64, TB])
    woutr = wout
    win = tl("win", [128, TB])
    tp = tl("tp", [128, TB])
    SC2PI = float(2.0 * np.pi * (1.0 - 1e-6))
    it = 0
    for nb in range(NF // TB):
        s = nb % 2
        nsl = slice(nb * TB, (nb + 1) * TB)
        P.dma(zt[s][:], k.k_zT[:, nsl], "yc_z%d" % s)
        P.dma(mrow[s][:, 0, :], bc_rows(k.k_fmask[0:1, nsl], 64), "yc_m%d" % s)
        P.dma(mrow[s][:, 1, :], bc_rows(k.k_fmask[1:2, nsl], 64), "yc_m%d" % s)
        P.dma(trow[s][:], bc_rows(k.k_trow[0:1, nsl], 128), "yc_t%d" % s)
        a_in = zt[s][:]
        for ly in range(3):
            pb = k.ps[0:64, 4 + (ly % 2), :]
            lw = w1[:] if ly == 0 else w2[:, ly - 1, :]
            P.mm(pb, lw, a_in)
            P.ts(uf[ly][:], pb, fq[:], fbs[:, ly:ly + 1], ALU.mult, ALU.add)
            P.copy(ui[ly][:], uf[ly][:])
            P.tt(uf[ly][:], uf[ly][:], ui[ly][:], ALU.subtract)
            P.act(aa[ly][:], uf[ly][:], AF.Sin, scale=SC2PI)
            a_in = aa[ly][:]
        P.tt(afw[s][:], aa[2][:], mrow[s][:, 0, :], ALU.mult, eng="pool")
        P.tt(abw[s][:], aa[2][:], mrow[s][:, 1, :], ALU.mult, eng="pool")
        for j in range(16):
            q = it % 2
            pb = k.ps[:, it % 4, :]
            P.mm(pb, woutr[:, j * 128:(j + 1) * 128], afw[s][:], start=True, stop=False)
            P.mm(pb, woutr[:, E + j * 128:E + (j + 1) * 128], abw[s][:], start=False, stop=True)
            P.act(win[q][:], trow[s][:], AF.Exp, scale=ndel[:, j:j + 1])
            P.tt(tp[q][:], pb, win[q][:], ALU.mult)
            P.dma(k.TAPS[j * 128:(j + 1) * 128, nsl], tp[q][:], "yc_s%d" % q)
            it += 1
    P.barrier()
    if _HYSTOP == "D":
        return
    P.reset("hyena_D")
    FT = k.FT
    k.tw1 = P.sbuf("yd_tw1", [NB, 2 * NA], F32)
    k.tw2 = P.sbuf("yd_tw2", [NA, 2 * NB], F32)
    fa0 = P.sbuf("yd_fa0", [NA, 2 * NA], F32)
    fb0 = P.sbuf("yd_fb0", [NB, 3 * NB], F32)
    fc0 = P.sbuf("yd_fc0", [NB, 4 * NB], F32)
    P.dma(fa0[:], k.k_fa, "yd_c")
    P.dma(fb0[:], k.k_fb, "yd_c")
    P.dma(fc0[:], k.k_fc, "yd_c")
    if FT != F32:
        k.fa = P.sbuf_r("yd_fa", [NA, 2 * NA], FT)
        k.fb = P.sbuf_r("yd_fb", [NB, 3 * NB], FT)
        k.fc = P.sbuf_r("yd_fc", [NB, 4 * NB], FT)
        k.r_xrr = [P.sbuf_r("yd_xrr%d" % i, [NA, LG, NB], FT) for i in range(2)]
        k.r_B = [P.sbuf_r("yd_rB%d" % i, [NB, 2, CG, NA], FT) for i in range(2)]
        k.r_Y = [P.sbuf_r("yd_rY%d" % i, [NB, 2, CG, NA], FT) for i in range(2)]
        k.r_D = [P.sbuf_r("yd_rD%d" % i, [NA, 2, CG, NB], FT) for i in range(2)]
        P.copy(k.fa[:], fa0[:])
        P.copy(k.fb[:], fb0[:])
        P.copy(k.fc[:], fc0[:], eng="pool")
    else:
        k.fa, k.fb, k.fc = fa0, fb0, fc0
    P.dma(k.tw1[:], k.k_tw1, "yd_c")
    P.dma(k.tw2[:], k.k_tw2, "yd_c")
    wm = P.bump
    rwm = P.rbump
    hsb = [P.sbuf("yd_h%d" % i, [NB, 2, CG, NA], F32) for i in range(2)]
    cnt = {"i": 0}

    def sink1(lg, gi, c0, psXr, psXi, s):
        P.copy(hsb[s][:, 0, :, :], psXr.rearrange("p (c k) -> p c k", c=CG), eng="act")
        yield
        P.copy(hsb[s][:, 1, :, :], psXi.rearrange("p (c k) -> p c k", c=CG), eng="act")
        yield
        for ri in range(2):
            P.dma(k.HS[ri, :, c0:c0 + CG, :], hsb[s][:, ri, :, :], "yd_sh%d" % s)

    k.fft_stagger = 5
    hy_fft(k, k.TAPS, NA, "yd1_", sink1)
    P.barrier()
    P.bump = wm
    P.rbump = rwm
    P.nphase += 1
    P.cur_phase = "%02d_hyena_D2" % P.nphase
    Hh = [P.sbuf("yd_H%d" % i, [NB, 2, CG, NA], F32) for i in range(2)]
    if FT != F32:
        Yt = k.r_Y
        Dt = k.r_D
    else:
        Yt = [P.sbuf("yd_Y%d" % i, [NB, 2, CG, NA], F32) for i in range(2)]
        Dt = [P.sbuf("yd_D%d" % i, [NA, 2, CG, NB], F32) for i in range(2)]
    MY = NA if FT != F32 else NA // 2
    m2 = [P.sbuf("yd_mm%d" % i, [128, CG * 128], F32) for i in range(8)]
    yout = [P.sbuf("yd_yo%d" % i, [NA // 2, LG, NB], F32) for i in range(2)]
    psD = k.ps[0:NA, 4:6, :].rearrange("p a b -> p (a b)")
    psY = k.psb[0:MY, 0, :].bitcast(F32)[:, 0:CG * NB]
    t2r = k.tw2[:, 0:NB].rearrange("p (o n) -> p o n", o=1).to_broadcast([NA, CG, NB])
    t2i = k.tw2[:, NB:2 * NB].rearrange("p (o n) -> p o n", o=1).to_broadcast([NA, CG, NB])
    cnt["i"] = 0

    def sink2(lg, gi, c0, psXr, psXi, s):
        for ri in range(2):
            P.dma(Hh[s][:, ri, :, :], k.HS[ri, :, c0:c0 + CG, :], "yd_lh%d" % s)
        Xr = psXr.rearrange("p (c k) -> p c k", c=CG)
        Xi = psXi.rearrange("p (c k) -> p c k", c=CG)
        mv = [m2[s * 4 + i][0:NB, 0:CG * NA].rearrange("p (c k) -> p c k", c=CG) for i in range(4)]
        P.tt(mv[0], Xr, Hh[s][:, 0, :, :], ALU.mult)
        yield
        P.tt(mv[1], Xi, Hh[s][:, 1, :, :], ALU.mult)
        yield
        P.tt(Yt[s][:, 0, :, :], mv[0], mv[1], ALU.subtract, eng="pool")
        P.tt(mv[2], Xr, Hh[s][:, 1, :, :], ALU.mult)
        yield
        P.tt(mv[3], Xi, Hh[s][:, 0, :, :], ALU.mult)
        yield
        P.tt(Yt[s][:, 1, :, :], mv[2], mv[3], ALU.add, eng="pool")
        yield
        for cc in range(CG):
            po = psD[:, cc * 2 * NB:(cc + 1) * 2 * NB]
            P.mm(po, Yt[s][:, 0, cc, :], k.fc[:, 0:2 * NB], start=True, stop=False)
            P.mm(po, Yt[s][:, 1, cc, :], k.fc[:, 2 * NB:4 * NB], start=False, stop=True)
        yield
        D4 = psD[:, 0:CG * 2 * NB].rearrange("p (c r n) -> p c r n", c=CG, r=2)
        Dr, Di = D4[:, :, 0, :], D4[:, :, 1, :]
        nv = [m2[s * 4 + i][0:NA, 0:CG * NB].rearrange("p (c n) -> p c n", c=CG) for i in range(4)]
        P.tt(nv[0], Dr, t2r, ALU.mult)
        yield
        P.tt(nv[1], Di, t2i, ALU.mult)
        yield
        P.tt(Dt[s][:, 0, :, :], nv[0], nv[1], ALU.subtract, eng="pool")
        P.tt(nv[2], Dr, t2i, ALU.mult)
        yield
        P.tt(nv[3], Di, t2r, ALU.mult)
        yield
        P.tt(Dt[s][:, 1, :, :], nv[2], nv[3], ALU.add, eng="pool")
        yield
        P.mm(psY, k.fa[:, 0:MY], Dt[s][:, 0, :, :].rearrange("p c n -> p (c n)"), start=True, stop=False)
        P.mm(psY, k.fa[:, NA:NA + MY], Dt[s][:, 1, :, :].rearrange("p c n -> p (c n)"), start=False, stop=True)
        yield
        yo = yout[lg % 2]
        P.act(yo[:, gi * CG:(gi + 1) * CG, :], psY[0:NA // 2, :].rearrange("p (c n) -> p c n", c=CG), AF.Copy, scale=float(1.0 / NF))
        if gi == LG // CG - 1:
            yield
            P.dma(YT[lg * LG:(lg + 1) * LG, :].rearrange("c (n1 n2) -> n1 c n2", n2=NB), yo[:], "yd_sy%d" % (lg % 2))

    k.fft_stagger = 9
    hy_fft(k, UT, NA // 2, "yd2_", sink2)
    P.barrier()
    if _HYSTOP == "E":
        return
    P.reset("hyena_E")
    skp = P.sbuf("ye_sk", [128, 16], F32)
    P.dma(skp[:], k.hy_skip.rearrange("o (j p) -> p (o j)", p=128), "ye_c", allow_slow_non_contiguous=True)
    tl = lambda nm, sh, dt=F32, n=2: [P.sbuf("ye_%s%d" % (nm, i), sh, dt) for i in range(n)]
    NL = 4
    tl = lambda nm, sh, dt=F32, n=NL: [P.sbuf("ye_%s%d" % (nm, i), sh, dt) for i in range(n)]
    yy, u2, x1t, zz, tt_ = (tl(n, [128, BL]) for n in ("yy", "u2", "x1", "zz", "tt"))
    gt = tl("gt", [128, BL], BF16)

    def bodyE(j, bi, s):
        rows = slice(j * 128, (j + 1) * 128)
        tsl = slice(bi * BL, (bi + 1) * BL)
        P.dma(yy[s][:], YT[rows, tsl], "ye_y%d" % s)
        P.dma(u2[s][:], UT[rows, tsl], "ye_u%d" % s)
        P.dma(x1t[s][:], X1[rows, tsl], "ye_x%d" % s)
        P.dma(zz[s][:], k.ZT[rows, tsl], "ye_z%d" % s)
        yield
        P.stt(tt_[s][:], u2[s][:], skp[:, j:j + 1], yy[s][:], ALU.mult, ALU.add)
        yield
        P.tt(tt_[s][:], tt_[s][:], x1t[s][:], ALU.mult, eng="pool")
        yield
        P.tt(gt[s][:], tt_[s][:], zz[s][:], ALU.mult, eng=("dve" if (j + bi) % 2 == 0 else "pool"))
        yield
        P.dma(k.QK[0, 0, rows, tsl], gt[s][:], "ye_g%d" % s)

    lanes = Lanes(NL)
    it = 0
    for j in range(16):
        for bi in range(NBK):
            lanes.push(bodyE(j, bi, it % NL))
            it += 1
    lanes.drain()
    P.barrier()
    if _HYSTOP == "F":
        return
    P.reset("hyena_F")
    st = outproj_setup(k, l, k.hy_w_out, "yf_")
    GT = [P.sbuf("yf_GT%d" % i, [128, 16, TB], BF16) for i in range(2)]
    for tb in range(NTB):
        P.dma(GT[tb % 2][:], k.QK[0, 0, :, tb * TB:(tb + 1) * TB].rearrange("(e p) t -> p e t", p=128), "yf_g%d" % (tb % 2))
        outproj_block(k, l, tb, GT[tb % 2], st, xsrc, "yf_")


_CACHE = {}


def kernel(**inputs):
    cfg = Cfg(NS=4, SL=2048, kinds=(0, 1, 2, 3))
    if "nc" not in _CACHE:
        _CACHE["nc"] = build(cfg)[0]
    nc = _CACHE["nc"]
    w = {n: np.asarray(v) for n, v in inputs.items() if n not in ("x_prompt", "x_sample", "c_prompt", "c_sample")}
    xp = np.asarray(inputs["x_prompt"], np.float32)
    xs = np.asarray(inputs["x_sample"], np.float32)
    cp = np.asarray(inputs["c_prompt"], np.float32)
    cs = np.asarray(inputs["c_sample"], np.float32)
    SL, T = cfg.SL, cfg.T
    in_maps = []
    for i in range(4):
        in_maps.append(core_inputs(cfg, [xp[i, s * SL:(s + 1) * SL] for s in range(4)], [cp[i]] * 4, 1.0, w,
                                   np.arange(T), T))
    for j in range(4):
        a, b = 2 * j, 2 * j + 1
        in_maps.append(core_inputs(cfg, [xs[a], None, xs[b], None], [cs[a], None, cs[b], None], 0.0, w,
                                   np.arange(T) % SL, SL))
    res = run_bass_kernel_spmd(nc, in_maps, core_ids=list(range(8)))
    yp = np.stack([np.asarray(res.results[i]["yout"], np.float32) for i in range(4)], 0)
    ys = np.zeros(xs.shape, np.float32)
    for j in range(4):
        yo = np.asarray(res.results[4 + j]["yout"], np.float32)
        ys[2 * j] = yo[0:SL]
        ys[2 * j + 1] = yo[2 * SL:3 * SL]
    return (yp, ys)
```
